# Optimizing a Trainium2 kernel written in Bass

```python
import jax, jax.numpy as jnp
from jax import lax
import numpy as np

D_MODEL = 1024
BATCH = 1
SEQ = 16384
DEPTH = 1

GRID_W = 64
CTX_LEN = 256
RET_HEADS = 4
RET_DK = 256
RET_DV = 512
RET_CHUNK = 128
FOURIER_GROUPS = 4
FOURIER_DG = 256
D_FF = 2816
ROPE_BASE = 10000.0
EPS = 1e-6
N_MOD = 9

RET_QK = RET_HEADS * RET_DK
RET_V = RET_HEADS * RET_DV
FOURIER_W = FOURIER_GROUPS * FOURIER_DG
IN_COLS = 2 * RET_QK + 2 * RET_V + FOURIER_W + 2 * D_MODEL
SPLITS = (RET_QK, 2 * RET_QK, 2 * RET_QK + RET_V, 2 * RET_QK + 2 * RET_V,
          2 * RET_QK + 2 * RET_V + FOURIER_W, 2 * RET_QK + 2 * RET_V + FOURIER_W + D_MODEL)

kernel_name = "hybrid_retention_fnet_macaron_dit_block"


def rmsnorm(x, w):
    xf = x.astype(jnp.float32)
    y = xf * lax.rsqrt(jnp.mean(xf * xf, axis=-1, keepdims=True) + EPS)
    return (y * w.astype(jnp.float32)).astype(x.dtype)


def modulate(h, shift, scale):
    return h * (1 + scale) + shift


def swiglu(h, w13, w2):
    a, b = jnp.split(h @ w13, 2, axis=-1)
    return (jax.nn.silu(a) * b) @ w2


def heads(a, dh):
    B, L, _ = a.shape
    return a.reshape(B, L, -1, dh).transpose(0, 2, 1, 3)


def project(h, w_in):
    u = h @ w_in
    q, k, v, g, f, ga, gf = jnp.split(u, SPLITS, axis=-1)
    q = heads(q, RET_DK).astype(jnp.float32) * (RET_DK ** -0.5)
    k = heads(k, RET_DK).astype(jnp.float32)
    v = heads(v, RET_DV).astype(jnp.float32)
    return q, k, v, g, f, ga, gf


def rope_2d(a, rows):
    row = jnp.repeat(jnp.arange(rows, dtype=jnp.float32), GRID_W)
    col = jnp.tile(jnp.arange(GRID_W, dtype=jnp.float32), rows)
    n_freq = RET_DK // 4
    inv = ROPE_BASE ** (-jnp.arange(n_freq, dtype=jnp.float32) / n_freq)
    ang = jnp.concatenate([row[:, None] * inv, col[:, None] * inv], axis=-1)
    cos, sin = jnp.cos(ang), jnp.sin(ang)
    a1, a2 = a[..., 0::2], a[..., 1::2]
    return jnp.stack([a1 * cos - a2 * sin, a1 * sin + a2 * cos], axis=-1).reshape(a.shape)


def retention_scan(q, k, v, log_gamma, s0):
    B, H, L, dk = q.shape
    dv = v.shape[-1]
    C = RET_CHUNK
    n = L // C
    idx = jnp.arange(C, dtype=jnp.float32)
    lg = log_gamma[:, None]
    diff = idx[:, None] - idx[None, :]
    decay_intra = jnp.where(diff >= 0, jnp.exp(lg[:, :, None] * jnp.maximum(diff, 0.0)), 0.0)
    xi = jnp.exp(lg * (idx + 1.0))
    zeta = jnp.exp(lg * (C - 1.0 - idx))
    g_chunk = jnp.exp(log_gamma * C)

    def to_chunks(a):
        return jnp.moveaxis(a.reshape(B, H, n, C, a.shape[-1]), 2, 0)

    def step(s, qkv):
        qc, kc, vc = qkv
        sc = jnp.einsum('bhid,bhjd->bhij', qc, kc) * decay_intra
        o = (jnp.einsum('bhij,bhjv->bhiv', sc, vc)
             + jnp.einsum('bhid,bhdv->bhiv', qc, s) * xi[None, :, :, None])
        s = s * g_chunk[None, :, None, None] + jnp.einsum('bhjd,bhjv->bhdv', kc * zeta[None, :, :, None], vc)
        return s, o

    s, o = lax.scan(step, s0, (to_chunks(q), to_chunks(k), to_chunks(v)))
    o = jnp.moveaxis(o, 0, 2).reshape(B, H, L, dv)
    return o, s


def bidir_retention(q, k, v, lg_f, lg_b, s_f0, s_b0):
    o_f, s_f = retention_scan(q, k, v, lg_f, s_f0)
    flip = lambda a: jnp.flip(a, axis=2)
    o_b, s_b = retention_scan(flip(q), flip(k), flip(v), lg_b, s_b0)
    return o_f + flip(o_b), s_f, s_b


def retention_out(o, g, gn_w, p_ret):
    mu = jnp.mean(o, axis=-1, keepdims=True)
    var = jnp.mean(jnp.square(o - mu), axis=-1, keepdims=True)
    on = (o - mu) * lax.rsqrt(var + EPS)
    B, H, L, dv = o.shape
    on = on.transpose(0, 2, 1, 3).reshape(B, L, H * dv).astype(g.dtype) * gn_w
    return (jax.nn.silu(g) * on) @ p_ret


def fourier_branch(f, p_four):
    B, L, _ = f.shape
    fg = f.reshape(B, L, FOURIER_GROUPS, FOURIER_DG).astype(jnp.float32)
    mixed = jnp.fft.fft2(fg, axes=(1, 3), norm="ortho").real
    return mixed.reshape(B, L, FOURIER_W).astype(f.dtype) @ p_four


def merge(r, fo, ga, gf, w_out):
    return (jax.nn.sigmoid(ga) * r + jax.nn.sigmoid(gf) * fo) @ w_out


def setup_inputs(seed: int = 0) -> dict:
    key = jax.random.key(seed)
    ks = jax.random.split(key, 24)

    def w(k, shape, fan_in, scale=1.0):
        return jax.random.normal(k, shape, jnp.float32) * (scale * fan_in ** -0.5)

    def gain(k, shape):
        return 1.0 + 0.02 * jax.random.normal(k, shape, jnp.float32)

    decay_base = jnp.asarray(np.log(2.0 ** (5 + np.arange(RET_HEADS)) - 1.0).astype(np.float32))
    return {
        "x": jax.random.normal(ks[0], (BATCH, SEQ, D_MODEL), jnp.float32),
        "c": jax.random.normal(ks[1], (BATCH, D_MODEL), jnp.float32),
        "ctx": jax.random.normal(ks[2], (BATCH, CTX_LEN, D_MODEL), jnp.float32),
        "c_ctx": 0.5 * jax.random.normal(ks[3], (D_MODEL,), jnp.float32),
        "w_ada": w(ks[4], (DEPTH, D_MODEL, N_MOD * D_MODEL), D_MODEL, 0.5),
        "b_ada": 0.02 * jax.random.normal(ks[5], (DEPTH, N_MOD * D_MODEL), jnp.float32),
        "norm_ffn1": gain(ks[6], (DEPTH, D_MODEL)),
        "w13_ffn1": w(ks[7], (DEPTH, D_MODEL, 2 * D_FF), D_MODEL),
        "w2_ffn1": w(ks[8], (DEPTH, D_FF, D_MODEL), D_FF),
        "norm_mix": gain(ks[9], (DEPTH, D_MODEL)),
        "w_in": w(ks[10], (DEPTH, D_MODEL, IN_COLS), D_MODEL),
        "decay_fwd": decay_base[None, :] + 0.05 * jax.random.normal(ks[11], (DEPTH, RET_HEADS), jnp.float32),
        "decay_bwd": decay_base[None, :] + 0.05 * jax.random.normal(ks[12], (DEPTH, RET_HEADS), jnp.float32),
        "ret_gn_w": gain(ks[13], (DEPTH, RET_V)),
        "p_ret": w(ks[14], (DEPTH, RET_V, D_MODEL), RET_V),
        "p_four": w(ks[15], (DEPTH, FOURIER_W, D_MODEL), FOURIER_W),
        "w_out": w(ks[16], (DEPTH, D_MODEL, D_MODEL), D_MODEL),
        "norm_ffn2": gain(ks[17], (DEPTH, D_MODEL)),
        "w13_ffn2": w(ks[18], (DEPTH, D_MODEL, 2 * D_FF), D_MODEL),
        "w2_ffn2": w(ks[19], (DEPTH, D_FF, D_MODEL), D_FF),
        "norm_final": gain(ks[20], (D_MODEL,)),
    }


def reference(x, c, ctx, c_ctx, w_ada, b_ada, norm_ffn1, w13_ffn1, w2_ffn1, norm_mix, w_in,
              decay_fwd, decay_bwd, ret_gn_w, p_ret, p_four, w_out, norm_ffn2, w13_ffn2, w2_ffn2,
              norm_final):
    B, L, _ = x.shape
    rows = L // GRID_W
    for l in range(DEPTH):
        last = l == DEPTH - 1
        mod_x = (jax.nn.silu(c) @ w_ada[l] + b_ada[l])[:, None, :]
        mod_c = (jax.nn.silu(c_ctx) @ w_ada[l] + b_ada[l])[None, None, :]
        sh1x, sc1x, g1x, sh2x, sc2x, g2x, sh3x, sc3x, g3x = jnp.split(mod_x, N_MOD, axis=-1)
        sh1c, sc1c, g1c, sh2c, sc2c, g2c, sh3c, sc3c, g3c = jnp.split(mod_c, N_MOD, axis=-1)

        x = x + 0.5 * g1x * swiglu(modulate(rmsnorm(x, norm_ffn1[l]), sh1x, sc1x), w13_ffn1[l], w2_ffn1[l])
        ctx = ctx + 0.5 * g1c * swiglu(modulate(rmsnorm(ctx, norm_ffn1[l]), sh1c, sc1c), w13_ffn1[l], w2_ffn1[l])

        lg_f = jax.nn.log_sigmoid(decay_fwd[l].astype(jnp.float32))
        lg_b = jax.nn.log_sigmoid(decay_bwd[l].astype(jnp.float32))

        qc, kc, vc, gc, fc, gac, gfc = project(modulate(rmsnorm(ctx, norm_mix[l]), sh2c, sc2c), w_in[l])
        zero = jnp.zeros((ctx.shape[0], RET_HEADS, RET_DK, RET_DV), jnp.float32)
        o_c, s_f, s_b = bidir_retention(qc, kc, vc, lg_f, lg_b, zero, zero)

        qx, kx, vx, gx, fx, gax, gfx = project(modulate(rmsnorm(x, norm_mix[l]), sh2x, sc2x), w_in[l])
        qx = rope_2d(qx, rows)
        kx = rope_2d(kx, rows)
        o_x, _, _ = bidir_retention(qx, kx, vx, lg_f, lg_b, s_f, s_b)
        y_x = merge(retention_out(o_x, gx, ret_gn_w[l], p_ret[l]), fourier_branch(fx, p_four[l]), gax, gfx, w_out[l])
        x = x + g2x * y_x

        x = x + 0.5 * g3x * swiglu(modulate(rmsnorm(x, norm_ffn2[l]), sh3x, sc3x), w13_ffn2[l], w2_ffn2[l])

        if not last:
            y_c = merge(retention_out(o_c, gc, ret_gn_w[l], p_ret[l]), fourier_branch(fc, p_four[l]), gac, gfc, w_out[l])
            ctx = ctx + g2c * y_c
            ctx = ctx + 0.5 * g3c * swiglu(modulate(rmsnorm(ctx, norm_ffn2[l]), sh3c, sc3c), w13_ffn2[l], w2_ffn2[l])
    return rmsnorm(x, norm_final)
```

```python
import os
import numpy as np
import concourse.bass as bass
import concourse.mybir as mybir
from concourse.bass_utils import run_bass_kernel_spmd

F32 = mybir.dt.float32
BF16 = mybir.dt.bfloat16
AF = mybir.ActivationFunctionType
ALU = mybir.AluOpType

NCORES = 8
D = 1024
L = 16384
NT = L // NCORES
DFF = 2816
NJ = DFF // 128
KC = D // 128
EPS = 1e-6
TG = 512
NTG = NT // TG


class Prog:
    ENG = ("pe", "act", "dve", "pool", "sp")

    def __init__(self, nc):
        self.nc = nc
        self.streams = {e: [] for e in self.ENG}
        self.cnt = {}
        self.res = {}
        self.waited = {e: {} for e in self.ENG}
        self.sems = {}
        self.n_ops = 0
        self._cid = {}

    def _tok_wait(self, eng, waits, tok):
        if tok is None:
            return
        k, v = tok
        if k == eng and eng == "pe":
            return
        if self.waited[eng].get(k, 0) >= v:
            return
        if waits.get(k, 0) < v:
            waits[k] = v

    def cid(self, e):
        k = id(e)
        if k not in self._cid:
            self._cid[k] = e.partition_id()
        return self._cid[k]

    def op(self, eng, fn, reads=(), writes=(), sem=None, inc=True, amt=None):
        waits = {}
        for r in reads:
            st = self.res.get(r)
            if st:
                self._tok_wait(eng, waits, st[0])
        for w in writes:
            st = self.res.get(w)
            if st:
                self._tok_wait(eng, waits, st[0])
                for t in st[1]:
                    self._tok_wait(eng, waits, t)
        if sem is None:
            key, amt = eng, 1
        elif amt is None:
            key, amt = sem, 16
        else:
            key = sem
        if inc:
            self.cnt[key] = self.cnt.get(key, 0) + amt
            tok = (key, self.cnt[key])
        else:
            assert sem is None
            tok = (key, self.cnt.get(key, 0) + 1)
        for k, v in waits.items():
            self.waited[eng][k] = v
        if inc and sem is None:
            pass
        for r in reads:
            self.res.setdefault(r, [None, []])[1].append(tok)
        for w in writes:
            self.res[w] = [tok, []]
        self.streams[eng].append((fn, waits, key if inc else None, amt))
        self.n_ops += 1

    def barrier(self, exclude=()):
        snap = {k: v for k, v in self.cnt.items() if k not in exclude}
        for e in self.ENG:
            waits = {}
            for k, v in snap.items():
                if k == e:
                    continue
                if self.waited[e].get(k, 0) < v:
                    waits[k] = v
                    self.waited[e][k] = v
            if waits:
                self.streams[e].append((None, waits, None, 0))

    def sem(self, key):
        if key not in self.sems:
            self.sems[key] = self.nc.alloc_semaphore("s_" + key)
        return self.sems[key]

    def emit(self):
        nc = self.nc
        self.barrier()
        for k in self.cnt:
            self.sem(k)
        print("semaphores used", len(self.sems))

        def run(ename, e):
            for fn, waits, key, amt in self.streams[ename]:
                for k, v in waits.items():
                    e.wait_ge(self.sems[k], v)
                if fn is not None:
                    ins = fn(e)
                    if key is not None:
                        ins.then_inc(self.sems[key], amt)

        with nc.Block() as block:
            @block.tensor
            def _(e):
                run("pe", e)

            @block.scalar
            def _(e):
                run("act", e)

            @block.vector
            def _(e):
                run("dve", e)

            @block.gpsimd
            def _(e):
                run("pool", e)

            @block.sync
            def _(e):
                run("sp", e)


def lay_kmajor(w, ncols_blk):
    K, N = w.shape
    nb = N // ncols_blk
    return np.ascontiguousarray(w.reshape(K // 128, 128, nb, ncols_blk).transpose(2, 1, 0, 3))


def lay_w13(w13):
    a = w13[:, :DFF].reshape(KC, 128, NJ, 128)
    b = w13[:, DFF:].reshape(KC, 128, NJ, 128)
    ab = np.concatenate([a, b], axis=3)
    return np.ascontiguousarray(ab.transpose(2, 1, 0, 3))


def lay_vec(v):
    return np.ascontiguousarray(v.reshape(-1, 128).T)


class KB:
    def __init__(self, name):
        self.nc = bass.Bass("TRN2", target_bir_lowering=False)
        self.P = Prog(self.nc)
        self.sb = lambda name, shape, dt: self.nc.alloc_sbuf_tensor("sb_" + name, shape, dt)
        self.psum = self.nc.alloc_psum_tensor("psum", [128, 8, 512], F32)
        self.bank_ctr = 0
        self.ctr = {}

    def din(self, name, shape, dt=F32):
        return self.nc.dram_tensor(name, list(shape), dt, kind="ExternalInput").ap()

    def dout(self, name, shape, dt=F32):
        return self.nc.dram_tensor(name, list(shape), dt, kind="ExternalOutput").ap()

    def bank(self, i):
        return self.psum[:, i, :]

    def next_bank(self, nb=6):
        b = self.bank_ctr % nb
        self.bank_ctr += 1
        return b

    def rot(self, key, n):
        v = self.ctr.get(key, 0)
        self.ctr[key] = v + 1
        return v % n

    def common(self):
        P, sb = self.P, self.sb
        self.ident = sb("identF", [128, 128], F32)
        self.ones_b = sb("onesB", [128, 128], BF16)
        self.epsc = sb("epsc", [128, 1], F32)
        self.sq = [sb(f"sq{i}", [128, TG], BF16) for i in range(2)]
        self.rstd = [sb(f"rstd{i}", [128, TG], F32) for i in range(2)]
        self.tmpf = [sb(f"tmpf{i}", [128, TG], F32) for i in range(4)]
        ident_d = self.din("ident", [128, 128])
        P.op("sp", lambda e: e.dma_start(out=self.ident[:], in_=ident_d), writes=["ident"], sem="ld_ident")
        P.op("dve", lambda e: e.memset(self.ones_b[:], 1.0), writes=["ones_b"])
        P.op("dve", lambda e: e.memset(self.epsc[:], EPS), writes=["epsc"])

    def rms_stats(self, xv, xres_key, ncols, slot, pb=6):
        P = self.P
        for kc in range(KC):
            s = self.rot("sq", 2)
            P.op("act", lambda e, kc=kc, s=s: e.activation(
                out=self.sq[s][:, :ncols], in_=xv(kc), func=AF.Square),
                reads=[xres_key], writes=[("sq", s)])
            P.op("pe", lambda e, kc=kc, s=s: e.matmul(
                self.psum[:, pb, :ncols], lhsT=self.ones_b[:], rhs=self.sq[s][:, :ncols],
                start=(kc == 0), stop=(kc == KC - 1)),
                reads=[("sq", s), "ones_b"], writes=[("ps", pb)], inc=True)
        P.op("act", lambda e: e.activation(
            out=self.rstd[slot][:, :ncols], in_=self.psum[:, pb, :ncols], func=AF.Sqrt, scale=1.0 / D,
            bias=self.epsc[:, 0:1]),
            reads=[("ps", pb), "epsc"], writes=[("rstd", slot)])
        P.op("dve", lambda e: e.reciprocal(out=self.rstd[slot][:, :ncols], in_=self.rstd[slot][:, :ncols]),
             reads=[("rstd", slot)], writes=[("rstd", slot)])

    def norm_mod(self, xv, xres_key, hv, h_key, ncols, scale_ap, bias_ap):
        P = self.P
        slot = self.rot("rstd", 2)
        self.rms_stats(xv, xres_key, ncols, slot)
        for kc in range(KC):
            t = self.rot("tmpf", 4)
            P.op("dve", lambda e, kc=kc, t=t: e.tensor_tensor(
                out=self.tmpf[t][:, :ncols], in0=xv(kc), in1=self.rstd[slot][:, :ncols], op=ALU.mult),
                reads=[xres_key, ("rstd", slot)], writes=[("tmpf", t)])
            P.op("act", lambda e, kc=kc, t=t: e.activation(
                out=hv(kc), in_=self.tmpf[t][:, :ncols], func=AF.Identity,
                scale=scale_ap(kc), bias=bias_ap(kc)),
                reads=[("tmpf", t), "modv"], writes=[h_key])

    def ffn(self, groups, w13_d, w2_d, w13b, w2b, gate_ap, wk13="w13b", wk2="w2b"):
        P = self.P
        for j in range(NJ):
            s = self.rot(wk13, len(w13b))
            P.op("pool", lambda e, j=j, s=s: e.dma_start(out=w13b[s][:], in_=w13_d[j]),
                 writes=[(wk13, s)], sem=f"{wk13}{s}")
            for G in groups:
                n = G["n"]
                pa, pb = self.next_bank(), self.next_bank()
                for (pbk, c0) in ((pa, 0), (pb, 128)):
                    for kc in range(KC):
                        P.op("pe", lambda e, s=s, kc=kc, G=G, pbk=pbk, c0=c0, n=n: e.matmul(
                            self.psum[:, pbk, :n], lhsT=w13b[s][:, kc, c0:c0 + 128], rhs=G["h"](kc),
                            start=(kc == 0), stop=(kc == KC - 1)),
                            reads=[(wk13, s), G["hk"]], writes=[("ps", pbk)], inc=(kc == KC - 1))
                t = self.rot("tmpf", 4)
                P.op("act", lambda e, pa=pa, t=t, n=n: e.activation(
                    out=self.tmpf[t][:, :n], in_=self.psum[:, pa, :n], func=AF.Silu),
                    reads=[("ps", pa)], writes=[("tmpf", t)])
                P.op("dve", lambda e, pb=pb, t=t, j=j, G=G, n=n: e.tensor_tensor(
                    out=G["g"](j), in0=self.psum[:, pb, :n], in1=self.tmpf[t][:, :n], op=ALU.mult),
                    reads=[("ps", pb), ("tmpf", t)], writes=[G["gk"]])
        for dc in range(KC):
            s = self.rot(wk2, len(w2b))
            for h2 in range(2):
                P.op("pool", lambda e, dc=dc, s=s, h2=h2: e.dma_start(
                    out=w2b[s][:, h2 * 11:(h2 + 1) * 11, :], in_=w2_d[dc, :, h2 * 11:(h2 + 1) * 11, :]),
                    writes=[(wk2, s)], sem=f"{wk2}{s}")
            for G in groups:
                n = G["n"]
                pb = self.next_bank()
                for j in range(NJ):
                    P.op("pe", lambda e, s=s, j=j, G=G, pb=pb, n=n: e.matmul(
                        self.psum[:, pb, :n], lhsT=w2b[s][:, j, :], rhs=G["g"](j),
                        start=(j == 0), stop=(j == NJ - 1)),
                        reads=[(wk2, s), G["gk"]], writes=[("ps", pb)], inc=(j == NJ - 1))
                P.op("dve", lambda e, dc=dc, G=G, pb=pb, n=n: e.scalar_tensor_tensor(
                    out=G["x"](dc), in0=self.psum[:, pb, :n], scalar=gate_ap(dc),
                    in1=G["x"](dc), op0=ALU.mult, op1=ALU.add),
                    reads=[("ps", pb), "modv", G["xk"]], writes=[G["xk"]])

    def load_tokmajor_to_featmajor(self, src_d, nchunks, dst_fn, dst_key_fn, xt):
        P = self.P
        for c in range(nchunks):
            s = self.rot("xt", 2)
            P.op("sp", lambda e, c=c, s=s: e.dma_start(out=xt[s][:], in_=src_d[c * 128:(c + 1) * 128, :]),
                 writes=[("xt", s)], sem=f"xt{s}")
            for half in range(2):
                b = self.next_bank()
                for q in range(4):
                    dc = half * 4 + q
                    P.op("pe", lambda e, s=s, dc=dc, b=b, q=q: e.transpose(
                        out=self.psum[:, b, q * 128:(q + 1) * 128], in_=xt[s][:, dc * 128:(dc + 1) * 128],
                        identity=self.ident[:]),
                        reads=[("xt", s), "ident"], writes=[("ps", b)], inc=(q == 3))
                src = self.psum[:, b, :].rearrange("p (q t) -> p q t", q=4)
                if half == 0:
                    P.op("act", lambda e, c=c, half=half, src=src: e.copy(out=dst_fn(c, half), in_=src),
                         reads=[("ps", b)], writes=[dst_key_fn(c)])
                else:
                    P.op("dve", lambda e, c=c, half=half, src=src: e.tensor_copy(out=dst_fn(c, half), in_=src),
                         reads=[("ps", b)], writes=[dst_key_fn(c)])


NCH_U = 72
CTX = 256


def build_l1():
    kb = KB("l1")
    nc, P, sb, psum = kb.nc, kb.P, kb.sb, kb.psum
    x_d = kb.din("x", [NT, D])
    ctx_d = kb.din("ctx", [CTX, D])
    c_d = kb.din("cvec", [128, KC, 2])
    wada_d = kb.din("w_ada", [72, 128, KC, 128])
    bada_d = kb.din("b_ada", [128, 72])
    nrm_d = kb.din("norms", [128, 4, KC])
    w13a_d = kb.din("w13_1", [NJ, 128, KC, 256])
    w2a_d = kb.din("w2_1", [KC, 128, NJ, 128])
    win_d = kb.din("w_in", [NCH_U, 128, KC, 128])
    rope_d = kb.din("rope", [2, 128, 4, NT // 2])
    x1_o = kb.dout("x1T", [128, KC, NT])
    u_o = kb.dout("uT", [NCH_U, 128, NT], BF16)
    kvc_o = kb.dout("kvc", [24, 128, CTX], BF16)
    mod_o = kb.dout("modv", [128, 72, 2])

    kb.common()
    xres = sb("xres", [128, KC, NT], F32)
    xc = sb("xc", [128, KC, CTX], F32)
    hT = sb("hT", [128, KC, NT // 2], BF16)
    gbuf = sb("gbuf", [128, NJ, NT // 2], BF16)
    w13b = [sb(f"w13b{i}", [128, KC, 256], BF16) for i in range(3)]
    w2b = [sb(f"w2b{i}", [128, NJ, 128], BF16) for i in range(2)]
    xt = [sb(f"xt{i}", [128, D], F32) for i in range(2)]
    ropet = sb("ropet", [128, 4, NT // 2], F32)
    cv = sb("cv", [128, KC, 2], F32)
    cvs = sb("cvs", [128, KC, 2], BF16)
    bada = sb("bada", [128, 72], F32)
    nrm = sb("nrm", [128, 4, KC], F32)
    mod = sb("mod", [128, 72, 2], F32)
    eff = sb("eff", [128, 3, KC, 2], F32)
    gate = sb("gate", [128, 3, KC, 2], F32)
    ubuf = [sb(f"ubuf{i}", [128, TG], BF16) for i in range(4)]

    P.op("sp", lambda e: e.dma_start(out=cv[:], in_=c_d), writes=["cv"], sem="ld0")
    P.op("sp", lambda e: e.dma_start(out=bada[:], in_=bada_d), writes=["bada"], sem="ld0")
    P.op("sp", lambda e: e.dma_start(out=nrm[:], in_=nrm_d), writes=["nrm"], sem="ld0")
    P.op("act", lambda e: e.activation(out=cvs[:], in_=cv[:], func=AF.Silu), reads=["cv"], writes=["cvs"])

    for nb in range(72):
        s = kb.rot("w13b", 3)
        P.op("pool", lambda e, nb=nb, s=s: e.dma_start(out=w13b[s][:, :, 0:128], in_=wada_d[nb]),
             writes=[("w13b", s)], sem=f"w13b{s}")
        for kc in range(KC):
            P.op("pe", lambda e, nb=nb, s=s, kc=kc: e.matmul(
                psum[:, 7, nb * 2:nb * 2 + 2], lhsT=w13b[s][:, kc, 0:128], rhs=cvs[:, kc, :],
                start=(kc == 0), stop=(kc == KC - 1)),
                reads=[("w13b", s), "cvs"], writes=[("ps", 7)], inc=(kc == KC - 1))
    P.op("dve", lambda e: e.tensor_tensor(
        out=mod[:], in0=psum[:, 7, 0:144].rearrange("p (n s) -> p n s", s=2),
        in1=bada[:].unsqueeze(2).broadcast_to([128, 72, 2]), op=ALU.add),
        reads=[("ps", 7), "bada"], writes=["modv"])
    for n in range(3):
        P.op("dve", lambda e, n=n: e.scalar_tensor_tensor(
            out=eff[:, n], in0=mod[:, (3 * n + 1) * 8:(3 * n + 2) * 8, :], scalar=1.0,
            in1=nrm[:, n, :].unsqueeze(2).broadcast_to([128, KC, 2]), op0=ALU.add, op1=ALU.mult),
            reads=["modv", "nrm"], writes=["modv"])
        P.op("dve", lambda e, n=n: e.tensor_scalar(
            out=gate[:, n], in0=mod[:, (3 * n + 2) * 8:(3 * n + 3) * 8, :],
            scalar1=(1.0 if n == 1 else 0.5), scalar2=None, op0=ALU.mult),
            reads=["modv"], writes=["modv"])
    P.op("sp", lambda e: e.dma_start(out=mod_o, in_=mod[:]), reads=["modv"], sem="st_mod")

    def sc_ap(n, s):
        return lambda kc: eff[:, n, kc, s:s + 1]

    def sh_ap(n, s):
        return lambda kc: mod[:, (3 * n) * 8 + kc, s:s + 1]

    def gt_ap(n, s):
        return lambda dc: gate[:, n, dc, s:s + 1]

    kb.load_tokmajor_to_featmajor(
        x_d, NT // 128,
        lambda c, half: xres[:, half * 4:half * 4 + 4, c * 128:(c + 1) * 128],
        lambda c: ("xres", c // 4), xt)
    kb.load_tokmajor_to_featmajor(
        ctx_d, CTX // 128,
        lambda c, half: xc[:, half * 4:half * 4 + 4, c * 128:(c + 1) * 128],
        lambda c: "xc", xt)

    def project(groups, chunks, rope, out_fn):
        pend = {}
        for ch in chunks:
            s = kb.rot("w13b", 3)
            P.op("pool", lambda e, ch=ch, s=s: e.dma_start(out=w13b[s][:, :, 0:128], in_=win_d[ch]),
                 writes=[("w13b", s)], sem=f"w13b{s}")
            for gi, G in enumerate(groups):
                n = G["n"]
                pb = kb.next_bank()
                for kc in range(KC):
                    P.op("pe", lambda e, s=s, kc=kc, G=G, pb=pb, n=n: e.matmul(
                        psum[:, pb, :n], lhsT=w13b[s][:, kc, 0:128], rhs=G["h"](kc),
                        start=(kc == 0), stop=(kc == KC - 1)),
                        reads=[("w13b", s), G["hk"]], writes=[("ps", pb)], inc=(kc == KC - 1))
                if rope and ch < 16:
                    if ch % 2 == 0:
                        pend[gi] = pb
                        continue
                    pa = pend[gi]
                    c0 = G["col0"]
                    ti = 0 if ch < 8 else 2
                    cs = ropet[:, ti, c0:c0 + n]
                    sn = ropet[:, ti + 1, c0:c0 + n]
                    t1, t2, t3, t4 = [kb.rot("tmpf", 4) for _ in range(4)]
                    tm = kb.tmpf
                    P.op("dve", lambda e, pa=pa, t1=t1, cs=cs, n=n: e.tensor_tensor(
                        out=tm[t1][:, :n], in0=psum[:, pa, :n], in1=cs, op=ALU.mult),
                        reads=[("ps", pa), "ropet"], writes=[("tmpf", t1)])
                    P.op("dve", lambda e, pb=pb, t2=t2, sn=sn, n=n: e.tensor_tensor(
                        out=tm[t2][:, :n], in0=psum[:, pb, :n], in1=sn, op=ALU.mult),
                        reads=[("ps", pb), "ropet"], writes=[("tmpf", t2)])
                    P.op("dve", lambda e, pa=pa, t3=t3, sn=sn, n=n: e.tensor_tensor(
                        out=tm[t3][:, :n], in0=psum[:, pa, :n], in1=sn, op=ALU.mult),
                        reads=[("ps", pa), "ropet"], writes=[("tmpf", t3)])
                    P.op("dve", lambda e, pb=pb, t4=t4, cs=cs, n=n: e.tensor_tensor(
                        out=tm[t4][:, :n], in0=psum[:, pb, :n], in1=cs, op=ALU.mult),
                        reads=[("ps", pb), "ropet"], writes=[("tmpf", t4)])
                    u1, u2 = kb.rot("ubuf", 4), kb.rot("ubuf", 4)
                    P.op("dve", lambda e, t1=t1, t2=t2, u1=u1, n=n: e.tensor_tensor(
                        out=ubuf[u1][:, :n], in0=tm[t1][:, :n], in1=tm[t2][:, :n], op=ALU.subtract),
                        reads=[("tmpf", t1), ("tmpf", t2)], writes=[("ubuf", u1)])
                    P.op("dve", lambda e, t3=t3, t4=t4, u2=u2, n=n: e.tensor_tensor(
                        out=ubuf[u2][:, :n], in0=tm[t3][:, :n], in1=tm[t4][:, :n], op=ALU.add),
                        reads=[("tmpf", t3), ("tmpf", t4)], writes=[("ubuf", u2)])
                    for (uu, chh) in ((u1, ch - 1), (u2, ch)):
                        P.op("sp", lambda e, uu=uu, chh=chh, G=G, n=n: e.dma_start(
                            out=out_fn(chh, G), in_=ubuf[uu][:, :n]),
                            reads=[("ubuf", uu)], sem=f"stu{uu}")
                else:
                    u = kb.rot("ubuf", 4)
                    P.op("act", lambda e, pb=pb, u=u, n=n: e.copy(out=ubuf[u][:, :n], in_=psum[:, pb, :n]),
                         reads=[("ps", pb)], writes=[("ubuf", u)])
                    P.op("sp", lambda e, u=u, ch=ch, G=G, n=n: e.dma_start(
                        out=out_fn(ch, G), in_=ubuf[u][:, :n]),
                        reads=[("ubuf", u)], sem=f"stu{u}")

    for tile in range(2):
        P.op("sp", lambda e, tile=tile: e.dma_start(out=ropet[:], in_=rope_d[tile]),
             writes=["ropet"], sem="ld_rope")
        groups = []
        for loc in range(2):
            tg = tile * 2 + loc
            groups.append(dict(
                h=(lambda kc, loc=loc: hT[:, kc, loc * TG:(loc + 1) * TG]), hk=("hT", loc),
                g=(lambda j, loc=loc: gbuf[:, j, loc * TG:(loc + 1) * TG]), gk=("gbuf", loc),
                x=(lambda dc, tg=tg: xres[:, dc, tg * TG:(tg + 1) * TG]), xk=("xres", tg),
                n=TG, col0=loc * TG, tg=tg))
        for G in groups:
            kb.norm_mod(G["x"], G["xk"], G["h"], G["hk"], TG, sc_ap(0, 0), sh_ap(0, 0))
        kb.ffn(groups, w13a_d, w2a_d, w13b, w2b, gt_ap(0, 0))
        for G in groups:
            tg = G["tg"]
            P.op("sp", lambda e, tg=tg: e.dma_start(
                out=x1_o[:, :, tg * TG:(tg + 1) * TG], in_=xres[:, :, tg * TG:(tg + 1) * TG]),
                reads=[("xres", tg)], sem="st_x1")
            kb.norm_mod(G["x"], G["xk"], G["h"], G["hk"], TG, sc_ap(1, 0), sh_ap(1, 0))
        project(groups, list(range(NCH_U)), True,
                lambda ch, G: u_o[ch, :, G["tg"] * TG:(G["tg"] + 1) * TG])

    Gc = dict(h=(lambda kc: hT[:, kc, 0:CTX]), hk=("hT", 0),
              g=(lambda j: gbuf[:, j, 0:CTX]), gk=("gbuf", 0),
              x=(lambda dc: xc[:, dc, :]), xk="xc", n=CTX, col0=0)
    kb.norm_mod(Gc["x"], "xc", Gc["h"], Gc["hk"], CTX, sc_ap(0, 1), sh_ap(0, 1))
    kb.ffn([Gc], w13a_d, w2a_d, w13b, w2b, gt_ap(0, 1))
    kb.norm_mod(Gc["x"], "xc", Gc["h"], Gc["hk"], CTX, sc_ap(1, 1), sh_ap(1, 1))
    project([Gc], list(range(8, 32)), False, lambda ch, G: kvc_o[ch - 8])

    P.emit()
    return nc

NCHK = 130


def build_l2():
    kb = KB("l2")
    nc, P, sb, psum = kb.nc, kb.P, kb.sb, kb.psum
    qk_d = kb.din("qk", [NCHK, 128, 512], BF16)
    kv_d = kb.din("kv", [NCHK, 128, 768], BF16)
    dec_d = kb.din("dec", [128, 1])
    tabs_d = kb.din("tabs", [128, 3, 128])
    pcol_d = kb.din("pcol", [128, 2])
    farr_d = kb.din("farr", [128, 128, 128], BF16)
    dft_d = kb.din("dft", [128, 3, 128], BF16)
    cs_d = kb.din("cs", [128, 256], BF16)
    tw_d = kb.din("tw", [128, 2, 128])
    o_o = kb.dout("o", [128, 128, 512])
    F_o = kb.dout("F", [2, 128, 128 * 128], BF16)

    qkb = [sb(f"qkb{i}", [128, 512], BF16) for i in range(3)]
    kvb = [sb(f"kvb{i}", [128, 768], BF16) for i in range(3)]
    dec = sb("dec", [128, 1], F32)
    tabs = sb("tabs", [128, 3, 128], F32)
    pcol = sb("pcol", [128, 2], F32)
    one1 = sb("one1", [128, 1], F32)
    lg = sb("lg", [128, 1], F32)
    zg = sb("zg", [128, 2], F32)
    MT = sb("MT", [128, 128], F32)
    xi = sb("xi", [128, 128], F32)
    tmpe = sb("tmpe", [128, 128], F32)
    SM = [sb(f"SM{i}", [128, 128], BF16) for i in range(2)]
    qx = [sb(f"qx{i}", [128, 256], BF16) for i in range(2)]
    kz = [sb(f"kz{i}", [128, 256], BF16) for i in range(2)]
    S = sb("S", [128, 2, 512], F32)
    Sbf = [sb(f"Sbf{i}", [128, 2, 512], BF16) for i in range(2)]
    osb = [sb(f"osb{i}", [128, 512], F32) for i in range(2)]
    farr = sb("farr", [128, 128, 128], BF16)
    Z = sb("Z", [128, 2, 128, 128], BF16)
    dft = sb("dft", [128, 3, 128], BF16)
    cs = sb("cs", [128, 256], BF16)
    tw = sb("tw", [128, 2, 128], F32)
    tf = [sb(f"tf{i}", [128, 2, 128], F32) for i in range(4)]
    fo = [sb(f"fo{i}", [128, 512], BF16) for i in range(2)]

    for (t, dsrc, key) in ((dec, dec_d, "dec"), (tabs, tabs_d, "tabs"), (pcol, pcol_d, "pcol"),
                           (dft, dft_d, "dft"), (cs, cs_d, "cs"), (tw, tw_d, "tw")):
        P.op("sp", lambda e, t=t, dsrc=dsrc: e.dma_start(out=t[:], in_=dsrc), writes=[key], sem="ld0")
    for q4 in range(4):
        P.op("sp", lambda e, q4=q4: e.dma_start(out=farr[:, q4 * 32:(q4 + 1) * 32, :],
                                                in_=farr_d[:, q4 * 32:(q4 + 1) * 32, :]),
             writes=["farr"], sem="ld_f")
    P.op("dve", lambda e: e.memset(one1[:], 1.0), writes=["one1"])
    P.op("dve", lambda e: e.memset(S[:], 0.0), writes=["S"])
    P.op("act", lambda e: e.activation(out=lg[:], in_=dec[:], func=AF.Exp, scale=-1.0), reads=["dec"], writes=["lg"])
    P.op("act", lambda e: e.activation(out=lg[:], in_=lg[:], func=AF.Ln, bias=one1[:, 0:1]),
         reads=["lg", "one1"], writes=["lg"])
    P.op("dve", lambda e: e.tensor_scalar(out=lg[:], in0=lg[:], scalar1=-1.0, scalar2=None, op0=ALU.mult),
         reads=["lg"], writes=["lg"])
    P.op("act", lambda e: e.activation(out=tmpe[:], in_=tabs[:, 0, :], func=AF.Exp, scale=lg[:, 0:1]),
         reads=["lg", "tabs"], writes=["tmpe"])
    P.op("dve", lambda e: e.tensor_tensor(out=MT[:], in0=tmpe[:], in1=tabs[:, 1, :], op=ALU.mult),
         reads=["tmpe", "tabs"], writes=["MT"])
    P.op("act", lambda e: e.activation(out=xi[:], in_=tabs[:, 2, :], func=AF.Exp, scale=lg[:, 0:1]),
         reads=["lg", "tabs"], writes=["xi"])
    P.op("act", lambda e: e.activation(out=zg[:], in_=pcol[:], func=AF.Exp, scale=lg[:, 0:1]),
         reads=["lg", "pcol"], writes=["zg"])

    def fft_stage1(pi):
        b = 6 + (pi % 2)
        for u in range(2):
            ch = pi * 2 + u
            P.op("pe", lambda e, ch=ch, u=u, b=b: e.matmul(
                psum[:, b, u * 256:(u + 1) * 256], lhsT=farr[:, :, ch], rhs=cs[:],
                start=True, stop=True),
                reads=["farr", "cs"], writes=[("ps", b)], inc=(u == 1))
        Y = psum[:, b, :].rearrange("p (u c l) -> p u c l", u=2, c=2)
        Yc, Ys = Y[:, :, 0, :], Y[:, :, 1, :]
        Tc = tw[:, 0, :].unsqueeze(1).broadcast_to([128, 2, 128])
        Ts = tw[:, 1, :].unsqueeze(1).broadcast_to([128, 2, 128])
        t = [kb.rot("tf", 4) for _ in range(4)]
        for (ti, a, bb) in ((t[0], Yc, Tc), (t[1], Ys, Ts), (t[2], Yc, Ts), (t[3], Ys, Tc)):
            P.op("dve", lambda e, ti=ti, a=a, bb=bb: e.tensor_tensor(out=tf[ti][:], in0=a, in1=bb, op=ALU.mult),
                 reads=[("ps", b), "tw"], writes=[("tf", ti)])
        P.op("pool", lambda e: e.tensor_tensor(
            out=Z[:, 0, pi * 2:pi * 2 + 2, :], in0=tf[t[0]][:], in1=tf[t[1]][:], op=ALU.subtract),
            reads=[("tf", t[0]), ("tf", t[1])], writes=["Z"])
        P.op("pool", lambda e: e.tensor_tensor(
            out=Z[:, 1, pi * 2:pi * 2 + 2, :], in0=tf[t[2]][:], in1=tf[t[3]][:], op=ALU.add),
            reads=[("tf", t[2]), ("tf", t[3])], writes=["Z"])

    def fft_stage3(gi):
        zc = Z[:, 0, gi * 4:(gi + 1) * 4, :].rearrange("p a b -> p (a b)")
        zs = Z[:, 1, gi * 4:(gi + 1) * 4, :].rearrange("p a b -> p (a b)")
        for comp in range(2):
            b = 6 + comp
            if comp == 0:
                pairs = ((dft[:, 0, :], zc), (dft[:, 2, :], zs))
            else:
                pairs = ((dft[:, 1, :], zc), (dft[:, 0, :], zs))
            for ii, (w, z) in enumerate(pairs):
                P.op("pe", lambda e, w=w, z=z, b=b, ii=ii: e.matmul(
                    psum[:, b, :], lhsT=w, rhs=z, start=(ii == 0), stop=(ii == 1)),
                    reads=["Z", "dft"], writes=[("ps", b)], inc=(ii == 1))
            s = kb.rot("fo", 2)
            P.op("act", lambda e, s=s, b=b: e.copy(out=fo[s][:], in_=psum[:, b, :]),
                 reads=[("ps", b)], writes=[("fo", s)])
            P.op("sp", lambda e, s=s, comp=comp: e.dma_start(
                out=F_o[comp, :, gi * 512:(gi + 1) * 512], in_=fo[s][:]),
                reads=[("fo", s)], sem=f"stF{s}")

    for c in range(NCHK):
        s = c % 3
        P.op("sp", lambda e, c=c, s=s: e.dma_start(out=kvb[s][:], in_=kv_d[c]), writes=[("kvb", s)], sem=f"kvb{s}")
        if c >= 2:
            P.op("sp", lambda e, c=c, s=s: e.dma_start(out=qkb[s][:], in_=qk_d[c]), writes=[("qkb", s)],
                 sem=f"qkb{s}")
            s2 = c % 2
            for dc in range(2):
                P.op("pe", lambda e, s=s, dc=dc: e.matmul(
                    psum[:, 0, 0:128], lhsT=qkb[s][:, 256 + dc * 128:384 + dc * 128],
                    rhs=qkb[s][:, dc * 128:(dc + 1) * 128], start=(dc == 0), stop=(dc == 1)),
                    reads=[("qkb", s)], writes=[("ps", 0)], inc=(dc == 1))
            P.op("dve", lambda e, s2=s2: e.tensor_tensor(out=SM[s2][:], in0=psum[:, 0, 0:128], in1=MT[:], op=ALU.mult),
                 reads=[("ps", 0), "MT"], writes=[("SM", s2)])
            P.op("pool", lambda e, s=s, s2=s2: e.tensor_tensor(
                out=qx[s2][:].rearrange("p (a b) -> p a b", a=2),
                in0=qkb[s][:, 0:256].rearrange("p (a b) -> p a b", a=2),
                in1=xi[:].unsqueeze(1).broadcast_to([128, 2, 128]), op=ALU.mult),
                reads=[("qkb", s), "xi"], writes=[("qx", s2)])
            pb = 1 + (c % 2)
            sprev = (c - 1) % 2
            P.op("pe", lambda e, s=s, s2=s2, pb=pb: e.matmul(
                psum[:, pb, :], lhsT=SM[s2][:], rhs=kvb[s][:, 256:768], start=True, stop=False),
                reads=[("SM", s2), ("kvb", s)], writes=[("ps", pb)], inc=False)
            for dc in range(2):
                P.op("pe", lambda e, s2=s2, pb=pb, dc=dc, sprev=sprev: e.matmul(
                    psum[:, pb, :], lhsT=qx[s2][:, dc * 128:(dc + 1) * 128], rhs=Sbf[sprev][:, dc, :],
                    start=False, stop=(dc == 1)),
                    reads=[("qx", s2), ("Sbf", sprev)], writes=[("ps", pb)], inc=(dc == 1))
            P.op("act", lambda e, s2=s2, pb=pb: e.copy(out=osb[s2][:], in_=psum[:, pb, :]),
                 reads=[("ps", pb)], writes=[("osb", s2)])
            P.op("sp", lambda e, c=c, s2=s2: e.dma_start(out=o_o[c - 2], in_=osb[s2][:]),
                 reads=[("osb", s2)], sem=f"sto{s2}")
        s2 = c % 2
        P.op("act", lambda e, s=s, s2=s2: e.activation(
            out=kz[s2][:], in_=kvb[s][:, 0:256], func=AF.Copy, scale=zg[:, 0:1]),
            reads=[("kvb", s), "zg"], writes=[("kz", s2)])
        for dc in range(2):
            pb = 3 + dc
            P.op("pe", lambda e, s=s, s2=s2, dc=dc, pb=pb: e.matmul(
                psum[:, pb, :], lhsT=kz[s2][:, dc * 128:(dc + 1) * 128], rhs=kvb[s][:, 256:768],
                start=True, stop=True),
                reads=[("kz", s2), ("kvb", s)], writes=[("ps", pb)], inc=True)
            P.op("dve", lambda e, dc=dc, pb=pb: e.scalar_tensor_tensor(
                out=S[:, dc, :], in0=S[:, dc, :], scalar=zg[:, 1:2], in1=psum[:, pb, :],
                op0=ALU.mult, op1=ALU.add),
                reads=[("ps", pb), "zg", "S"], writes=["S"])
        P.op("act", lambda e, s2=s2: e.copy(out=Sbf[s2][:], in_=S[:]), reads=["S"], writes=[("Sbf", s2)])
        if c < 64:
            fft_stage1(c)
        elif c < 96:
            fft_stage3(c - 64)

    P.emit()
    return nc

def build_l3():
    kb = KB("l3")
    nc, P, sb, psum = kb.nc, kb.P, kb.sb, kb.psum
    x1_d = kb.din("x1T", [128, KC, NT])
    of_d = kb.din("ofT", [128, 16, NT])
    ob_d = kb.din("obT", [128, 16, NT])
    g_d = kb.din("gT", [128, 16, NT], BF16)
    ga_d = kb.din("gaT", [128, KC, NT], BF16)
    gf_d = kb.din("gfT", [128, KC, NT], BF16)
    fc_d = kb.din("FcT", [128, KC, NT], BF16)
    fs_d = kb.din("FsT", [128, KC, NT], BF16)
    mod_d = kb.din("modv", [128, 72, 2])
    nrm_d = kb.din("norms", [128, 4, KC])
    gnw_d = kb.din("gnw", [128, 16])
    cdsd_d = kb.din("cdsd", [128, 2, 2, 256], BF16)
    pret_d = kb.din("p_ret", [KC, 128, 16, 128])
    pfour_d = kb.din("p_four", [KC, 128, KC, 128])
    wout_d = kb.din("w_out", [KC, 128, KC, 128])
    w13_d = kb.din("w13_2", [NJ, 128, KC, 256])
    w2_d = kb.din("w2_2", [KC, 128, NJ, 128])
    out_d = kb.dout("out", [NT, D])

    kb.common()
    xres = sb("xres", [128, KC, NT], F32)
    hT = sb("hT", [128, KC, NT // 2], BF16)
    gbuf = sb("gbuf", [128, NJ, NT // 2], BF16)
    w13b = [sb(f"w13b{i}", [128, KC, 256], BF16) for i in range(2)]
    w2b = [sb(f"w2b{i}", [128, NJ, 128], BF16) for i in range(2)]
    mbuf = sb("mbuf", [128, KC, TG], F32)
    osum = sb("osum", [128, 4, TG], F32)
    ldf = [sb(f"ldf{i}", [128, TG], F32) for i in range(4)]
    ldb = [sb(f"ldb{i}", [128, TG], BF16) for i in range(4)]
    b16 = [sb(f"b16{i}", [128, TG], BF16) for i in range(4)]
    mean = sb("mean", [128, TG], F32)
    rs2 = sb("rs2", [128, TG], F32)
    mod = sb("mod", [128, 72, 2], F32)
    nrm = sb("nrm", [128, 4, KC], F32)
    gnw = sb("gnw", [128, 16], F32)
    eff = sb("eff", [128, 3, KC, 2], F32)
    gate = sb("gate", [128, 3, KC, 2], F32)
    cdsd = sb("cdsd", [128, 2, 2, 256], BF16)
    tmpf = kb.tmpf

    ogT = gbuf[:, 0:8, :].rearrange("p a (b t) -> p (a b) t", t=TG)
    mixT = gbuf[:, 8:12, :].rearrange("p a (b t) -> p (a b) t", t=TG)
    mT = gbuf[:, 12:16, :].rearrange("p a (b t) -> p (a b) t", t=TG)
    fcT = hT[:, 0:4, :].rearrange("p a (b t) -> p (a b) t", t=TG)
    fsT = hT[:, 4:8, :].rearrange("p a (b t) -> p (a b) t", t=TG)
    xn = mbuf
    otile = [osum[:, 0:2, :].rearrange("p a t -> p (a t)"), osum[:, 2:4, :].rearrange("p a t -> p (a t)")]

    for (t, dsrc, key) in ((mod, mod_d, "modv"), (nrm, nrm_d, "nrm"), (gnw, gnw_d, "gnw"), (cdsd, cdsd_d, "cdsd")):
        P.op("sp", lambda e, t=t, dsrc=dsrc: e.dma_start(out=t[:], in_=dsrc), writes=[key], sem="ld0")
    for n in range(3):
        P.op("dve", lambda e, n=n: e.scalar_tensor_tensor(
            out=eff[:, n], in0=mod[:, (3 * n + 1) * 8:(3 * n + 2) * 8, :], scalar=1.0,
            in1=nrm[:, n, :].unsqueeze(2).broadcast_to([128, KC, 2]), op0=ALU.add, op1=ALU.mult),
            reads=["modv", "nrm"], writes=["modv"])
        P.op("dve", lambda e, n=n: e.tensor_scalar(
            out=gate[:, n], in0=mod[:, (3 * n + 2) * 8:(3 * n + 3) * 8, :],
            scalar1=(1.0 if n == 1 else 0.5), scalar2=None, op0=ALU.mult),
            reads=["modv"], writes=["modv"])
    for tg in range(NTG):
        P.op("sp", lambda e, tg=tg: e.dma_start(out=xres[:, :, tg * TG:(tg + 1) * TG],
                                                in_=x1_d[:, :, tg * TG:(tg + 1) * TG]),
             writes=[("xres", tg)], sem="ld_x")

    def wstream(src_blk, nk, s):
        dst = w13b[s][:].rearrange("p a b -> p (a b)")[:, 0:nk * 128].rearrange("p (a b) -> p a b", b=128)
        P.op("pool", lambda e: e.dma_start(out=dst, in_=src_blk), writes=[("w13b", s)], sem=f"w13b{s}")
        return dst

    def mixer(tg):
        c0, c1 = tg * TG, (tg + 1) * TG
        for h in range(4):
            for vc in range(4):
                f = h * 4 + vc
                a, b = kb.rot("ldf", 4), kb.rot("ldf", 4)
                P.op("sp", lambda e, f=f, a=a: e.dma_start(out=ldf[a][:], in_=of_d[:, f, c0:c1]),
                     writes=[("ldf", a)], sem=f"ldf{a}")
                P.op("sp", lambda e, f=f, b=b: e.dma_start(out=ldf[b][:], in_=ob_d[:, f, c0:c1]),
                     writes=[("ldf", b)], sem=f"ldf{b}")
                P.op("dve", lambda e, vc=vc, a=a, b=b: e.tensor_tensor(
                    out=osum[:, vc, :], in0=ldf[a][:], in1=ldf[b][:], op=ALU.add),
                    reads=[("ldf", a), ("ldf", b)], writes=[("osum", vc)])
                u1, u2 = kb.rot("b16", 4), kb.rot("b16", 4)
                P.op("act", lambda e, vc=vc, u1=u1: e.copy(out=b16[u1][:], in_=osum[:, vc, :]),
                     reads=[("osum", vc)], writes=[("b16", u1)])
                P.op("act", lambda e, vc=vc, u2=u2: e.activation(out=b16[u2][:], in_=osum[:, vc, :], func=AF.Square),
                     reads=[("osum", vc)], writes=[("b16", u2)])
                P.op("pe", lambda e, vc=vc, u1=u1: e.matmul(
                    psum[:, 6, :], lhsT=kb.ones_b[:], rhs=b16[u1][:], start=(vc == 0), stop=(vc == 3)),
                    reads=[("b16", u1), "ones_b"], writes=[("ps", 6)], inc=True)
                P.op("pe", lambda e, vc=vc, u2=u2: e.matmul(
                    psum[:, 7, :], lhsT=kb.ones_b[:], rhs=b16[u2][:], start=(vc == 0), stop=(vc == 3)),
                    reads=[("b16", u2), "ones_b"], writes=[("ps", 7)], inc=True)
            P.op("act", lambda e: e.activation(out=mean[:], in_=psum[:, 6, :], func=AF.Copy, scale=1.0 / DV),
                 reads=[("ps", 6)], writes=["mean"])
            t = kb.rot("tmpf", 4)
            P.op("dve", lambda e, t=t: e.tensor_tensor(out=tmpf[t][:], in0=mean[:], in1=mean[:], op=ALU.mult),
                 reads=["mean"], writes=[("tmpf", t)])
            P.op("dve", lambda e, t=t: e.scalar_tensor_tensor(
                out=rs2[:], in0=psum[:, 7, :], scalar=1.0 / DV, in1=tmpf[t][:], op0=ALU.mult, op1=ALU.subtract),
                reads=[("ps", 7), ("tmpf", t)], writes=["rs2"])
            P.op("act", lambda e: e.activation(out=rs2[:], in_=rs2[:], func=AF.Sqrt, bias=kb.epsc[:, 0:1]),
                 reads=["rs2", "epsc"], writes=["rs2"])
            P.op("dve", lambda e: e.reciprocal(out=rs2[:], in_=rs2[:]), reads=["rs2"], writes=["rs2"])
            for vc in range(4):
                f = h * 4 + vc
                gb = kb.rot("ldb", 4)
                P.op("sp", lambda e, f=f, gb=gb: e.dma_start(out=ldb[gb][:], in_=g_d[:, f, c0:c1]),
                     writes=[("ldb", gb)], sem=f"ldb{gb}")
                t1, t2 = kb.rot("tmpf", 4), kb.rot("tmpf", 4)
                P.op("act", lambda e, gb=gb, t1=t1: e.activation(out=tmpf[t1][:], in_=ldb[gb][:], func=AF.Silu),
                     reads=[("ldb", gb)], writes=[("tmpf", t1)])
                P.op("dve", lambda e, vc=vc, t2=t2: e.tensor_tensor(
                    out=tmpf[t2][:], in0=osum[:, vc, :], in1=mean[:], op=ALU.subtract),
                    reads=[("osum", vc), "mean"], writes=[("tmpf", t2)])
                P.op("dve", lambda e, t2=t2: e.tensor_tensor(
                    out=tmpf[t2][:], in0=tmpf[t2][:], in1=rs2[:], op=ALU.mult),
                    reads=[("tmpf", t2), "rs2"], writes=[("tmpf", t2)])
                P.op("dve", lambda e, f=f, t1=t1, t2=t2: e.scalar_tensor_tensor(
                    out=ogT[:, f, :], in0=tmpf[t2][:], scalar=gnw[:, f:f + 1], in1=tmpf[t1][:],
                    op0=ALU.mult, op1=ALU.mult),
                    reads=[("tmpf", t1), ("tmpf", t2), "gnw"], writes=["ogT"])
        P.op("sp", lambda e: e.dma_start(out=fcT, in_=fc_d[:, :, c0:c1]), writes=["fcT"], sem="ld_fc")
        P.op("sp", lambda e: e.dma_start(out=fsT, in_=fs_d[:, :, c0:c1]), writes=["fcT"], sem="ld_fc")
        for dc in range(KC):
            s = kb.rot("w13b", 2)
            wv = wstream(pret_d[dc], 16, s)
            pb = kb.next_bank()
            for kc in range(16):
                P.op("pe", lambda e, wv=wv, kc=kc, pb=pb: e.matmul(
                    psum[:, pb, :], lhsT=wv[:, kc, :], rhs=ogT[:, kc, :], start=(kc == 0), stop=(kc == 15)),
                    reads=[("w13b", s), "ogT"], writes=[("ps", pb)], inc=(kc == 15))
            gb = kb.rot("ldb", 4)
            P.op("sp", lambda e, dc=dc, gb=gb: e.dma_start(out=ldb[gb][:], in_=ga_d[:, dc, c0:c1]),
                 writes=[("ldb", gb)], sem=f"ldb{gb}")
            t1 = kb.rot("tmpf", 4)
            P.op("act", lambda e, gb=gb, t1=t1: e.activation(out=tmpf[t1][:], in_=ldb[gb][:], func=AF.Sigmoid),
                 reads=[("ldb", gb)], writes=[("tmpf", t1)])
            P.op("dve", lambda e, dc=dc, pb=pb, t1=t1: e.tensor_tensor(
                out=mbuf[:, dc, :], in0=psum[:, pb, :], in1=tmpf[t1][:], op=ALU.mult),
                reads=[("ps", pb), ("tmpf", t1)], writes=[("mbuf", dc)])
        for co in range(KC):
            gI, cc = co // 2, co % 2
            pb = kb.next_bank()
            k = 0
            for kc2 in range(2):
                for comp, src in ((0, fcT), (1, fsT)):
                    P.op("pe", lambda e, kc2=kc2, comp=comp, src=src, pb=pb, k=k, gI=gI, cc=cc: e.matmul(
                        psum[:, pb, :], lhsT=cdsd[:, kc2, comp, cc * 128:(cc + 1) * 128],
                        rhs=src[:, gI * 2 + kc2, :], start=(k == 0), stop=(k == 3)),
                        reads=["cdsd", "fcT"], writes=[("ps", pb)], inc=(k == 3))
                    k += 1
            P.op("act", lambda e, co=co, pb=pb: e.copy(out=mixT[:, co, :], in_=psum[:, pb, :]),
                 reads=[("ps", pb)], writes=["mixT"])
        for dc in range(KC):
            s = kb.rot("w13b", 2)
            wv = wstream(pfour_d[dc], KC, s)
            pb = kb.next_bank()
            for kc in range(KC):
                P.op("pe", lambda e, wv=wv, kc=kc, pb=pb: e.matmul(
                    psum[:, pb, :], lhsT=wv[:, kc, :], rhs=mixT[:, kc, :], start=(kc == 0), stop=(kc == KC - 1)),
                    reads=[("w13b", s), "mixT"], writes=[("ps", pb)], inc=(kc == KC - 1))
            gb = kb.rot("ldb", 4)
            P.op("sp", lambda e, dc=dc, gb=gb: e.dma_start(out=ldb[gb][:], in_=gf_d[:, dc, c0:c1]),
                 writes=[("ldb", gb)], sem=f"ldb{gb}")
            t1, t2 = kb.rot("tmpf", 4), kb.rot("tmpf", 4)
            P.op("act", lambda e, gb=gb, t1=t1: e.activation(out=tmpf[t1][:], in_=ldb[gb][:], func=AF.Sigmoid),
                 reads=[("ldb", gb)], writes=[("tmpf", t1)])
            P.op("dve", lambda e, pb=pb, t1=t1, t2=t2: e.tensor_tensor(
                out=tmpf[t2][:], in0=psum[:, pb, :], in1=tmpf[t1][:], op=ALU.mult),
                reads=[("ps", pb), ("tmpf", t1)], writes=[("tmpf", t2)])
            P.op("dve", lambda e, dc=dc, t2=t2: e.tensor_tensor(
                out=mT[:, dc, :], in0=mbuf[:, dc, :], in1=tmpf[t2][:], op=ALU.add),
                reads=[("mbuf", dc), ("tmpf", t2)], writes=["mT"])
        for dc in range(KC):
            s = kb.rot("w13b", 2)
            wv = wstream(wout_d[dc], KC, s)
            pb = kb.next_bank()
            for kc in range(KC):
                P.op("pe", lambda e, wv=wv, kc=kc, pb=pb: e.matmul(
                    psum[:, pb, :], lhsT=wv[:, kc, :], rhs=mT[:, kc, :], start=(kc == 0), stop=(kc == KC - 1)),
                    reads=[("w13b", s), "mT"], writes=[("ps", pb)], inc=(kc == KC - 1))
            P.op("dve", lambda e, dc=dc, pb=pb: e.scalar_tensor_tensor(
                out=xres[:, dc, c0:c1], in0=psum[:, pb, :], scalar=gate[:, 1, dc, 0:1],
                in1=xres[:, dc, c0:c1], op0=ALU.mult, op1=ALU.add),
                reads=[("ps", pb), "modv", ("xres", tg)], writes=[("xres", tg)])

    for tile in range(2):
        for loc in range(2):
            mixer(tile * 2 + loc)
        P.barrier()
        groups = []
        for loc in range(2):
            tg = tile * 2 + loc
            groups.append(dict(
                h=(lambda kc, loc=loc: hT[:, kc, loc * TG:(loc + 1) * TG]), hk=("hT", loc),
                g=(lambda j, loc=loc: gbuf[:, j, loc * TG:(loc + 1) * TG]), gk=("gbuf", loc),
                x=(lambda dc, tg=tg: xres[:, dc, tg * TG:(tg + 1) * TG]), xk=("xres", tg),
                n=TG, tg=tg))
        for G in groups:
            kb.norm_mod(G["x"], G["xk"], G["h"], G["hk"], TG,
                        lambda kc: eff[:, 2, kc, 0:1], lambda kc: mod[:, 6 * 8 + kc, 0:1])
        kb.ffn(groups, w13_d, w2_d, w13b, w2b, lambda dc: gate[:, 2, dc, 0:1])
        for G in groups:
            tg = G["tg"]
            slot = kb.rot("rstd", 2)
            kb.rms_stats(G["x"], G["xk"], TG, slot)
            for kc in range(KC):
                t = kb.rot("tmpf", 4)
                P.op("dve", lambda e, kc=kc, t=t, G=G, slot=slot: e.tensor_tensor(
                    out=tmpf[t][:], in0=G["x"](kc), in1=kb.rstd[slot][:], op=ALU.mult),
                    reads=[G["xk"], ("rstd", slot)], writes=[("tmpf", t)])
                P.op("act", lambda e, kc=kc, t=t: e.activation(
                    out=xn[:, kc, :], in_=tmpf[t][:], func=AF.Copy, scale=nrm[:, 3, kc:kc + 1]),
                    reads=[("tmpf", t), "nrm"], writes=["xn"])
            for cc in range(4):
                c = tg * 4 + cc
                s = c % 2
                for half in range(2):
                    b = kb.next_bank()
                    for q in range(4):
                        dc = half * 4 + q
                        P.op("pe", lambda e, cc=cc, dc=dc, b=b, q=q: e.transpose(
                            out=psum[:, b, q * 128:(q + 1) * 128], in_=xn[:, dc, cc * 128:(cc + 1) * 128],
                            identity=kb.ident[:]),
                            reads=["xn", "ident"], writes=[("ps", b)], inc=(q == 3))
                    if half == 0:
                        P.op("act", lambda e, s=s, b=b: e.copy(out=otile[s][:, 0:512], in_=psum[:, b, :]),
                             reads=[("ps", b)], writes=[("otile", s)])
                    else:
                        P.op("dve", lambda e, s=s, b=b: e.tensor_copy(out=otile[s][:, 512:1024], in_=psum[:, b, :]),
                             reads=[("ps", b)], writes=[("otile", s)])
                P.op("sp", lambda e, c=c, s=s: e.dma_start(out=out_d[c * 128:(c + 1) * 128, :], in_=otile[s][:]),
                     reads=[("otile", s)], sem=f"st{s}")
        P.barrier()

    P.emit()
    return nc

def carve(t, off, shape, dt):
    n = int(np.prod(shape[1:])) * (2 if dt == F32 else 1)
    ap = t[:, off:off + n]
    if dt == F32:
        ap = ap.bitcast(F32)
    if len(shape) == 3:
        ap = ap.rearrange("p (a b) -> p a b", a=shape[1])
    elif len(shape) == 4:
        ap = ap.rearrange("p (a b c) -> p a b c", a=shape[1], b=shape[2])
    return ap


NB = NT // 128
RG = [list(range(NCORES))]


def build_fused():
    kb = KB("fused")
    nc, P, sb, psum = kb.nc, kb.P, kb.sb, kb.psum
    x_d = kb.din("x", [NT, D])
    ctx_d = kb.din("ctx", [CTX, D])
    c_d = kb.din("cvec", [128, KC, 2])
    wada_d = kb.din("w_ada", [72, 128, KC, 128])
    bada_d = kb.din("b_ada", [128, 72])
    nrm_d = kb.din("norms", [128, 4, KC])
    w13a_d = kb.din("w13_1", [NJ, 128, KC, 256])
    w2a_d = kb.din("w2_1", [KC, 128, NJ, 128])
    w13c_d = kb.din("w13_2", [NJ, 128, KC, 256])
    w2c_d = kb.din("w2_2", [KC, 128, NJ, 128])
    winf_d = kb.din("w_inF", [32, 128, KC, 128])
    wint_d = kb.din("w_inT", [12, 128, KC, 512])
    rope_d = kb.din("rope", [2, 128, 4, NT // 2])
    dec_d = kb.din("dec", [128, 8])
    tabs_d = kb.din("tabs", [128, 4, 128])
    xiex_d = kb.din("xiex", [128, 2, 128])
    pex_d = kb.din("pex", [128, 40])
    coefx_d = kb.din("coefx", [128, 2, 9])
    coefm_d = kb.din("coefm", [128, 2, 9])
    gnw_d = kb.din("gnw", [128, 16])
    cdsd_d = kb.din("cdsd", [128, 2, 2, 256], BF16)
    dft_d = kb.din("dft", [128, 3, 128], BF16)
    cs_d = kb.din("cs", [128, 256], BF16)
    tw_d = kb.din("tw", [128, 2, 128])
    idb_d = kb.din("identb", [128, 128], BF16)
    pret_d = kb.din("p_ret", [KC, 128, 16, 128])
    pfour_d = kb.din("p_four", [KC, 128, KC, 128])
    wout_d = kb.din("w_out", [KC, 128, KC, 128])
    out_d = kb.dout("out", [NT, D])

    def dscr(name, shape, dt=BF16, shared=False):
        if shared:
            return nc.dram_tensor(name, list(shape), dt, addr_space="Shared").ap()
        if os.environ.get("MK_DBG") and name not in ("fsend", "totsend", "f2send"):
            return nc.dram_tensor(name, list(shape), dt, kind="ExternalOutput").ap()
        return nc.dram_tensor(name, list(shape), dt).ap()

    Uq = dscr("Uq", [16, 128, NT])
    Ug = dscr("Ug", [16, 128, NT])
    Ktok = dscr("Ktok", [NT, 1024])
    Vtok = dscr("Vtok", [NT, 2048])
    Gtok = dscr("Gtok", [NT, 2048])
    fsend = dscr("fsend", [8 * NT, 128])
    fgath = dscr("fgath", [8 * 8 * NT, 128], shared=True)
    totsend = dscr("totsend", [8 * 256, 512], F32)
    totg = dscr("totg", [8 * 8 * 256, 512], F32, shared=True)
    sctx = dscr("sctx", [8 * 256, 512], F32)
    sstart = dscr("sstart", [8 * 256, 512], F32)
    snaps = dscr("snaps", [4 * NB * 128, 1024])
    f2send = dscr("f2send", [2 * 8 * 128, NT])
    f2g = dscr("f2g", [8 * 2 * 8 * 128, NT], shared=True)
    floc = dscr("floc", [8 * 16, 128 * 128])
    f2loc = dscr("f2loc", [8 * 2 * 128, NT])

    kb.common()
    xres = sb("xres", [128, KC, NT], F32)
    gb_raw = sb("gbuf", [128, NJ * 1024], BF16)
    A2 = sb("A2", [128, 19456], BF16)
    gbuf = carve(gb_raw, 0, [128, NJ, 1024], BF16)
    wbuf = [sb(f"wbuf{i}", [128, NJ * 128], BF16) for i in range(2)]
    w13b = [carve(w, 0, [128, KC, 256], BF16) for w in wbuf]
    w2b = [carve(w, 0, [128, NJ, 128], BF16) for w in wbuf]
    identb = sb("identb", [128, 128], BF16)
    cv = sb("cv", [128, KC, 2], F32)
    cvs = sb("cvs", [128, KC, 2], BF16)
    bada = sb("bada", [128, 72], F32)
    nrm = sb("nrm", [128, 4, KC], F32)
    mod = sb("mod", [128, 72, 2], F32)
    eff = sb("eff", [128, 3, KC, 2], F32)
    gate = sb("gate", [128, 3, KC, 2], F32)
    ubuf = [sb(f"ubuf{i}", [128, TG], BF16) for i in range(4)]
    dec = sb("dec", [128, 8], F32)
    lg = sb("lg", [128, 8], F32)
    one1 = sb("one1", [128, 1], F32)
    tabs = sb("tabs", [128, 4, 128], F32)
    xiex = sb("xiex", [128, 2, 128], F32)
    pex = sb("pex", [128, 40], F32)
    coefx = sb("coefx", [128, 2, 9], F32)
    coefm = sb("coefm", [128, 2, 9], F32)
    coef = sb("coef", [128, 8, 9], F32)
    zt = sb("zt", [128, 8, 16], F32)
    zc = sb("zc", [128, 8, 2], F32)
    zeta = sb("zeta", [128, 8], F32)
    gch = sb("gch", [128, 8], F32)
    mt = sb("mt", [128, 4, 128], F32)
    xi = sb("xi", [128, 8, 128], F32)
    tmpe = sb("tmpe", [128, 128], F32)
    gnw = sb("gnw", [128, 16], F32)
    cdsd = sb("cdsd", [128, 2, 2, 256], BF16)
    dft = sb("dft", [128, 3, 128], BF16)
    cs = sb("cs", [128, 256], BF16)
    tw = sb("tw", [128, 2, 128], F32)
    small = sb("small", [128, 32], F32)
    tmpf = kb.tmpf

    for (t, dsrc, key) in ((cv, c_d, "cv"), (bada, bada_d, "bada"), (nrm, nrm_d, "nrm"), (dec, dec_d, "dec"),
                           (tabs, tabs_d, "tabs"), (xiex, xiex_d, "xiex"), (pex, pex_d, "pex"),
                           (coefx, coefx_d, "coefx"), (coefm, coefm_d, "coefm"), (gnw, gnw_d, "gnw"),
                           (cdsd, cdsd_d, "cdsd"), (dft, dft_d, "dft"), (cs, cs_d, "cs"), (tw, tw_d, "tw"),
                           (identb, idb_d, "identb")):
        P.op("sp", lambda e, t=t, dsrc=dsrc: e.dma_start(out=t[:], in_=dsrc), writes=[key], sem=f"ld_{key}")
    P.op("dve", lambda e: e.memset(one1[:], 1.0), writes=["one1"])
    P.op("act", lambda e: e.activation(out=cvs[:], in_=cv[:], func=AF.Silu), reads=["cv"], writes=["cvs"])
    P.op("act", lambda e: e.activation(out=lg[:], in_=dec[:], func=AF.Exp, scale=-1.0), reads=["dec"], writes=["lg"])
    P.op("act", lambda e: e.activation(out=lg[:], in_=lg[:], func=AF.Ln, bias=one1[:, 0:1]),
         reads=["lg", "one1"], writes=["lg"])
    P.op("dve", lambda e: e.tensor_scalar(out=lg[:], in0=lg[:], scalar1=-1.0, scalar2=None, op0=ALU.mult),
         reads=["lg"], writes=["lg"])
    for dh in range(8):
        d, h = dh // 4, dh % 4
        sc = lg[:, dh:dh + 1]
        ecol = 2 + 16 * d
        P.op("act", lambda e, dh=dh, sc=sc, ecol=ecol: e.activation(
            out=zt[:, dh, :], in_=pex[:, ecol:ecol + 16], func=AF.Exp, scale=sc), reads=["lg", "pex"], writes=["dtab"])
        P.op("act", lambda e, dh=dh, sc=sc, d=d: e.activation(
            out=zc[:, dh, :], in_=pex[:, 34 + 2 * d:36 + 2 * d], func=AF.Exp, scale=sc), reads=["lg", "pex"], writes=["dtab"])
        P.op("act", lambda e, dh=dh, sc=sc, d=d: e.activation(
            out=zeta[:, dh:dh + 1], in_=pex[:, d:d + 1], func=AF.Exp, scale=sc), reads=["lg", "pex"], writes=["dtab"])
        P.op("act", lambda e, dh=dh, sc=sc: e.activation(
            out=gch[:, dh:dh + 1], in_=pex[:, 38:39], func=AF.Exp, scale=sc), reads=["lg", "pex"], writes=["dtab"])
        P.op("act", lambda e, dh=dh, sc=sc, d=d: e.activation(
            out=xi[:, dh, :], in_=xiex[:, d, :], func=AF.Exp, scale=sc), reads=["lg", "xiex"], writes=["dtab"])
        P.op("act", lambda e, dh=dh, sc=sc, d=d: e.activation(
            out=coef[:, dh, :], in_=coefx[:, d, :], func=AF.Exp, scale=sc), reads=["lg", "coefx"], writes=["coef"])
        P.op("dve", lambda e, dh=dh, d=d: e.tensor_tensor(
            out=coef[:, dh, :], in0=coef[:, dh, :], in1=coefm[:, d, :], op=ALU.mult),
            reads=["coef", "coefm"], writes=["coef"])
    for h in range(4):
        P.op("act", lambda e, h=h: e.activation(out=tmpe[:], in_=tabs[:, 0, :], func=AF.Exp, scale=lg[:, h:h + 1]),
             reads=["lg", "tabs"], writes=["tmpe"])
        P.op("dve", lambda e, h=h: e.tensor_tensor(out=mt[:, h, :], in0=tmpe[:], in1=tabs[:, 1, :], op=ALU.mult),
             reads=["tmpe", "tabs"], writes=["dtab"])
        P.op("act", lambda e, h=h: e.activation(out=tmpe[:], in_=tabs[:, 2, :], func=AF.Exp,
                                                scale=lg[:, 4 + h:5 + h]),
             reads=["lg", "tabs", "dtab"], writes=["tmpe"])
        P.op("dve", lambda e, h=h: e.tensor_tensor(out=tmpe[:], in0=tmpe[:], in1=tabs[:, 3, :], op=ALU.mult),
             reads=["tmpe", "tabs"], writes=["tmpe"])
        P.op("dve", lambda e, h=h: e.tensor_tensor(out=mt[:, h, :], in0=mt[:, h, :], in1=tmpe[:], op=ALU.add),
             reads=["tmpe", "dtab"], writes=["dtab"])

    for nb in range(72):
        s = kb.rot("wbuf", 2)
        P.op("pool", lambda e, nb=nb, s=s: e.dma_start(out=w13b[s][:, :, 0:128], in_=wada_d[nb]),
             writes=[("wbuf", s)], sem=f"wbuf{s}")
        for kc in range(KC):
            P.op("pe", lambda e, nb=nb, s=s, kc=kc: e.matmul(
                psum[:, 7, nb * 2:nb * 2 + 2], lhsT=w13b[s][:, kc, 0:128], rhs=cvs[:, kc, :],
                start=(kc == 0), stop=(kc == KC - 1)),
                reads=[("wbuf", s), "cvs"], writes=[("ps", 7)], inc=(kc == KC - 1))
    P.op("dve", lambda e: e.tensor_tensor(
        out=mod[:], in0=psum[:, 7, 0:144].rearrange("p (n s) -> p n s", s=2),
        in1=bada[:].unsqueeze(2).broadcast_to([128, 72, 2]), op=ALU.add),
        reads=[("ps", 7), "bada"], writes=["modv"])
    for n in range(3):
        P.op("dve", lambda e, n=n: e.scalar_tensor_tensor(
            out=eff[:, n], in0=mod[:, (3 * n + 1) * 8:(3 * n + 2) * 8, :], scalar=1.0,
            in1=nrm[:, n, :].unsqueeze(2).broadcast_to([128, KC, 2]), op0=ALU.add, op1=ALU.mult),
            reads=["modv", "nrm"], writes=["modv"])
        P.op("dve", lambda e, n=n: e.tensor_scalar(
            out=gate[:, n], in0=mod[:, (3 * n + 2) * 8:(3 * n + 3) * 8, :],
            scalar1=(1.0 if n == 1 else 0.5), scalar2=None, op0=ALU.mult),
            reads=["modv"], writes=["modv"])

    def sc_ap(n, s):
        return lambda kc: eff[:, n, kc, s:s + 1]

    def sh_ap(n, s):
        return lambda kc: mod[:, (3 * n) * 8 + kc, s:s + 1]

    def gt_ap(n, s):
        return lambda dc: gate[:, n, dc, s:s + 1]

    hT = carve(A2, 0, [128, KC, 1024], BF16)
    ropet = carve(A2, 8192, [128, 4, 1024], F32)
    xt = [carve(A2, 8192 + i * 2048, [128, D], F32) for i in range(2)]

    kb.load_tokmajor_to_featmajor(
        x_d, NT // 128, lambda c, half: xres[:, half * 4:half * 4 + 4, c * 128:(c + 1) * 128],
        lambda c: ("xres", c // 4), xt)
    xcs = carve(A2, 12288, [128, KC, CTX], F32)
    P.barrier()

    ktokbuf = carve(gb_raw, 0, [128, 8, 1024], BF16)
    wtok = [carve(gb_raw, 8192 + i * 4096, [128, KC, 512], BF16) for i in range(2)]
    stg = [carve(gb_raw, 16384 + i * 512, [128, 512], BF16) for i in range(4)]
    kctok = sb("kctok", [128, 2, 1024], BF16)
    vctok = sb("vctok", [128, 2, 2048], BF16)

    def proj_feat(groups, chunks, rope, out_fn, tile):
        pend = {}
        for ch in chunks:
            s = kb.rot("wbuf", 2)
            P.op("pool", lambda e, ch=ch, s=s: e.dma_start(out=w13b[s][:, :, 0:128], in_=winf_d[ch]),
                 writes=[("wbuf", s)], sem=f"wbuf{s}")
            for gi, G in enumerate(groups):
                n = G["n"]
                pb = kb.next_bank()
                for kc in range(KC):
                    P.op("pe", lambda e, s=s, kc=kc, G=G, pb=pb, n=n: e.matmul(
                        psum[:, pb, :n], lhsT=w13b[s][:, kc, 0:128], rhs=G["h"](kc),
                        start=(kc == 0), stop=(kc == KC - 1)),
                        reads=[("wbuf", s), G["hk"]], writes=[("ps", pb)], inc=(kc == KC - 1))
                if rope and ch < 16:
                    if ch % 2 == 0:
                        pend[gi] = pb
                        continue
                    pa = pend[gi]
                    c0 = G["col0"]
                    ti = 0 if ch < 8 else 2
                    csn = ropet[:, ti, c0:c0 + n]
                    sn = ropet[:, ti + 1, c0:c0 + n]
                    t1, t2, t3, t4 = [kb.rot("tmpf", 4) for _ in range(4)]
                    tm = tmpf
                    for (tt, pp, tb) in ((t1, pa, csn), (t2, pb, sn), (t3, pa, sn), (t4, pb, csn)):
                        P.op("dve", lambda e, tt=tt, pp=pp, tb=tb, n=n: e.tensor_tensor(
                            out=tm[tt][:, :n], in0=psum[:, pp, :n], in1=tb, op=ALU.mult),
                            reads=[("ps", pp), "ropet"], writes=[("tmpf", tt)])
                    u1, u2 = kb.rot("ubuf", 4), kb.rot("ubuf", 4)
                    P.op("dve", lambda e, t1=t1, t2=t2, u1=u1, n=n: e.tensor_tensor(
                        out=ubuf[u1][:, :n], in0=tm[t1][:, :n], in1=tm[t2][:, :n], op=ALU.subtract),
                        reads=[("tmpf", t1), ("tmpf", t2)], writes=[("ubuf", u1)])
                    P.op("dve", lambda e, t3=t3, t4=t4, u2=u2, n=n: e.tensor_tensor(
                        out=ubuf[u2][:, :n], in0=tm[t3][:, :n], in1=tm[t4][:, :n], op=ALU.add),
                        reads=[("tmpf", t3), ("tmpf", t4)], writes=[("ubuf", u2)])
                    for (uu, chh) in ((u1, ch - 1), (u2, ch)):
                        P.op("sp", lambda e, uu=uu, chh=chh, G=G, n=n: e.dma_start(
                            out=out_fn(chh, G), in_=ubuf[uu][:, :n]),
                            reads=[("ubuf", uu)], writes=["Uq"], sem=f"stu{uu}")
                        if chh >= 8:
                            pt = kb.next_bank()
                            pv = psum[:, pt, 0:256].bitcast(BF16)
                            for q in range(4):
                                P.op("pe", lambda e, uu=uu, q=q, pv=pv: e.transpose(
                                    out=pv[:, q * 128:(q + 1) * 128], in_=ubuf[uu][:, q * 128:(q + 1) * 128],
                                    identity=identb[:]),
                                    reads=[("ubuf", uu), "identb"], writes=[("ps", pt)], inc=(q == 3))
                            b0 = (G["col0"] // 128)
                            P.op("act", lambda e, pv=pv, b0=b0, chh=chh: e.copy(
                                out=ktokbuf[:, b0:b0 + 4, (chh - 8) * 128:(chh - 7) * 128],
                                in_=pv.rearrange("p (q t) -> p q t", q=4)),
                                reads=[("ps", pt)], writes=["ktokbuf"])
                else:
                    u = kb.rot("ubuf", 4)
                    P.op("act", lambda e, pb=pb, u=u, n=n: e.copy(out=ubuf[u][:, :n], in_=psum[:, pb, :n]),
                         reads=[("ps", pb)], writes=[("ubuf", u)])
                    P.op("sp", lambda e, u=u, ch=ch, G=G, n=n: e.dma_start(
                        out=out_fn(ch, G), in_=ubuf[u][:, :n]),
                        reads=[("ubuf", u)], writes=["Uq"], sem=f"stu{u}")

    def proj_tok(hview, hk, nblk, cbs, sink):
        for cb in cbs:
            s = kb.rot("wtok", 2)
            for half in range(4):
                P.op("pool", lambda e, cb=cb, s=s, half=half: e.dma_start(
                    out=wtok[s][:, half * 2:half * 2 + 2, :], in_=wint_d[cb, :, half * 2:half * 2 + 2, :]),
                    writes=[("wtok", s)], sem=f"wtok{s}")
            for blk in range(nblk):
                pb = kb.next_bank()
                for kc in range(KC):
                    P.op("pe", lambda e, s=s, kc=kc, blk=blk, pb=pb: e.matmul(
                        psum[:, pb, :], lhsT=hview(kc, blk), rhs=wtok[s][:, kc, :],
                        start=(kc == 0), stop=(kc == KC - 1)),
                        reads=[("wtok", s), hk], writes=[("ps", pb)], inc=(kc == KC - 1))
                sink(cb, blk, pb)

    for tile in range(2):
        P.op("sp", lambda e, tile=tile: e.dma_start(out=ropet, in_=rope_d[tile]), writes=["ropet"], sem="ld_rope")
        groups = []
        for loc in range(2):
            tg = tile * 2 + loc
            groups.append(dict(
                h=(lambda kc, loc=loc: hT[:, kc, loc * TG:(loc + 1) * TG]), hk=("hT", loc),
                g=(lambda j, loc=loc: gbuf[:, j, loc * TG:(loc + 1) * TG]), gk=("gbuf", loc),
                x=(lambda dc, tg=tg: xres[:, dc, tg * TG:(tg + 1) * TG]), xk=("xres", tg),
                n=TG, col0=loc * TG, tg=tg))
        for G in groups:
            kb.norm_mod(G["x"], G["xk"], G["h"], G["hk"], TG, sc_ap(0, 0), sh_ap(0, 0))
        kb.ffn(groups, w13a_d, w2a_d, w13b, w2b, gt_ap(0, 0), wk13="wbuf", wk2="wbuf")
        for G in groups:
            kb.norm_mod(G["x"], G["xk"], G["h"], G["hk"], TG, sc_ap(1, 0), sh_ap(1, 0))
        P.barrier()

        def outf(ch, G):
            if ch < 16:
                return Uq[ch, :, G["tg"] * TG:(G["tg"] + 1) * TG]
            return Ug[ch - 16, :, G["tg"] * TG:(G["tg"] + 1) * TG]
        proj_feat(groups, list(range(32)), True, outf, tile)
        for blk in range(8):
            r0 = tile * 1024 + blk * 128
            P.op("sp", lambda e, blk=blk, r0=r0: e.dma_start(out=Ktok[r0:r0 + 128, :], in_=ktokbuf[:, blk, :]),
                 reads=["ktokbuf"], writes=["Ktok"], sem="st_kt")

        def sink(cb, blk, pb, tile=tile):
            u = kb.rot("stg", 4)
            P.op("act", lambda e, pb=pb, u=u: e.copy(out=stg[u], in_=psum[:, pb, :]),
                 reads=[("ps", pb)], writes=[("stg", u)])
            r0 = tile * 1024 + blk * 128
            if cb < 4:
                P.op("sp", lambda e, u=u: e.dma_start(out=Vtok[r0:r0 + 128, cb * 512:(cb + 1) * 512], in_=stg[u]),
                     reads=[("stg", u)], writes=["Vtok"], sem=f"ststg{u}")
            elif cb < 8:
                P.op("sp", lambda e, u=u: e.dma_start(
                    out=Gtok[r0:r0 + 128, (cb - 4) * 512:(cb - 3) * 512], in_=stg[u]),
                    reads=[("stg", u)], writes=["Gtok"], sem=f"ststg{u}")
            else:
                c4 = (cb - 8) * 4
                dst = fsend.rearrange("(c t) ch -> t c ch", c=8)[r0:r0 + 128, c4:c4 + 4, :]
                P.op("sp", lambda e, u=u, dst=dst: e.dma_start(
                    out=dst, in_=stg[u].rearrange("p (c ch) -> p c ch", c=4)),
                    reads=[("stg", u)], writes=["fsend"], sem=f"ststg{u}")
        proj_tok(lambda kc, blk: hT[:, kc, blk * 128:(blk + 1) * 128], ("hT", 0), 8, list(range(10)), sink)
        P.barrier()

    P.op("pool", lambda e: e.collective_compute("AllGather", ALU.bypass, replica_groups=RG, ins=[fsend], outs=[fgath]),
         reads=["fsend"], writes=["fgath"], sem="cc_f", amt=1)

    kb.load_tokmajor_to_featmajor(
        ctx_d, CTX // 128, lambda c, half: xcs[:, half * 4:half * 4 + 4, c * 128:(c + 1) * 128],
        lambda c: "xc", xt)
    Gc = dict(h=(lambda kc: hT[:, kc, 0:CTX]), hk=("hT", 0),
              g=(lambda j: gbuf[:, j, 0:CTX]), gk=("gbuf", 0),
              x=(lambda dc: xcs[:, dc, :]), xk="xc", n=CTX, col0=0)
    kb.norm_mod(Gc["x"], "xc", Gc["h"], Gc["hk"], CTX, sc_ap(0, 1), sh_ap(0, 1))
    kb.ffn([Gc], w13a_d, w2a_d, w13b, w2b, gt_ap(0, 1), wk13="wbuf", wk2="wbuf")
    kb.norm_mod(Gc["x"], "xc", Gc["h"], Gc["hk"], CTX, sc_ap(1, 1), sh_ap(1, 1))
    P.barrier()

    def sinkc(cb, blk, pb):
        if cb >= 10:
            P.op("act", lambda e: e.copy(out=kctok[:, blk, (cb - 10) * 512:(cb - 9) * 512], in_=psum[:, pb, :]),
                 reads=[("ps", pb)], writes=["kvctok"])
        else:
            P.op("act", lambda e: e.copy(out=vctok[:, blk, cb * 512:(cb + 1) * 512], in_=psum[:, pb, :]),
                 reads=[("ps", pb)], writes=["kvctok"])
    proj_tok(lambda kc, blk: hT[:, kc, blk * 128:(blk + 1) * 128], ("hT", 0), 2, [0, 1, 2, 3, 10, 11], sinkc)
    P.barrier()

    kzb = [carve(A2, i * 256, [128, 256], BF16) for i in range(2)]
    sst = [carve(A2, 512 + i * 2048, [128, 2, 512], F32) for i in range(2)]
    kall = carve(A2, 4608, [128, NB, 256], BF16)
    vall = carve(A2, 4608 + 4096, [128, NB, 512], BF16)

    def accumulate_state(kview, vview, nblk, zcol, dst_rows, dh, inkey="kv_in"):
        for blk in range(nblk):
            s = kb.rot("kzb", 2)
            P.op("act", lambda e, blk=blk, s=s: e.activation(
                out=kzb[s], in_=kview(blk), func=AF.Copy, scale=zcol(blk)),
                reads=[inkey, "dtab"], writes=[("kzb", s)])
            for dc in range(2):
                P.op("pe", lambda e, blk=blk, s=s, dc=dc: e.matmul(
                    psum[:, 3 + dc, :], lhsT=kzb[s][:, dc * 128:(dc + 1) * 128], rhs=vview(blk),
                    start=(blk == 0), stop=(blk == nblk - 1)),
                    reads=[("kzb", s), inkey], writes=[("ps", 3 + dc)], inc=True)
        s = kb.rot("sst", 2)
        for dc in range(2):
            P.op("act", lambda e, s=s, dc=dc: e.copy(out=sst[s][:, dc, :], in_=psum[:, 3 + dc, :]),
                 reads=[("ps", 3 + dc)], writes=[("sst", s)])
        P.op("sp", lambda e, s=s: e.dma_start(
            out=dst_rows.rearrange("(dc p) v -> p dc v", dc=2), in_=sst[s]),
            reads=[("sst", s)], writes=["states"], sem=f"stst{s}")

    for dh in range(8):
        h = dh % 4
        accumulate_state(lambda blk, h=h: kctok[:, blk, h * 256:(h + 1) * 256],
                         lambda blk, h=h: vctok[:, blk, h * 512:(h + 1) * 512], 2,
                         lambda blk, dh=dh: zc[:, dh, blk:blk + 1], sctx[dh * 256:(dh + 1) * 256, :], dh,
                         inkey="kvctok")
    for h in range(4):
        P.op("sp", lambda e, h=h: e.dma_start(
            out=kall, in_=Ktok[:, h * 256:(h + 1) * 256].rearrange("(b p) d -> p b d", p=128)),
            reads=["Ktok"], writes=["kv_in"], sem="ld_kv")
        P.op("sp", lambda e, h=h: e.dma_start(
            out=vall, in_=Vtok[:, h * 512:(h + 1) * 512].rearrange("(b p) d -> p b d", p=128)),
            reads=["Vtok"], writes=["kv_in"], sem="ld_kv")
        for d in range(2):
            dh = d * 4 + h
            accumulate_state(lambda blk: kall[:, blk, :], lambda blk: vall[:, blk, :], NB,
                             lambda blk, dh=dh: zt[:, dh, blk:blk + 1], totsend[dh * 256:(dh + 1) * 256, :], dh)
    P.barrier()
    if os.environ.get("MK_DBG"):
        totdbg = dscr("totdbg", [8 * 256, 512], F32)
        P.op("sp", lambda e: e.dma_start(out=totdbg, in_=totsend), reads=["states"], sem="dbg")
    P.op("pool", lambda e: e.collective_compute("AllGather", ALU.bypass, replica_groups=RG,
                                                ins=[totsend], outs=[totg]),
         reads=["states"], writes=["totg"], sem="cc_t", amt=1)

    acc = kctok[:].rearrange("p a b -> p (a b)").bitcast(F32).rearrange("p (a b) -> p a b", a=2)
    vflat = vctok[:].rearrange("p a b -> p (a b)")
    term = [vflat[:, i * 2048:(i + 1) * 2048].bitcast(F32).rearrange("p (a b) -> p a b", a=2) for i in range(2)]

    def phase6_piece(dh):
        for t9 in range(9):
            s = kb.rot("term", 2)
            if t9 == 0:
                src = sctx[dh * 256:(dh + 1) * 256, :]
            else:
                r0 = ((t9 - 1) * 8 + dh) * 256
                src = totg[r0:r0 + 256, :]
            P.op("sp", lambda e, s=s, src=src: e.dma_start(
                out=term[s], in_=src.rearrange("(dc p) v -> p dc v", dc=2)),
                reads=["states", "totg"], writes=[("term", s)], sem=f"ldterm{s}")
            if t9 == 0:
                P.op("dve", lambda e, s=s, dh=dh: e.tensor_scalar(
                    out=acc, in0=term[s], scalar1=coef[:, dh, 0:1], scalar2=None, op0=ALU.mult),
                    reads=[("term", s), "coef"], writes=["acc"])
            else:
                P.op("dve", lambda e, s=s, dh=dh, t9=t9: e.scalar_tensor_tensor(
                    out=acc, in0=term[s], scalar=coef[:, dh, t9:t9 + 1], in1=acc, op0=ALU.mult, op1=ALU.add),
                    reads=[("term", s), "coef", "acc"], writes=["acc"])
        P.op("sp", lambda e, dh=dh: e.dma_start(
            out=sstart[dh * 256:(dh + 1) * 256, :].rearrange("(dc p) v -> p dc v", dc=2), in_=acc),
            reads=["acc"], writes=["sstart"], sem="st_ss")

    farr = carve(gb_raw, 0, [128, 128, 128], BF16)
    Zlo = carve(A2, 0, [128, 2, 64, 128], BF16)
    NTF, NFO = 6, 6
    tf = [carve(gb_raw, 16384 + i * 512, [128, 2, 128], F32) for i in range(NTF)]
    fo = [carve(gb_raw, 16384 + NTF * 512 + i * 512, [128, 512], BF16) for i in range(NFO)]
    fg5 = fgath.rearrange("(r c tb l2) ch -> r c tb (l2 ch)", r=8, c=8, tb=16)
    P.op("sp", lambda e: e.dma_start(
        out=floc.rearrange("(r a t) n -> r a t n", r=8, a=1),
        in_=fg5[:, bass.ds(P.cid(e), 1), :, :]),
        reads=["fgath"], writes=["floc"], sem="ld_floc")
    for q4 in range(4):
        P.op("sp", lambda e, q4=q4: e.dma_start(
            out=farr[32 * q4:32 * q4 + 32, :, :].rearrange("p a b -> p (a b)"),
            in_=floc[32 * q4:32 * q4 + 32, :]),
            reads=["floc"], writes=["farr"], sem="ld_farr")
    f2v = f2send.rearrange("(comp r ch) t -> comp r ch t", comp=2, r=8)
    for hf in range(2):
        for pi in range(32):
            if pi % 8 == 4:
                phase6_piece(hf * 4 + pi // 8)
            b = 6 + (pi % 2)
            for u in range(2):
                ch = hf * 64 + pi * 2 + u
                P.op("pe", lambda e, ch=ch, u=u, b=b: e.matmul(
                    psum[:, b, u * 256:(u + 1) * 256], lhsT=farr[:, :, ch], rhs=cs[:], start=True, stop=True),
                    reads=["farr", "cs"], writes=[("ps", b)], inc=(u == 1))
            Y = psum[:, b, :].rearrange("p (u c l) -> p u c l", u=2, c=2)
            Yc, Ys = Y[:, :, 0, :], Y[:, :, 1, :]
            Tc = tw[:, 0, :].unsqueeze(1).broadcast_to([128, 2, 128])
            Ts = tw[:, 1, :].unsqueeze(1).broadcast_to([128, 2, 128])
            t = [kb.rot("tf", NTF) for _ in range(4)]
            for (ti, a, bb) in ((t[0], Yc, Tc), (t[1], Ys, Ts), (t[2], Yc, Ts), (t[3], Ys, Tc)):
                P.op("dve", lambda e, ti=ti, a=a, bb=bb: e.tensor_tensor(out=tf[ti], in0=a, in1=bb, op=ALU.mult),
                     reads=[("ps", b), "tw"], writes=[("tf", ti)])
            P.op("pool", lambda e, t=t, pi=pi: e.tensor_tensor(
                out=Zlo[:, 0, pi * 2:pi * 2 + 2, :], in0=tf[t[0]], in1=tf[t[1]], op=ALU.subtract),
                reads=[("tf", t[0]), ("tf", t[1])], writes=["Z"])
            P.op("pool", lambda e, t=t, pi=pi: e.tensor_tensor(
                out=Zlo[:, 1, pi * 2:pi * 2 + 2, :], in0=tf[t[2]], in1=tf[t[3]], op=ALU.add),
                reads=[("tf", t[2]), ("tf", t[3])], writes=["Z"])
        for gi in range(16):
            zcv = Zlo[:, 0, gi * 4:(gi + 1) * 4, :].rearrange("p a b -> p (a b)")
            zsv = Zlo[:, 1, gi * 4:(gi + 1) * 4, :].rearrange("p a b -> p (a b)")
            for comp in range(2):
                b = 6 + comp
                pairs = ((dft[:, 0, :], zcv), (dft[:, 2, :], zsv)) if comp == 0 else \
                        ((dft[:, 1, :], zcv), (dft[:, 0, :], zsv))
                for ii, (w, z) in enumerate(pairs):
                    P.op("pe", lambda e, w=w, z=z, b=b, ii=ii: e.matmul(
                        psum[:, b, :], lhsT=w, rhs=z, start=(ii == 0), stop=(ii == 1)),
                        reads=["Z", "dft"], writes=[("ps", b)], inc=(ii == 1))
                s = kb.rot("fo", NFO)
                P.op("act", lambda e, s=s, b=b: e.copy(out=fo[s], in_=psum[:, b, :]),
                     reads=[("ps", b)], writes=[("fo", s)])
                ch0 = hf * 64 + gi * 4
                for r in range(8):
                    dst = f2v[comp, r, ch0:ch0 + 4, :].rearrange("ch (tb l) -> tb ch l", tb=16)
                    P.op("act", lambda e, s=s, r=r, dst=dst: e.dma_start(
                        out=dst, in_=fo[s][16 * r:16 * r + 16, :].rearrange("p (ch l) -> p ch l", ch=4)),
                        reads=[("fo", s)], writes=["f2send"], sem=f"stF{s}")
    P.barrier()
    P.op("pool", lambda e: e.collective_compute("AllGather", ALU.bypass, replica_groups=RG,
                                                ins=[f2send], outs=[f2g]),
         reads=["f2send"], writes=["f2g"], sem="cc_F", amt=1)

    if os.environ.get("MK_DBG"):
        ssdbg = dscr("ssdbg", [8 * 256, 512], F32)
        P.op("sp", lambda e: e.dma_start(out=ssdbg, in_=sstart), reads=["sstart"], sem="dbg")
        P.barrier(exclude=("cc_F",))
    Sst = carve(A2, 0, [128, 2, 512], F32)
    snapb = [carve(A2, 2048 + i * 1024, [128, 2, 512], BF16) for i in range(2)]
    kzb2 = [carve(A2, 4096 + i * 256, [128, 256], BF16) for i in range(2)]
    snv = snaps.rearrange("(h b p) n -> h b p n", h=4, b=NB)
    for h in range(4):
        dh = 4 + h
        P.op("sp", lambda e, h=h: e.dma_start(
            out=kall, in_=Ktok[:, h * 256:(h + 1) * 256].rearrange("(b p) d -> p b d", p=128)),
            reads=["Ktok"], writes=["kv_in"], sem="ld_kv")
        P.op("sp", lambda e, h=h: e.dma_start(
            out=vall, in_=Vtok[:, h * 512:(h + 1) * 512].rearrange("(b p) d -> p b d", p=128)),
            reads=["Vtok"], writes=["kv_in"], sem="ld_kv")
        P.op("sp", lambda e, dh=dh: e.dma_start(
            out=Sst, in_=sstart[dh * 256:(dh + 1) * 256, :].rearrange("(dc p) v -> p dc v", dc=2)),
            reads=["sstart"], writes=["Sst"], sem="ld_ss")
        for blk in range(NB - 1, -1, -1):
            s = kb.rot("snapb", 2)
            P.op("act", lambda e, s=s: e.copy(out=snapb[s], in_=Sst), reads=["Sst"], writes=[("snapb", s)])
            P.op("sp", lambda e, s=s, h=h, blk=blk: e.dma_start(
                out=snv[h, blk], in_=snapb[s].rearrange("p a b -> p (a b)")),
                reads=[("snapb", s)], writes=["snaps"], sem=f"stsn{s}")
            if blk == 0:
                break
            s2 = kb.rot("kzb2", 2)
            P.op("act", lambda e, s2=s2, blk=blk, dh=dh: e.activation(
                out=kzb2[s2], in_=kall[:, blk, :], func=AF.Copy, scale=zeta[:, dh:dh + 1]),
                reads=["kv_in", "dtab"], writes=[("kzb2", s2)])
            for dc in range(2):
                P.op("pe", lambda e, s2=s2, blk=blk, dc=dc: e.matmul(
                    psum[:, 3 + dc, :], lhsT=kzb2[s2][:, dc * 128:(dc + 1) * 128], rhs=vall[:, blk, :],
                    start=True, stop=True),
                    reads=[("kzb2", s2), "kv_in"], writes=[("ps", 3 + dc)], inc=True)
                P.op("dve", lambda e, dc=dc, dh=dh: e.scalar_tensor_tensor(
                    out=Sst[:, dc, :], in0=Sst[:, dc, :], scalar=gch[:, dh:dh + 1], in1=psum[:, 3 + dc, :],
                    op0=ALU.mult, op1=ALU.add),
                    reads=[("ps", 3 + dc), "Sst", "dtab"], writes=["Sst"])
    P.barrier(exclude=("cc_F",))

    ogT = carve(gb_raw, 0, [128, 16, 1024], BF16)
    qT = carve(A2, 0, [128, 2, 1024], BF16)
    kT = carve(A2, 2048, [128, 2, 1024], BF16)
    ktk = carve(A2, 4096, [128, 8, 256], BF16)
    vtk = carve(A2, 6144, [128, 8, 512], BF16)
    gtk = [carve(A2, 10240 + i * 512, [128, 512], BF16) for i in range(2)]
    snp = [carve(A2, 11264 + i * 1024, [128, 2, 512], BF16) for i in range(2)]
    Sf = carve(A2, 13312, [128, 2, 512], F32)
    Sfb = carve(A2, 15360, [128, 2, 512], BF16)
    SMb = [carve(A2, 16384 + i * 128, [128, 128], BF16) for i in range(2)]
    qfb = [carve(A2, 16640 + i * 512, [128, 2, 2, 128], BF16) for i in range(2)]
    kzf = [carve(A2, 17664 + i * 256, [128, 256], BF16) for i in range(2)]
    ogh = [carve(A2, 18176 + i * 512, [128, 512], BF16) for i in range(2)]
    mixT = carve(A2, 0, [128, 8, 512], BF16)
    mT = carve(A2, 4096, [128, 8, 512], BF16)
    mbuf = carve(A2, 8192, [128, 8, 512], F32)
    ldb = [carve(A2, 16384 + i * 512, [128, 512], BF16) for i in range(4)]
    xn = mbuf
    otile = [carve(A2, 0 + i * 2048, [128, 1024], F32) for i in range(2)]

    def wstream(src_blk, nk, s):
        dst = wbuf[s][:, 0:nk * 128].rearrange("p (a b) -> p a b", b=128)
        P.op("pool", lambda e: e.dma_start(out=dst, in_=src_blk), writes=[("wbuf", s)], sem=f"wbuf{s}")
        return dst

    f2gv = f2g.rearrange("(k comp r ch) t -> k comp r ch t", k=8, comp=2, r=8)

    def retention_tile(tile):
        t0 = tile * 1024
        for h in range(4):
            P.op("sp", lambda e, h=h: e.dma_start(
                out=qT, in_=Uq[2 * h:2 * h + 2, :, t0:t0 + 1024].rearrange("c p t -> p c t")),
                reads=["Uq"], writes=["qT"], sem="ld_q")
            P.op("sp", lambda e, h=h: e.dma_start(
                out=kT, in_=Uq[8 + 2 * h:10 + 2 * h, :, t0:t0 + 1024].rearrange("c p t -> p c t")),
                reads=["Uq"], writes=["qT"], sem="ld_q")
            P.op("sp", lambda e, h=h: e.dma_start(
                out=ktk, in_=Ktok[t0:t0 + 1024, h * 256:(h + 1) * 256].rearrange("(b p) d -> p b d", p=128)),
                reads=["Ktok"], writes=["ktk"], sem="ld_kt")
            P.op("sp", lambda e, h=h: e.dma_start(
                out=vtk, in_=Vtok[t0:t0 + 1024, h * 512:(h + 1) * 512].rearrange("(b p) d -> p b d", p=128)),
                reads=["Vtok"], writes=["ktk"], sem="ld_kt")
            P.op("sp", lambda e, h=h: e.dma_start(
                out=Sf, in_=sstart[h * 256:(h + 1) * 256, :].rearrange("(dc p) v -> p dc v", dc=2)),
                reads=["sstart"], writes=["Sf"], sem="ld_sf")
            P.op("act", lambda e: e.copy(out=Sfb, in_=Sf), reads=["Sf"], writes=["Sfb"])
            st = {}

            def stageA(b8):
                blk = tile * 8 + b8
                c0 = b8 * 128
                sg = kb.rot("gtk", 2)
                P.op("sp", lambda e, sg=sg, blk=blk, h=h: e.dma_start(
                    out=gtk[sg], in_=Gtok[blk * 128:(blk + 1) * 128, h * 512:(h + 1) * 512]),
                    reads=["Gtok"], writes=[("gtk", sg)], sem=f"ld_g{sg}")
                sn = kb.rot("snp", 2)
                P.op("sp", lambda e, sn=sn, blk=blk, h=h: e.dma_start(
                    out=snp[sn].rearrange("p a b -> p (a b)"), in_=snv[h, blk]),
                    reads=["snaps"], writes=[("snp", sn)], sem=f"ld_sn{sn}")
                for dc in range(2):
                    P.op("pe", lambda e, dc=dc, c0=c0: e.matmul(
                        psum[:, 0, 0:128], lhsT=kT[:, dc, c0:c0 + 128], rhs=qT[:, dc, c0:c0 + 128],
                        start=(dc == 0), stop=(dc == 1)),
                        reads=["qT"], writes=[("ps", 0)], inc=(dc == 1))
                sm = kb.rot("SMb", 2)
                P.op("dve", lambda e, sm=sm, h=h: e.tensor_tensor(
                    out=SMb[sm], in0=psum[:, 0, 0:128], in1=mt[:, h, :], op=ALU.mult),
                    reads=[("ps", 0), "dtab"], writes=[("SMb", sm)])
                sq_ = kb.rot("qfb", 2)
                for d in range(2):
                    P.op("pool", lambda e, sq_=sq_, d=d, c0=c0, h=h: e.tensor_tensor(
                        out=qfb[sq_][:, d, :, :], in0=qT[:, :, c0:c0 + 128],
                        in1=xi[:, d * 4 + h, :].unsqueeze(1).broadcast_to([128, 2, 128]), op=ALU.mult),
                        reads=["qT", "dtab"], writes=[("qfb", sq_)])
                sk = kb.rot("kzf", 2)
                P.op("act", lambda e, sk=sk, b8=b8, h=h: e.activation(
                    out=kzf[sk], in_=ktk[:, b8, :], func=AF.Copy, scale=zeta[:, h:h + 1]),
                    reads=["ktk", "dtab"], writes=[("kzf", sk)])
                st[b8] = dict(sg=sg, sn=sn, sm=sm, sq=sq_, sk=sk)

            def stageB(b8):
                d_ = st[b8]
                sm, sq_, sn, sk = d_["sm"], d_["sq"], d_["sn"], d_["sk"]
                pb = 1 + (b8 % 2)
                P.op("pe", lambda e, sm=sm, b8=b8, pb=pb: e.matmul(
                    psum[:, pb, :], lhsT=SMb[sm], rhs=vtk[:, b8, :], start=True, stop=False),
                    reads=[("SMb", sm), "ktk"], writes=[("ps", pb)], inc=False)
                for dc in range(2):
                    P.op("pe", lambda e, sq_=sq_, dc=dc, pb=pb, sn=sn: e.matmul(
                        psum[:, pb, :], lhsT=qfb[sq_][:, 1, dc, :], rhs=snp[sn][:, dc, :],
                        start=False, stop=False),
                        reads=[("qfb", sq_), ("snp", sn)], writes=[("ps", pb)], inc=False)
                for dc in range(2):
                    P.op("pe", lambda e, sq_=sq_, dc=dc, pb=pb: e.matmul(
                        psum[:, pb, :], lhsT=qfb[sq_][:, 0, dc, :], rhs=Sfb[:, dc, :], start=False, stop=(dc == 1)),
                        reads=[("qfb", sq_), "Sfb"], writes=[("ps", pb)], inc=(dc == 1))
                for dc in range(2):
                    P.op("pe", lambda e, sk=sk, b8=b8, dc=dc: e.matmul(
                        psum[:, 3 + dc, :], lhsT=kzf[sk][:, dc * 128:(dc + 1) * 128], rhs=vtk[:, b8, :],
                        start=True, stop=True),
                        reads=[("kzf", sk), "ktk"], writes=[("ps", 3 + dc)], inc=True)
                    P.op("dve", lambda e, dc=dc, h=h: e.scalar_tensor_tensor(
                        out=Sf[:, dc, :], in0=Sf[:, dc, :], scalar=gch[:, h:h + 1], in1=psum[:, 3 + dc, :],
                        op0=ALU.mult, op1=ALU.add),
                        reads=[("ps", 3 + dc), "Sf", "dtab"], writes=["Sf"])
                P.op("act", lambda e: e.copy(out=Sfb, in_=Sf), reads=["Sf"], writes=["Sfb"])

            def stageC(b8):
                d_ = st[b8]
                sg = d_["sg"]
                c0 = b8 * 128
                pb = 1 + (b8 % 2)
                sl = kb.rot("small", 2) * 16
                sm_ = small[:, sl:sl + 16]
                key = ("small", sl)
                P.op("dve", lambda e, pb=pb: e.bn_stats(out=sm_[:, 0:6], in_=psum[:, pb, :]),
                     reads=[("ps", pb)], writes=[key])
                P.op("dve", lambda e: e.bn_aggr(out=sm_[:, 6:8], in_=sm_[:, 0:6]), reads=[key], writes=[key])
                P.op("act", lambda e: e.activation(out=sm_[:, 8:9], in_=sm_[:, 7:8], func=AF.Sqrt,
                                                   bias=kb.epsc[:, 0:1]),
                     reads=[key, "epsc"], writes=[key])
                P.op("dve", lambda e: e.reciprocal(out=sm_[:, 8:9], in_=sm_[:, 8:9]), reads=[key], writes=[key])
                P.op("dve", lambda e: e.scalar_tensor_tensor(
                    out=sm_[:, 9:10], in0=sm_[:, 6:7], scalar=-1.0, in1=sm_[:, 8:9],
                    op0=ALU.mult, op1=ALU.mult), reads=[key], writes=[key])
                t1, t2 = kb.rot("tmpf", 4), kb.rot("tmpf", 4)
                P.op("act", lambda e, pb=pb, t1=t1: e.activation(
                    out=tmpf[t1][:], in_=psum[:, pb, :], func=AF.Identity, scale=sm_[:, 8:9], bias=sm_[:, 9:10]),
                    reads=[("ps", pb), key], writes=[("tmpf", t1)])
                P.op("act", lambda e, sg=sg, t2=t2: e.activation(out=tmpf[t2][:], in_=gtk[sg], func=AF.Silu),
                     reads=[("gtk", sg)], writes=[("tmpf", t2)])
                so = kb.rot("ogh", 2)
                P.op("dve", lambda e, t1=t1, t2=t2, so=so: e.tensor_tensor(
                    out=ogh[so], in0=tmpf[t1][:], in1=tmpf[t2][:], op=ALU.mult),
                    reads=[("tmpf", t1), ("tmpf", t2)], writes=[("ogh", so)])
                pv = psum[:, 5, 0:256].bitcast(BF16)
                for q in range(4):
                    P.op("pe", lambda e, so=so, q=q, pv=pv: e.transpose(
                        out=pv[:, q * 128:(q + 1) * 128], in_=ogh[so][:, q * 128:(q + 1) * 128], identity=identb[:]),
                        reads=[("ogh", so), "identb"], writes=[("ps", 5)], inc=(q == 3))
                for q in range(4):
                    f_ = h * 4 + q
                    P.op("act", lambda e, q=q, f_=f_, c0=c0, pv=pv: e.activation(
                        out=ogT[:, f_, c0:c0 + 128], in_=pv[:, q * 128:(q + 1) * 128], func=AF.Copy,
                        scale=gnw[:, f_:f_ + 1]),
                        reads=[("ps", 5), "gnw"], writes=["ogT"])

            for b8 in range(8):
                stageA(b8)
                stageB(b8)
                if b8 > 0:
                    stageC(b8 - 1)
            stageC(7)
            P.op("sp", lambda e, h=h: e.dma_start(
                out=sstart[h * 256:(h + 1) * 256, :].rearrange("(dc p) v -> p dc v", dc=2), in_=Sf),
                reads=["Sf"], writes=["sstart"], sem="st_sf")

    def mixer(tile, loc):
        tg = tile * 2 + loc
        c0, c1 = tg * TG, (tg + 1) * TG
        l0, l1 = loc * TG, (loc + 1) * TG
        for dc in range(KC):
            s = kb.rot("wbuf", 2)
            wv = wstream(pret_d[dc], 16, s)
            pb = kb.next_bank()
            for kc in range(16):
                P.op("pe", lambda e, wv=wv, kc=kc, pb=pb: e.matmul(
                    psum[:, pb, :], lhsT=wv[:, kc, :], rhs=ogT[:, kc, l0:l1], start=(kc == 0), stop=(kc == 15)),
                    reads=[("wbuf", s), "ogT"], writes=[("ps", pb)], inc=(kc == 15))
            gb = kb.rot("ldb", 4)
            P.op("sp", lambda e, dc=dc, gb=gb: e.dma_start(out=ldb[gb], in_=Ug[dc, :, c0:c1]),
                 reads=["Uq"], writes=[("ldb", gb)], sem=f"ldb{gb}")
            t1 = kb.rot("tmpf", 4)
            P.op("act", lambda e, gb=gb, t1=t1: e.activation(out=tmpf[t1][:], in_=ldb[gb], func=AF.Sigmoid),
                 reads=[("ldb", gb)], writes=[("tmpf", t1)])
            P.op("dve", lambda e, dc=dc, pb=pb, t1=t1: e.tensor_tensor(
                out=mbuf[:, dc, :], in0=psum[:, pb, :], in1=tmpf[t1][:], op=ALU.mult),
                reads=[("ps", pb), ("tmpf", t1)], writes=[("mbuf", dc)])
        for co in range(KC):
            gI, cc = co // 2, co % 2
            pb = kb.next_bank()
            k = 0
            for kc2 in range(2):
                for comp in range(2):
                    gb = kb.rot("ldb", 4)
                    kk = gI * 2 + kc2
                    P.op("sp", lambda e, gb=gb, kk=kk, comp=comp: e.dma_start(
                        out=ldb[gb],
                        in_=f2loc[(kk * 2 + comp) * 128:(kk * 2 + comp + 1) * 128, c0:c1]),
                        reads=["f2loc"], writes=[("ldb", gb)], sem=f"ldb{gb}")
                    P.op("pe", lambda e, kc2=kc2, comp=comp, gb=gb, pb=pb, k=k, cc=cc: e.matmul(
                        psum[:, pb, :], lhsT=cdsd[:, kc2, comp, cc * 128:(cc + 1) * 128],
                        rhs=ldb[gb], start=(k == 0), stop=(k == 3)),
                        reads=["cdsd", ("ldb", gb)], writes=[("ps", pb)], inc=(k == 3))
                    k += 1
            P.op("act", lambda e, co=co, pb=pb: e.copy(out=mixT[:, co, :], in_=psum[:, pb, :]),
                 reads=[("ps", pb)], writes=["mixT"])
        for dc in range(KC):
            s = kb.rot("wbuf", 2)
            wv = wstream(pfour_d[dc], KC, s)
            pb = kb.next_bank()
            for kc in range(KC):
                P.op("pe", lambda e, wv=wv, kc=kc, pb=pb: e.matmul(
                    psum[:, pb, :], lhsT=wv[:, kc, :], rhs=mixT[:, kc, :], start=(kc == 0), stop=(kc == KC - 1)),
                    reads=[("wbuf", s), "mixT"], writes=[("ps", pb)], inc=(kc == KC - 1))
            gb = kb.rot("ldb", 4)
            P.op("sp", lambda e, dc=dc, gb=gb: e.dma_start(out=ldb[gb], in_=Ug[8 + dc, :, c0:c1]),
                 reads=["Uq"], writes=[("ldb", gb)], sem=f"ldb{gb}")
            t1, t2 = kb.rot("tmpf", 4), kb.rot("tmpf", 4)
            P.op("act", lambda e, gb=gb, t1=t1: e.activation(out=tmpf[t1][:], in_=ldb[gb], func=AF.Sigmoid),
                 reads=[("ldb", gb)], writes=[("tmpf", t1)])
            P.op("dve", lambda e, pb=pb, t1=t1, t2=t2: e.tensor_tensor(
                out=tmpf[t2][:], in0=psum[:, pb, :], in1=tmpf[t1][:], op=ALU.mult),
                reads=[("ps", pb), ("tmpf", t1)], writes=[("tmpf", t2)])
            P.op("dve", lambda e, dc=dc, t2=t2: e.tensor_tensor(
                out=mT[:, dc, :], in0=mbuf[:, dc, :], in1=tmpf[t2][:], op=ALU.add),
                reads=[("mbuf", dc), ("tmpf", t2)], writes=["mT"])
        for dc in range(KC):
            s = kb.rot("wbuf", 2)
            wv = wstream(wout_d[dc], KC, s)
            pb = kb.next_bank()
            for kc in range(KC):
                P.op("pe", lambda e, wv=wv, kc=kc, pb=pb: e.matmul(
                    psum[:, pb, :], lhsT=wv[:, kc, :], rhs=mT[:, kc, :], start=(kc == 0), stop=(kc == KC - 1)),
                    reads=[("wbuf", s), "mT"], writes=[("ps", pb)], inc=(kc == KC - 1))
            P.op("dve", lambda e, dc=dc, pb=pb: e.scalar_tensor_tensor(
                out=xres[:, dc, c0:c1], in0=psum[:, pb, :], scalar=gate[:, 1, dc, 0:1],
                in1=xres[:, dc, c0:c1], op0=ALU.mult, op1=ALU.add),
                reads=[("ps", pb), "modv", ("xres", tg)], writes=[("xres", tg)])

    for tile in range(2):
        retention_tile(tile)
        if tile == 0:
            P.op("sp", lambda e: e.dma_start(
                out=f2loc.rearrange("(k comp a ch) t -> k comp a ch t", k=8, comp=2, a=1),
                in_=f2gv[:, :, bass.ds(P.cid(e), 1), :, :]),
                reads=["f2g"], writes=["f2loc"], sem="ld_f2loc")
        P.barrier()
        for loc in range(2):
            mixer(tile, loc)
        P.barrier()
        groups = []
        for loc in range(2):
            tg = tile * 2 + loc
            groups.append(dict(
                h=(lambda kc, loc=loc: hT[:, kc, loc * TG:(loc + 1) * TG]), hk=("hT", loc),
                g=(lambda j, loc=loc: gbuf[:, j, loc * TG:(loc + 1) * TG]), gk=("gbuf", loc),
                x=(lambda dc, tg=tg: xres[:, dc, tg * TG:(tg + 1) * TG]), xk=("xres", tg),
                n=TG, tg=tg))
        for G in groups:
            kb.norm_mod(G["x"], G["xk"], G["h"], G["hk"], TG,
                        lambda kc: eff[:, 2, kc, 0:1], lambda kc: mod[:, 6 * 8 + kc, 0:1])
        kb.ffn(groups, w13c_d, w2c_d, w13b, w2b, lambda dc: gate[:, 2, dc, 0:1], wk13="wbuf", wk2="wbuf")
        P.barrier()
        for G in groups:
            tg = G["tg"]
            slot = kb.rot("rstd", 2)
            kb.rms_stats(G["x"], G["xk"], TG, slot)
            for kc in range(KC):
                t = kb.rot("tmpf", 4)
                P.op("dve", lambda e, kc=kc, t=t, G=G, slot=slot: e.tensor_tensor(
                    out=tmpf[t][:], in0=G["x"](kc), in1=kb.rstd[slot][:], op=ALU.mult),
                    reads=[G["xk"], ("rstd", slot)], writes=[("tmpf", t)])
                P.op("act", lambda e, kc=kc, t=t: e.activation(
                    out=xn[:, kc, :], in_=tmpf[t][:], func=AF.Copy, scale=nrm[:, 3, kc:kc + 1]),
                    reads=[("tmpf", t), "nrm"], writes=["xn"])
            for cc in range(4):
                c = tg * 4 + cc
                s = c % 2
                for half in range(2):
                    b = kb.next_bank()
                    for q in range(4):
                        dc = half * 4 + q
                        P.op("pe", lambda e, cc=cc, dc=dc, b=b, q=q: e.transpose(
                            out=psum[:, b, q * 128:(q + 1) * 128], in_=xn[:, dc, cc * 128:(cc + 1) * 128],
                            identity=kb.ident[:]),
                            reads=["xn", "ident"], writes=[("ps", b)], inc=(q == 3))
                    if half == 0:
                        P.op("act", lambda e, s=s, b=b: e.copy(out=otile[s][:, 0:512], in_=psum[:, b, :]),
                             reads=[("ps", b)], writes=[("otile", s)])
                    else:
                        P.op("dve", lambda e, s=s, b=b: e.tensor_copy(out=otile[s][:, 512:1024], in_=psum[:, b, :]),
                             reads=[("ps", b)], writes=[("otile", s)])
                P.op("sp", lambda e, c=c, s=s: e.dma_start(out=out_d[c * 128:(c + 1) * 128, :], in_=otile[s]),
                     reads=[("otile", s)], sem=f"st{s}")
        P.barrier()

    print("sbuf bytes remaining", nc.sbuf_bytes_remaining, "ops", P.n_ops)
    P.emit()
    return nc

RET_H = 4
DK = 256
DV = 512
_NC = {}


def _get(name, fn):
    if name not in _NC:
        _NC[name] = fn()
    return _NC[name]


def _run(nc, in_maps):
    if os.environ.get("MK_TRACE"):
        res = run_bass_kernel_spmd(nc, in_maps, core_ids=list(range(NCORES)), trace=True)
        print("EXEC_TIME_NS", res.exec_time_ns)
    else:
        res = run_bass_kernel_spmd(nc, in_maps, core_ids=list(range(NCORES)))
    return res.results


def rope_tables():
    n_freq = DK // 4
    inv = (10000.0 ** (-np.arange(n_freq, dtype=np.float32) / n_freq)).astype(np.float32)
    tok = np.arange(L)
    row = (tok // 64).astype(np.float32)
    col = (tok % 64).astype(np.float32)
    ang = np.concatenate([row[:, None] * inv, col[:, None] * inv], axis=-1).astype(np.float32)
    cos = np.cos(ang).astype(np.float32).T
    sin = np.sin(ang).astype(np.float32).T
    t = np.stack([cos / 16.0, sin / 16.0, cos, sin], axis=1)
    t = t.reshape(128, 4, NCORES, 2, NT // 2).transpose(2, 3, 0, 1, 4)
    return np.ascontiguousarray(t.astype(np.float32))


def perm_win(w_in):
    idx = np.arange(9216)
    for base in (0, 1024):
        for h in range(RET_H):
            o = base + h * DK
            idx[o:o + DK] = np.concatenate([np.arange(o, o + DK, 2), np.arange(o + 1, o + DK, 2)])
    return w_in[:, idx]


def launch1(x, c, ctx, c_ctx, w_ada, b_ada, norm_ffn1, w13_ffn1, w2_ffn1, norm_mix, w_in,
            norm_ffn2, norm_final):
    nc = _get("l1", build_l1)
    cvec = np.ascontiguousarray(np.stack([lay_vec(c[0]), lay_vec(c_ctx)], axis=2))
    rt = rope_tables()
    shared = {
        "cvec": cvec,
        "w_ada": lay_kmajor(w_ada[0], 128),
        "b_ada": lay_vec(b_ada[0]),
        "norms": np.ascontiguousarray(np.stack(
            [lay_vec(norm_ffn1[0]), lay_vec(norm_mix[0]), lay_vec(norm_ffn2[0]), lay_vec(norm_final)], axis=1)),
        "w13_1": lay_w13(w13_ffn1[0]),
        "w2_1": lay_kmajor(w2_ffn1[0], 128),
        "w_in": lay_kmajor(perm_win(w_in[0]), 128),
        "ident": np.eye(128, dtype=np.float32),
        "ctx": np.ascontiguousarray(ctx[0]),
    }
    in_maps = []
    for k in range(NCORES):
        m = dict(shared)
        m["x"] = np.ascontiguousarray(x[0, k * NT:(k + 1) * NT])
        m["rope"] = rt[k]
        in_maps.append(m)
    return _run(nc, in_maps)


_DBG = {}


def dft_tables():
    n = np.arange(128, dtype=np.float64)
    ang = 2.0 * np.pi * np.outer(n, n) / 128.0
    C, S = np.cos(ang), np.sin(ang)
    dft = np.stack([C, S, -S], axis=1)
    cs = np.concatenate([C, S], axis=1)
    tau = 2.0 * np.pi * np.outer(n, n) / float(L)
    tw = np.stack([np.cos(tau), np.sin(tau)], axis=1)
    m = np.arange(256, dtype=np.float64)
    phi = 2.0 * np.pi * np.outer(m, m) / 256.0
    nrmz = 1.0 / np.sqrt(float(L) * 256.0)
    CD, SD = np.cos(phi) * nrmz, -np.sin(phi) * nrmz
    cdsd = np.stack([CD, SD], axis=1).reshape(2, 128, 2, 256).transpose(1, 0, 2, 3)
    import ml_dtypes
    bf = ml_dtypes.bfloat16
    return (dft.astype(np.float32).astype(bf), cs.astype(np.float32).astype(bf), tw.astype(np.float32),
            np.ascontiguousarray(cdsd).astype(np.float32).astype(bf))


def ret_tables():
    j = np.arange(128)[:, None]
    i = np.arange(128)[None, :]
    d1 = np.maximum(i - j, 0).astype(np.float32)
    u1 = (i >= j).astype(np.float32)
    pos = np.broadcast_to((np.arange(128) + 1).astype(np.float32)[None, :], (128, 128))
    tabs = np.ascontiguousarray(np.stack([d1, u1, pos], axis=1))
    pcol = np.stack([127.0 - np.arange(128), np.full(128, 128.0)], axis=1).astype(np.float32)
    return tabs, pcol


def launch2(uT, kvc, decay_fwd, decay_bwd):
    nc = _get("l2", build_l2)
    dft, cs, tw, _ = dft_tables()
    tabs, pcol = ret_tables()
    bf = uT.dtype
    in_maps = []
    for core in range(NCORES):
        d, h = core // 4, core % 4
        kT = np.concatenate([kvc[2 * h:2 * h + 2], uT[8 + 2 * h:8 + 2 * h + 2]], axis=2)
        qT = np.concatenate([np.zeros((2, 128, CTX), dtype=bf), uT[2 * h:2 * h + 2]], axis=2)
        vT = np.concatenate([kvc[8 + 4 * h:8 + 4 * h + 4], uT[16 + 4 * h:16 + 4 * h + 4]], axis=2)
        if d == 1:
            kT = np.concatenate([kT[:, :, :CTX][:, :, ::-1], kT[:, :, CTX:][:, :, ::-1]], axis=2)
            qT = np.concatenate([qT[:, :, :CTX][:, :, ::-1], qT[:, :, CTX:][:, :, ::-1]], axis=2)
            vT = np.concatenate([vT[:, :, :CTX][:, :, ::-1], vT[:, :, CTX:][:, :, ::-1]], axis=2)
        S_ = NCHK * 128
        q4 = qT.reshape(2, 128, NCHK, 128).transpose(2, 1, 0, 3)
        k4 = kT.reshape(2, 128, NCHK, 128).transpose(2, 1, 0, 3)
        qk = np.ascontiguousarray(np.concatenate([q4, k4], axis=2).reshape(NCHK, 128, 512))
        ktok = kT.transpose(2, 0, 1).reshape(S_, 256)
        vtok = vT.transpose(2, 0, 1).reshape(S_, 512)
        kv = np.ascontiguousarray(np.concatenate([ktok, vtok], axis=1).reshape(NCHK, 128, 768))
        dec = (decay_fwd if d == 0 else decay_bwd)[0, h]
        farr = np.ascontiguousarray(uT[48 + core].T.reshape(128, 128, 128))
        in_maps.append({
            "qk": qk, "kv": kv, "dec": np.full((128, 1), dec, dtype=np.float32),
            "tabs": tabs, "pcol": pcol, "farr": farr, "dft": dft, "cs": cs, "tw": tw,
        })
    return _run(nc, in_maps)


def launch3(x1T, uT, res2, modv, norms, ret_gn_w, p_ret, p_four, w_out, w13_ffn2, w2_ffn2):
    nc = _get("l3", build_l3)
    _, _, _, cdsd = dft_tables()
    of = np.concatenate([res2[h]["o"].reshape(L, DV) for h in range(4)], axis=1)
    ob = np.concatenate([res2[4 + h]["o"].reshape(L, DV)[::-1] for h in range(4)], axis=1)
    Fc = np.stack([np.asarray(res2[k]["F"])[0].reshape(128, 128, 128).transpose(1, 0, 2).reshape(128, L)
                   for k in range(NCORES)], axis=0)
    Fs = np.stack([np.asarray(res2[k]["F"])[1].reshape(128, 128, 128).transpose(1, 0, 2).reshape(128, L)
                   for k in range(NCORES)], axis=0)
    shared = {
        "modv": modv, "norms": norms, "gnw": lay_vec(ret_gn_w[0]), "cdsd": cdsd,
        "p_ret": lay_kmajor(p_ret[0], 128), "p_four": lay_kmajor(p_four[0], 128),
        "w_out": lay_kmajor(w_out[0], 128), "w13_2": lay_w13(w13_ffn2[0]), "w2_2": lay_kmajor(w2_ffn2[0], 128),
        "ident": np.eye(128, dtype=np.float32),
    }
    in_maps = []
    for k in range(NCORES):
        sl = slice(k * NT, (k + 1) * NT)
        m = dict(shared)
        m["x1T"] = np.ascontiguousarray(x1T[k])
        m["ofT"] = np.ascontiguousarray(of[sl].T.reshape(16, 128, NT).transpose(1, 0, 2))
        m["obT"] = np.ascontiguousarray(ob[sl].T.reshape(16, 128, NT).transpose(1, 0, 2))
        m["gT"] = np.ascontiguousarray(uT[32:48, :, sl].transpose(1, 0, 2))
        m["gaT"] = np.ascontiguousarray(uT[56:64, :, sl].transpose(1, 0, 2))
        m["gfT"] = np.ascontiguousarray(uT[64:72, :, sl].transpose(1, 0, 2))
        m["FcT"] = np.ascontiguousarray(Fc[:, :, sl].transpose(1, 0, 2))
        m["FsT"] = np.ascontiguousarray(Fs[:, :, sl].transpose(1, 0, 2))
        in_maps.append(m)
    return _run(nc, in_maps)


def kernel_unfused(x, c, ctx, c_ctx, w_ada, b_ada, norm_ffn1, w13_ffn1, w2_ffn1, norm_mix, w_in,
           decay_fwd, decay_bwd, ret_gn_w, p_ret, p_four, w_out, norm_ffn2, w13_ffn2, w2_ffn2,
           norm_final):
    f = lambda a: np.asarray(a, dtype=np.float32)
    (x, c, ctx, c_ctx, w_ada, b_ada, norm_ffn1, w13_ffn1, w2_ffn1, norm_mix, w_in, decay_fwd, decay_bwd,
     ret_gn_w, p_ret, p_four, w_out, norm_ffn2, w13_ffn2, w2_ffn2, norm_final) = [f(a) for a in (
        x, c, ctx, c_ctx, w_ada, b_ada, norm_ffn1, w13_ffn1, w2_ffn1, norm_mix, w_in, decay_fwd, decay_bwd,
        ret_gn_w, p_ret, p_four, w_out, norm_ffn2, w13_ffn2, w2_ffn2, norm_final)]
    r1 = launch1(x, c, ctx, c_ctx, w_ada, b_ada, norm_ffn1, w13_ffn1, w2_ffn1, norm_mix, w_in,
                 norm_ffn2, norm_final)
    uT = np.concatenate([np.asarray(r["uT"]) for r in r1], axis=2)
    kvc = np.asarray(r1[0]["kvc"])
    x1T = [np.asarray(r["x1T"]) for r in r1]
    modv = np.asarray(r1[0]["modv"])
    norms = np.ascontiguousarray(np.stack(
        [lay_vec(norm_ffn1[0]), lay_vec(norm_mix[0]), lay_vec(norm_ffn2[0]), lay_vec(norm_final)], axis=1))
    r2 = launch2(uT, kvc, decay_fwd, decay_bwd)
    _DBG["r2"] = r2
    r3 = launch3(x1T, uT, r2, modv, norms, ret_gn_w, p_ret, p_four, w_out, w13_ffn2, w2_ffn2)
    out = np.concatenate([r["out"] for r in r3], axis=0)
    return out[None].astype(np.float32)


def fused_tables(core):
    j = np.arange(128)[:, None]
    i = np.arange(128)[None, :]
    tabs = np.stack([np.maximum(i - j, 0), (i >= j), np.maximum(j - i, 0), (j >= i)], axis=1).astype(np.float32)
    row = np.arange(128, dtype=np.float32)
    xiex = np.stack([np.broadcast_to(row + 1.0, (128, 128)), np.broadcast_to(128.0 - row, (128, 128))], axis=1)
    p = np.arange(128, dtype=np.float32)
    pex = np.zeros((128, 40), dtype=np.float32)
    pex[:, 0] = 127.0 - p
    pex[:, 1] = p
    for blk in range(16):
        pex[:, 2 + blk] = 2047.0 - 128.0 * blk - p
        pex[:, 18 + blk] = 128.0 * blk + p
    for blk in range(2):
        pex[:, 34 + blk] = 255.0 - 128.0 * blk - p
        pex[:, 36 + blk] = 128.0 * blk + p
    pex[:, 38] = 128.0
    cx = np.zeros((2, 9), dtype=np.float32)
    cm = np.zeros((2, 9), dtype=np.float32)
    cx[0, 0], cm[0, 0] = 2048.0 * core, 1.0
    cx[1, 0], cm[1, 0] = 2048.0 * (7 - core), 1.0
    for k2 in range(8):
        if k2 < core:
            cx[0, 1 + k2], cm[0, 1 + k2] = 2048.0 * (core - 1 - k2), 1.0
        if k2 > core:
            cx[1, 1 + k2], cm[1, 1 + k2] = 2048.0 * (k2 - core - 1), 1.0
    coefx = np.ascontiguousarray(np.broadcast_to(cx[None], (128, 2, 9))).astype(np.float32)
    coefm = np.ascontiguousarray(np.broadcast_to(cm[None], (128, 2, 9))).astype(np.float32)
    return (np.ascontiguousarray(tabs), np.ascontiguousarray(xiex.astype(np.float32)), pex, coefx, coefm)


def kernel_fused(x, c, ctx, c_ctx, w_ada, b_ada, norm_ffn1, w13_ffn1, w2_ffn1, norm_mix, w_in,
                 decay_fwd, decay_bwd, ret_gn_w, p_ret, p_four, w_out, norm_ffn2, w13_ffn2, w2_ffn2,
                 norm_final):
    import ml_dtypes
    nc = _get("fused", build_fused)
    dft, cs, tw, cdsd = dft_tables()
    rt = rope_tables()
    wp = perm_win(w_in[0])
    w_inF = np.concatenate([lay_kmajor(wp[:, 0:1024], 128), lay_kmajor(wp[:, 1024:2048], 128),
                            lay_kmajor(wp[:, 7168:8192], 128), lay_kmajor(wp[:, 8192:9216], 128)], axis=0)
    w_inT = np.concatenate([lay_kmajor(wp[:, 2048:4096], 512), lay_kmajor(wp[:, 4096:6144], 512),
                            lay_kmajor(wp[:, 6144:7168], 512), lay_kmajor(wp[:, 1024:2048], 512)], axis=0)
    dec = np.ascontiguousarray(np.broadcast_to(
        np.concatenate([decay_fwd[0], decay_bwd[0]])[None, :], (128, 8))).astype(np.float32)
    shared = {
        "ctx": np.ascontiguousarray(ctx[0]),
        "cvec": np.ascontiguousarray(np.stack([lay_vec(c[0]), lay_vec(c_ctx)], axis=2)),
        "w_ada": lay_kmajor(w_ada[0], 128), "b_ada": lay_vec(b_ada[0]),
        "norms": np.ascontiguousarray(np.stack(
            [lay_vec(norm_ffn1[0]), lay_vec(norm_mix[0]), lay_vec(norm_ffn2[0]), lay_vec(norm_final)], axis=1)),
        "w13_1": lay_w13(w13_ffn1[0]), "w2_1": lay_kmajor(w2_ffn1[0], 128),
        "w13_2": lay_w13(w13_ffn2[0]), "w2_2": lay_kmajor(w2_ffn2[0], 128),
        "w_inF": np.ascontiguousarray(w_inF), "w_inT": np.ascontiguousarray(w_inT),
        "dec": dec, "gnw": lay_vec(ret_gn_w[0]), "cdsd": cdsd, "dft": dft, "cs": cs, "tw": tw,
        "identb": np.eye(128, dtype=np.float32).astype(ml_dtypes.bfloat16),
        "ident": np.eye(128, dtype=np.float32),
        "p_ret": lay_kmajor(p_ret[0], 128), "p_four": lay_kmajor(p_four[0], 128), "w_out": lay_kmajor(w_out[0], 128),
    }
    in_maps = []
    for k in range(NCORES):
        m = dict(shared)
        m["x"] = np.ascontiguousarray(x[0, k * NT:(k + 1) * NT])
        m["rope"] = rt[k]
        m["tabs"], m["xiex"], m["pex"], m["coefx"], m["coefm"] = fused_tables(k)
        in_maps.append(m)
    r = _run(nc, in_maps)
    _DBG["fused"] = r
    out = np.concatenate([rr["out"] for rr in r], axis=0)
    return out[None].astype(np.float32)


def kernel(x, c, ctx, c_ctx, w_ada, b_ada, norm_ffn1, w13_ffn1, w2_ffn1, norm_mix, w_in,
           decay_fwd, decay_bwd, ret_gn_w, p_ret, p_four, w_out, norm_ffn2, w13_ffn2, w2_ffn2,
           norm_final):
    f = lambda a: np.asarray(a, dtype=np.float32)
    args = [f(a) for a in (x, c, ctx, c_ctx, w_ada, b_ada, norm_ffn1, w13_ffn1, w2_ffn1, norm_mix, w_in,
                           decay_fwd, decay_bwd, ret_gn_w, p_ret, p_four, w_out, norm_ffn2, w13_ffn2, w2_ffn2,
                           norm_final)]
    return kernel_fused(*args)
```

```python
import os
import numpy as np
import concourse.bass as bass
import concourse.mybir as mybir
from concourse.bass_utils import run_bass_kernel_spmd

F32 = mybir.dt.float32
BF16 = mybir.dt.bfloat16
AF = mybir.ActivationFunctionType
ALU = mybir.AluOpType

NCORES = 8
D = 1024
L = 16384
NT = L // NCORES
DFF = 2816
NJ = DFF // 128
KC = D // 128
EPS = 1e-6
TG = 512
NTG = NT // TG


class Prog:
    ENG = ("pe", "act", "dve", "pool", "sp")

    def __init__(self, nc):
        self.nc = nc
        self.streams = {e: [] for e in self.ENG}
        self.cnt = {}
        self.res = {}
        self.waited = {e: {} for e in self.ENG}
        self.sems = {}
        self.n_ops = 0
        self._cid = {}

    def _tok_wait(self, eng, waits, tok):
        if tok is None:
            return
        k, v = tok
        if k == eng and eng == "pe":
            return
        if self.waited[eng].get(k, 0) >= v:
            return
        if waits.get(k, 0) < v:
            waits[k] = v

    def uniq(self, name):
        self._u = getattr(self, "_u", 0) + 1
        return (name, "u", self._u)

    def cid(self, e):
        k = id(e)
        if k not in self._cid:
            self._cid[k] = e.partition_id()
        return self._cid[k]

    def op(self, eng, fn, reads=(), writes=(), sem=None, inc=True, amt=None):
        waits = {}
        for r in reads:
            st = self.res.get(r)
            if st:
                self._tok_wait(eng, waits, st[0])
        for w in writes:
            st = self.res.get(w)
            if st:
                self._tok_wait(eng, waits, st[0])
                for t in st[1]:
                    self._tok_wait(eng, waits, t)
        if sem is None:
            key, amt = eng, 1
        elif amt is None:
            key, amt = sem, 16
        else:
            key = sem
        if inc:
            self.cnt[key] = self.cnt.get(key, 0) + amt
            tok = (key, self.cnt[key])
        else:
            assert sem is None
            tok = (key, self.cnt.get(key, 0) + 1)
        for k, v in waits.items():
            self.waited[eng][k] = v
        if inc and sem is None:
            pass
        for r in reads:
            self.res.setdefault(r, [None, []])[1].append(tok)
        for w in writes:
            self.res[w] = [tok, []]
        self.streams[eng].append((fn, waits, key if inc else None, amt))
        self.n_ops += 1

    def barrier(self, exclude=()):
        snap = {k: v for k, v in self.cnt.items() if k not in exclude}
        for e in self.ENG:
            waits = {}
            for k, v in snap.items():
                if k == e:
                    continue
                if self.waited[e].get(k, 0) < v:
                    waits[k] = v
                    self.waited[e][k] = v
            if waits:
                self.streams[e].append((None, waits, None, 0))

    def sem(self, key):
        if key not in self.sems:
            self.sems[key] = self.nc.alloc_semaphore("s_" + key)
        return self.sems[key]

    def emit(self):
        nc = self.nc
        self.barrier()
        for k in self.cnt:
            self.sem(k)
        print("semaphores used", len(self.sems))

        def run(ename, e):
            for fn, waits, key, amt in self.streams[ename]:
                for k, v in waits.items():
                    e.wait_ge(self.sems[k], v)
                if fn is not None:
                    ins = fn(e)
                    if key is not None:
                        ins.then_inc(self.sems[key], amt)

        with nc.Block() as block:
            @block.tensor
            def _(e):
                run("pe", e)

            @block.scalar
            def _(e):
                run("act", e)

            @block.vector
            def _(e):
                run("dve", e)

            @block.gpsimd
            def _(e):
                run("pool", e)

            @block.sync
            def _(e):
                run("sp", e)


def lay_kmajor(w, ncols_blk):
    K, N = w.shape
    nb = N // ncols_blk
    return np.ascontiguousarray(w.reshape(K // 128, 128, nb, ncols_blk).transpose(2, 1, 0, 3))


def lay_w13(w13):
    a = w13[:, :DFF].reshape(KC, 128, NJ, 128)
    b = w13[:, DFF:].reshape(KC, 128, NJ, 128)
    ab = np.concatenate([a, b], axis=3)
    return np.ascontiguousarray(ab.transpose(2, 1, 0, 3))


def lay_vec(v):
    return np.ascontiguousarray(v.reshape(-1, 128).T)


class KB:
    def __init__(self, name):
        self.nc = bass.Bass("TRN2", target_bir_lowering=False)
        self.P = Prog(self.nc)
        self.sb = lambda name, shape, dt: self.nc.alloc_sbuf_tensor("sb_" + name, shape, dt)
        self.psum = self.nc.alloc_psum_tensor("psum", [128, 8, 512], F32)
        self.bank_ctr = 0
        self.ctr = {}

    def din(self, name, shape, dt=F32):
        return self.nc.dram_tensor(name, list(shape), dt, kind="ExternalInput").ap()

    def dout(self, name, shape, dt=F32):
        return self.nc.dram_tensor(name, list(shape), dt, kind="ExternalOutput").ap()

    def bank(self, i):
        return self.psum[:, i, :]

    def next_bank(self, nb=6):
        b = self.bank_ctr % nb
        self.bank_ctr += 1
        return b

    def rot(self, key, n):
        v = self.ctr.get(key, 0)
        self.ctr[key] = v + 1
        return v % n

    def common(self):
        P, sb = self.P, self.sb
        self.ident = sb("identF", [128, 128], F32)
        self.ones_b = sb("onesB", [128, 128], BF16)
        self.epsc = sb("epsc", [128, 1], F32)
        self.sq = [sb(f"sq{i}", [128, TG], BF16) for i in range(2)]
        self.rstd = [sb(f"rstd{i}", [128, TG], F32) for i in range(2)]
        self.tmpf = [sb(f"tmpf{i}", [128, TG], F32) for i in range(4)]
        ident_d = self.din("ident", [128, 128])
        P.op("sp", lambda e: e.dma_start(out=self.ident[:], in_=ident_d), writes=["ident"], sem="ld_ident")
        P.op("dve", lambda e: e.memset(self.ones_b[:], 1.0), writes=["ones_b"])
        P.op("dve", lambda e: e.memset(self.epsc[:], EPS), writes=["epsc"])

    def rms_stats(self, xv, xres_key, ncols, slot, pb=6):
        P = self.P
        for kc in range(KC):
            s = self.rot("sq", 2)
            P.op("act", lambda e, kc=kc, s=s: e.activation(
                out=self.sq[s][:, :ncols], in_=xv(kc), func=AF.Square),
                reads=[xres_key], writes=[("sq", s)])
            P.op("pe", lambda e, kc=kc, s=s: e.matmul(
                self.psum[:, pb, :ncols], lhsT=self.ones_b[:], rhs=self.sq[s][:, :ncols],
                start=(kc == 0), stop=(kc == KC - 1)),
                reads=[("sq", s), "ones_b"], writes=[("ps", pb)], inc=True)
        P.op("act", lambda e: e.activation(
            out=self.rstd[slot][:, :ncols], in_=self.psum[:, pb, :ncols], func=AF.Sqrt, scale=1.0 / D,
            bias=self.epsc[:, 0:1]),
            reads=[("ps", pb), "epsc"], writes=[("rstd", slot)])
        P.op("dve", lambda e: e.reciprocal(out=self.rstd[slot][:, :ncols], in_=self.rstd[slot][:, :ncols]),
             reads=[("rstd", slot)], writes=[("rstd", slot)])

    def norm_mod(self, xv, xres_key, hv, h_key, ncols, scale_ap, bias_ap):
        P = self.P
        slot = self.rot("rstd", 2)
        self.rms_stats(xv, xres_key, ncols, slot)
        for kc in range(KC):
            t = self.rot("tmpf", 4)
            P.op("dve", lambda e, kc=kc, t=t: e.tensor_tensor(
                out=self.tmpf[t][:, :ncols], in0=xv(kc), in1=self.rstd[slot][:, :ncols], op=ALU.mult),
                reads=[xres_key, ("rstd", slot)], writes=[("tmpf", t)])
            P.op("act", lambda e, kc=kc, t=t: e.activation(
                out=hv(kc), in_=self.tmpf[t][:, :ncols], func=AF.Identity,
                scale=scale_ap(kc), bias=bias_ap(kc)),
                reads=[("tmpf", t), "modv"], writes=[h_key])

    def ffn(self, groups, w13_d, w2_d, w13b, w2b, gate_ap, wk13="w13b", wk2="w2b"):
        P = self.P
        for j in range(NJ):
            s = self.rot(wk13, len(w13b))
            P.op("pool", lambda e, j=j, s=s: e.dma_start(out=w13b[s][:], in_=w13_d[j]),
                 writes=[(wk13, s)], sem=f"{wk13}{s}")
            for G in groups:
                n = G["n"]
                pa, pb = self.next_bank(), self.next_bank()
                for (pbk, c0) in ((pa, 0), (pb, 128)):
                    for kc in range(KC):
                        P.op("pe", lambda e, s=s, kc=kc, G=G, pbk=pbk, c0=c0, n=n: e.matmul(
                            self.psum[:, pbk, :n], lhsT=w13b[s][:, kc, c0:c0 + 128], rhs=G["h"](kc),
                            start=(kc == 0), stop=(kc == KC - 1)),
                            reads=[(wk13, s), G["hk"]], writes=[("ps", pbk)], inc=(kc == KC - 1))
                t = self.rot("tmpf", 4)
                P.op("act", lambda e, pa=pa, t=t, n=n: e.activation(
                    out=self.tmpf[t][:, :n], in_=self.psum[:, pa, :n], func=AF.Silu),
                    reads=[("ps", pa)], writes=[("tmpf", t)])
                P.op("dve", lambda e, pb=pb, t=t, j=j, G=G, n=n: e.tensor_tensor(
                    out=G["g"](j), in0=self.psum[:, pb, :n], in1=self.tmpf[t][:, :n], op=ALU.mult),
                    reads=[("ps", pb), ("tmpf", t)], writes=[G["gk"]])
        for dc in range(KC):
            s = self.rot(wk2, len(w2b))
            for h2 in range(2):
                P.op("pool", lambda e, dc=dc, s=s, h2=h2: e.dma_start(
                    out=w2b[s][:, h2 * 11:(h2 + 1) * 11, :], in_=w2_d[dc, :, h2 * 11:(h2 + 1) * 11, :]),
                    writes=[(wk2, s)], sem=f"{wk2}{s}")
            for G in groups:
                n = G["n"]
                pb = self.next_bank()
                for j in range(NJ):
                    P.op("pe", lambda e, s=s, j=j, G=G, pb=pb, n=n: e.matmul(
                        self.psum[:, pb, :n], lhsT=w2b[s][:, j, :], rhs=G["g"](j),
                        start=(j == 0), stop=(j == NJ - 1)),
                        reads=[(wk2, s), G["gk"]], writes=[("ps", pb)], inc=(j == NJ - 1))
                P.op("dve", lambda e, dc=dc, G=G, pb=pb, n=n: e.scalar_tensor_tensor(
                    out=G["x"](dc), in0=self.psum[:, pb, :n], scalar=gate_ap(dc),
                    in1=G["x"](dc), op0=ALU.mult, op1=ALU.add),
                    reads=[("ps", pb), "modv", G["xk"]], writes=[G["xk"]])

    def load_tokmajor_to_featmajor(self, src_d, nchunks, dst_fn, dst_key_fn, xt):
        P = self.P
        for c in range(nchunks):
            s = self.rot("xt", 2)
            P.op("sp", lambda e, c=c, s=s: e.dma_start(out=xt[s][:], in_=src_d[c * 128:(c + 1) * 128, :]),
                 writes=[("xt", s)], sem=f"xt{s}")
            for half in range(2):
                b = self.next_bank()
                for q in range(4):
                    dc = half * 4 + q
                    P.op("pe", lambda e, s=s, dc=dc, b=b, q=q: e.transpose(
                        out=self.psum[:, b, q * 128:(q + 1) * 128], in_=xt[s][:, dc * 128:(dc + 1) * 128],
                        identity=self.ident[:]),
                        reads=[("xt", s), "ident"], writes=[("ps", b)], inc=(q == 3))
                src = self.psum[:, b, :].rearrange("p (q t) -> p q t", q=4)
                if half == 0:
                    P.op("act", lambda e, c=c, half=half, src=src: e.copy(out=dst_fn(c, half), in_=src),
                         reads=[("ps", b)], writes=[dst_key_fn(c)])
                else:
                    P.op("dve", lambda e, c=c, half=half, src=src: e.tensor_copy(out=dst_fn(c, half), in_=src),
                         reads=[("ps", b)], writes=[dst_key_fn(c)])


NCH_U = 72
CTX = 256


def build_l1():
    kb = KB("l1")
    nc, P, sb, psum = kb.nc, kb.P, kb.sb, kb.psum
    x_d = kb.din("x", [NT, D])
    ctx_d = kb.din("ctx", [CTX, D])
    c_d = kb.din("cvec", [128, KC, 2])
    wada_d = kb.din("w_ada", [72, 128, KC, 128])
    bada_d = kb.din("b_ada", [128, 72])
    nrm_d = kb.din("norms", [128, 4, KC])
    w13a_d = kb.din("w13_1", [NJ, 128, KC, 256])
    w2a_d = kb.din("w2_1", [KC, 128, NJ, 128])
    win_d = kb.din("w_in", [NCH_U, 128, KC, 128])
    rope_d = kb.din("rope", [2, 128, 4, NT // 2])
    x1_o = kb.dout("x1T", [128, KC, NT])
    u_o = kb.dout("uT", [NCH_U, 128, NT], BF16)
    kvc_o = kb.dout("kvc", [24, 128, CTX], BF16)
    mod_o = kb.dout("modv", [128, 72, 2])

    kb.common()
    xres = sb("xres", [128, KC, NT], F32)
    xc = sb("xc", [128, KC, CTX], F32)
    hT = sb("hT", [128, KC, NT // 2], BF16)
    gbuf = sb("gbuf", [128, NJ, NT // 2], BF16)
    w13b = [sb(f"w13b{i}", [128, KC, 256], BF16) for i in range(3)]
    w2b = [sb(f"w2b{i}", [128, NJ, 128], BF16) for i in range(2)]
    xt = [sb(f"xt{i}", [128, D], F32) for i in range(2)]
    ropet = sb("ropet", [128, 4, NT // 2], F32)
    cv = sb("cv", [128, KC, 2], F32)
    cvs = sb("cvs", [128, KC, 2], BF16)
    bada = sb("bada", [128, 72], F32)
    nrm = sb("nrm", [128, 4, KC], F32)
    mod = sb("mod", [128, 72, 2], F32)
    eff = sb("eff", [128, 3, KC, 2], F32)
    gate = sb("gate", [128, 3, KC, 2], F32)
    ubuf = [sb(f"ubuf{i}", [128, TG], BF16) for i in range(4)]

    P.op("sp", lambda e: e.dma_start(out=cv[:], in_=c_d), writes=["cv"], sem="ld0")
    P.op("sp", lambda e: e.dma_start(out=bada[:], in_=bada_d), writes=["bada"], sem="ld0")
    P.op("sp", lambda e: e.dma_start(out=nrm[:], in_=nrm_d), writes=["nrm"], sem="ld0")
    P.op("act", lambda e: e.activation(out=cvs[:], in_=cv[:], func=AF.Silu), reads=["cv"], writes=["cvs"])

    for nb in range(72):
        s = kb.rot("w13b", 3)
        P.op("pool", lambda e, nb=nb, s=s: e.dma_start(out=w13b[s][:, :, 0:128], in_=wada_d[nb]),
             writes=[("w13b", s)], sem=f"w13b{s}")
        for kc in range(KC):
            P.op("pe", lambda e, nb=nb, s=s, kc=kc: e.matmul(
                psum[:, 7, nb * 2:nb * 2 + 2], lhsT=w13b[s][:, kc, 0:128], rhs=cvs[:, kc, :],
                start=(kc == 0), stop=(kc == KC - 1)),
                reads=[("w13b", s), "cvs"], writes=[("ps", 7)], inc=(kc == KC - 1))
    P.op("dve", lambda e: e.tensor_tensor(
        out=mod[:], in0=psum[:, 7, 0:144].rearrange("p (n s) -> p n s", s=2),
        in1=bada[:].unsqueeze(2).broadcast_to([128, 72, 2]), op=ALU.add),
        reads=[("ps", 7), "bada"], writes=["modv"])
    for n in range(3):
        P.op("dve", lambda e, n=n: e.scalar_tensor_tensor(
            out=eff[:, n], in0=mod[:, (3 * n + 1) * 8:(3 * n + 2) * 8, :], scalar=1.0,
            in1=nrm[:, n, :].unsqueeze(2).broadcast_to([128, KC, 2]), op0=ALU.add, op1=ALU.mult),
            reads=["modv", "nrm"], writes=["modv"])
        P.op("dve", lambda e, n=n: e.tensor_scalar(
            out=gate[:, n], in0=mod[:, (3 * n + 2) * 8:(3 * n + 3) * 8, :],
            scalar1=(1.0 if n == 1 else 0.5), scalar2=None, op0=ALU.mult),
            reads=["modv"], writes=["modv"])
    P.op("sp", lambda e: e.dma_start(out=mod_o, in_=mod[:]), reads=["modv"], sem="st_mod")

    def sc_ap(n, s):
        return lambda kc: eff[:, n, kc, s:s + 1]

    def sh_ap(n, s):
        return lambda kc: mod[:, (3 * n) * 8 + kc, s:s + 1]

    def gt_ap(n, s):
        return lambda dc: gate[:, n, dc, s:s + 1]

    kb.load_tokmajor_to_featmajor(
        x_d, NT // 128,
        lambda c, half: xres[:, half * 4:half * 4 + 4, c * 128:(c + 1) * 128],
        lambda c: ("xres", c // 4), xt)
    kb.load_tokmajor_to_featmajor(
        ctx_d, CTX // 128,
        lambda c, half: xc[:, half * 4:half * 4 + 4, c * 128:(c + 1) * 128],
        lambda c: "xc", xt)

    def project(groups, chunks, rope, out_fn):
        pend = {}
        for ch in chunks:
            s = kb.rot("w13b", 3)
            P.op("pool", lambda e, ch=ch, s=s: e.dma_start(out=w13b[s][:, :, 0:128], in_=win_d[ch]),
                 writes=[("w13b", s)], sem=f"w13b{s}")
            for gi, G in enumerate(groups):
                n = G["n"]
                pb = kb.next_bank()
                for kc in range(KC):
                    P.op("pe", lambda e, s=s, kc=kc, G=G, pb=pb, n=n: e.matmul(
                        psum[:, pb, :n], lhsT=w13b[s][:, kc, 0:128], rhs=G["h"](kc),
                        start=(kc == 0), stop=(kc == KC - 1)),
                        reads=[("w13b", s), G["hk"]], writes=[("ps", pb)], inc=(kc == KC - 1))
                if rope and ch < 16:
                    if ch % 2 == 0:
                        pend[gi] = pb
                        continue
                    pa = pend[gi]
                    c0 = G["col0"]
                    ti = 0 if ch < 8 else 2
                    cs = ropet[:, ti, c0:c0 + n]
                    sn = ropet[:, ti + 1, c0:c0 + n]
                    t1, t2, t3, t4 = [kb.rot("tmpf", 4) for _ in range(4)]
                    tm = kb.tmpf
                    P.op("dve", lambda e, pa=pa, t1=t1, cs=cs, n=n: e.tensor_tensor(
                        out=tm[t1][:, :n], in0=psum[:, pa, :n], in1=cs, op=ALU.mult),
                        reads=[("ps", pa), "ropet"], writes=[("tmpf", t1)])
                    P.op("dve", lambda e, pb=pb, t2=t2, sn=sn, n=n: e.tensor_tensor(
                        out=tm[t2][:, :n], in0=psum[:, pb, :n], in1=sn, op=ALU.mult),
                        reads=[("ps", pb), "ropet"], writes=[("tmpf", t2)])
                    P.op("dve", lambda e, pa=pa, t3=t3, sn=sn, n=n: e.tensor_tensor(
                        out=tm[t3][:, :n], in0=psum[:, pa, :n], in1=sn, op=ALU.mult),
                        reads=[("ps", pa), "ropet"], writes=[("tmpf", t3)])
                    P.op("dve", lambda e, pb=pb, t4=t4, cs=cs, n=n: e.tensor_tensor(
                        out=tm[t4][:, :n], in0=psum[:, pb, :n], in1=cs, op=ALU.mult),
                        reads=[("ps", pb), "ropet"], writes=[("tmpf", t4)])
                    u1, u2 = kb.rot("ubuf", 4), kb.rot("ubuf", 4)
                    P.op("dve", lambda e, t1=t1, t2=t2, u1=u1, n=n: e.tensor_tensor(
                        out=ubuf[u1][:, :n], in0=tm[t1][:, :n], in1=tm[t2][:, :n], op=ALU.subtract),
                        reads=[("tmpf", t1), ("tmpf", t2)], writes=[("ubuf", u1)])
                    P.op("dve", lambda e, t3=t3, t4=t4, u2=u2, n=n: e.tensor_tensor(
                        out=ubuf[u2][:, :n], in0=tm[t3][:, :n], in1=tm[t4][:, :n], op=ALU.add),
                        reads=[("tmpf", t3), ("tmpf", t4)], writes=[("ubuf", u2)])
                    for (uu, chh) in ((u1, ch - 1), (u2, ch)):
                        P.op("sp", lambda e, uu=uu, chh=chh, G=G, n=n: e.dma_start(
                            out=out_fn(chh, G), in_=ubuf[uu][:, :n]),
                            reads=[("ubuf", uu)], sem=f"stu{uu}")
                else:
                    u = kb.rot("ubuf", 4)
                    P.op("act", lambda e, pb=pb, u=u, n=n: e.copy(out=ubuf[u][:, :n], in_=psum[:, pb, :n]),
                         reads=[("ps", pb)], writes=[("ubuf", u)])
                    P.op("sp", lambda e, u=u, ch=ch, G=G, n=n: e.dma_start(
                        out=out_fn(ch, G), in_=ubuf[u][:, :n]),
                        reads=[("ubuf", u)], sem=f"stu{u}")

    for tile in range(2):
        P.op("sp", lambda e, tile=tile: e.dma_start(out=ropet[:], in_=rope_d[tile]),
             writes=["ropet"], sem="ld_rope")
        groups = []
        for loc in range(2):
            tg = tile * 2 + loc
            groups.append(dict(
                h=(lambda kc, loc=loc: hT[:, kc, loc * TG:(loc + 1) * TG]), hk=("hT", loc),
                g=(lambda j, loc=loc: gbuf[:, j, loc * TG:(loc + 1) * TG]), gk=("gbuf", loc),
                x=(lambda dc, tg=tg: xres[:, dc, tg * TG:(tg + 1) * TG]), xk=("xres", tg),
                n=TG, col0=loc * TG, tg=tg))
        for G in groups:
            kb.norm_mod(G["x"], G["xk"], G["h"], G["hk"], TG, sc_ap(0, 0), sh_ap(0, 0))
        kb.ffn(groups, w13a_d, w2a_d, w13b, w2b, gt_ap(0, 0))
        for G in groups:
            tg = G["tg"]
            P.op("sp", lambda e, tg=tg: e.dma_start(
                out=x1_o[:, :, tg * TG:(tg + 1) * TG], in_=xres[:, :, tg * TG:(tg + 1) * TG]),
                reads=[("xres", tg)], sem="st_x1")
            kb.norm_mod(G["x"], G["xk"], G["h"], G["hk"], TG, sc_ap(1, 0), sh_ap(1, 0))
        project(groups, list(range(NCH_U)), True,
                lambda ch, G: u_o[ch, :, G["tg"] * TG:(G["tg"] + 1) * TG])

    Gc = dict(h=(lambda kc: hT[:, kc, 0:CTX]), hk=("hT", 0),
              g=(lambda j: gbuf[:, j, 0:CTX]), gk=("gbuf", 0),
              x=(lambda dc: xc[:, dc, :]), xk="xc", n=CTX, col0=0)
    kb.norm_mod(Gc["x"], "xc", Gc["h"], Gc["hk"], CTX, sc_ap(0, 1), sh_ap(0, 1))
    kb.ffn([Gc], w13a_d, w2a_d, w13b, w2b, gt_ap(0, 1))
    kb.norm_mod(Gc["x"], "xc", Gc["h"], Gc["hk"], CTX, sc_ap(1, 1), sh_ap(1, 1))
    project([Gc], list(range(8, 32)), False, lambda ch, G: kvc_o[ch - 8])

    P.emit()
    return nc

NCHK = 130


def build_l2():
    kb = KB("l2")
    nc, P, sb, psum = kb.nc, kb.P, kb.sb, kb.psum
    qk_d = kb.din("qk", [NCHK, 128, 512], BF16)
    kv_d = kb.din("kv", [NCHK, 128, 768], BF16)
    dec_d = kb.din("dec", [128, 1])
    tabs_d = kb.din("tabs", [128, 3, 128])
    pcol_d = kb.din("pcol", [128, 2])
    farr_d = kb.din("farr", [128, 128, 128], BF16)
    dft_d = kb.din("dft", [128, 3, 128], BF16)
    cs_d = kb.din("cs", [128, 256], BF16)
    tw_d = kb.din("tw", [128, 2, 128])
    o_o = kb.dout("o", [128, 128, 512])
    F_o = kb.dout("F", [2, 128, 128 * 128], BF16)

    qkb = [sb(f"qkb{i}", [128, 512], BF16) for i in range(3)]
    kvb = [sb(f"kvb{i}", [128, 768], BF16) for i in range(3)]
    dec = sb("dec", [128, 1], F32)
    tabs = sb("tabs", [128, 3, 128], F32)
    pcol = sb("pcol", [128, 2], F32)
    one1 = sb("one1", [128, 1], F32)
    lg = sb("lg", [128, 1], F32)
    zg = sb("zg", [128, 2], F32)
    MT = sb("MT", [128, 128], F32)
    xi = sb("xi", [128, 128], F32)
    tmpe = sb("tmpe", [128, 128], F32)
    SM = [sb(f"SM{i}", [128, 128], BF16) for i in range(2)]
    qx = [sb(f"qx{i}", [128, 256], BF16) for i in range(2)]
    kz = [sb(f"kz{i}", [128, 256], BF16) for i in range(2)]
    S = sb("S", [128, 2, 512], F32)
    Sbf = [sb(f"Sbf{i}", [128, 2, 512], BF16) for i in range(2)]
    osb = [sb(f"osb{i}", [128, 512], F32) for i in range(2)]
    farr = sb("farr", [128, 128, 128], BF16)
    Z = sb("Z", [128, 2, 128, 128], BF16)
    dft = sb("dft", [128, 3, 128], BF16)
    cs = sb("cs", [128, 256], BF16)
    tw = sb("tw", [128, 2, 128], F32)
    tf = [sb(f"tf{i}", [128, 2, 128], F32) for i in range(4)]
    fo = [sb(f"fo{i}", [128, 512], BF16) for i in range(2)]

    for (t, dsrc, key) in ((dec, dec_d, "dec"), (tabs, tabs_d, "tabs"), (pcol, pcol_d, "pcol"),
                           (dft, dft_d, "dft"), (cs, cs_d, "cs"), (tw, tw_d, "tw")):
        P.op("sp", lambda e, t=t, dsrc=dsrc: e.dma_start(out=t[:], in_=dsrc), writes=[key], sem="ld0")
    for q4 in range(4):
        P.op("sp", lambda e, q4=q4: e.dma_start(out=farr[:, q4 * 32:(q4 + 1) * 32, :],
                                                in_=farr_d[:, q4 * 32:(q4 + 1) * 32, :]),
             writes=["farr"], sem="ld_f")
    P.op("dve", lambda e: e.memset(one1[:], 1.0), writes=["one1"])
    P.op("dve", lambda e: e.memset(S[:], 0.0), writes=["S"])
    P.op("act", lambda e: e.activation(out=lg[:], in_=dec[:], func=AF.Exp, scale=-1.0), reads=["dec"], writes=["lg"])
    P.op("act", lambda e: e.activation(out=lg[:], in_=lg[:], func=AF.Ln, bias=one1[:, 0:1]),
         reads=["lg", "one1"], writes=["lg"])
    P.op("dve", lambda e: e.tensor_scalar(out=lg[:], in0=lg[:], scalar1=-1.0, scalar2=None, op0=ALU.mult),
         reads=["lg"], writes=["lg"])
    P.op("act", lambda e: e.activation(out=tmpe[:], in_=tabs[:, 0, :], func=AF.Exp, scale=lg[:, 0:1]),
         reads=["lg", "tabs"], writes=["tmpe"])
    P.op("dve", lambda e: e.tensor_tensor(out=MT[:], in0=tmpe[:], in1=tabs[:, 1, :], op=ALU.mult),
         reads=["tmpe", "tabs"], writes=["MT"])
    P.op("act", lambda e: e.activation(out=xi[:], in_=tabs[:, 2, :], func=AF.Exp, scale=lg[:, 0:1]),
         reads=["lg", "tabs"], writes=["xi"])
    P.op("act", lambda e: e.activation(out=zg[:], in_=pcol[:], func=AF.Exp, scale=lg[:, 0:1]),
         reads=["lg", "pcol"], writes=["zg"])

    def fft_stage1(pi):
        b = 6 + (pi % 2)
        for u in range(2):
            ch = pi * 2 + u
            P.op("pe", lambda e, ch=ch, u=u, b=b: e.matmul(
                psum[:, b, u * 256:(u + 1) * 256], lhsT=farr[:, :, ch], rhs=cs[:],
                start=True, stop=True),
                reads=["farr", "cs"], writes=[("ps", b)], inc=(u == 1))
        Y = psum[:, b, :].rearrange("p (u c l) -> p u c l", u=2, c=2)
        Yc, Ys = Y[:, :, 0, :], Y[:, :, 1, :]
        Tc = tw[:, 0, :].unsqueeze(1).broadcast_to([128, 2, 128])
        Ts = tw[:, 1, :].unsqueeze(1).broadcast_to([128, 2, 128])
        t = [kb.rot("tf", 4) for _ in range(4)]
        for (ti, a, bb) in ((t[0], Yc, Tc), (t[1], Ys, Ts), (t[2], Yc, Ts), (t[3], Ys, Tc)):
            P.op("dve", lambda e, ti=ti, a=a, bb=bb: e.tensor_tensor(out=tf[ti][:], in0=a, in1=bb, op=ALU.mult),
                 reads=[("ps", b), "tw"], writes=[("tf", ti)])
        P.op("pool", lambda e: e.tensor_tensor(
            out=Z[:, 0, pi * 2:pi * 2 + 2, :], in0=tf[t[0]][:], in1=tf[t[1]][:], op=ALU.subtract),
            reads=[("tf", t[0]), ("tf", t[1])], writes=["Z"])
        P.op("pool", lambda e: e.tensor_tensor(
            out=Z[:, 1, pi * 2:pi * 2 + 2, :], in0=tf[t[2]][:], in1=tf[t[3]][:], op=ALU.add),
            reads=[("tf", t[2]), ("tf", t[3])], writes=["Z"])

    def fft_stage3(gi):
        zc = Z[:, 0, gi * 4:(gi + 1) * 4, :].rearrange("p a b -> p (a b)")
        zs = Z[:, 1, gi * 4:(gi + 1) * 4, :].rearrange("p a b -> p (a b)")
        for comp in range(2):
            b = 6 + comp
            if comp == 0:
                pairs = ((dft[:, 0, :], zc), (dft[:, 2, :], zs))
            else:
                pairs = ((dft[:, 1, :], zc), (dft[:, 0, :], zs))
            for ii, (w, z) in enumerate(pairs):
                P.op("pe", lambda e, w=w, z=z, b=b, ii=ii: e.matmul(
                    psum[:, b, :], lhsT=w, rhs=z, start=(ii == 0), stop=(ii == 1)),
                    reads=["Z", "dft"], writes=[("ps", b)], inc=(ii == 1))
            s = kb.rot("fo", 2)
            P.op("act", lambda e, s=s, b=b: e.copy(out=fo[s][:], in_=psum[:, b, :]),
                 reads=[("ps", b)], writes=[("fo", s)])
            P.op("sp", lambda e, s=s, comp=comp: e.dma_start(
                out=F_o[comp, :, gi * 512:(gi + 1) * 512], in_=fo[s][:]),
                reads=[("fo", s)], sem=f"stF{s}")

    for c in range(NCHK):
        s = c % 3
        P.op("sp", lambda e, c=c, s=s: e.dma_start(out=kvb[s][:], in_=kv_d[c]), writes=[("kvb", s)], sem=f"kvb{s}")
        if c >= 2:
            P.op("sp", lambda e, c=c, s=s: e.dma_start(out=qkb[s][:], in_=qk_d[c]), writes=[("qkb", s)],
                 sem=f"qkb{s}")
            s2 = c % 2
            for dc in range(2):
                P.op("pe", lambda e, s=s, dc=dc: e.matmul(
                    psum[:, 0, 0:128], lhsT=qkb[s][:, 256 + dc * 128:384 + dc * 128],
                    rhs=qkb[s][:, dc * 128:(dc + 1) * 128], start=(dc == 0), stop=(dc == 1)),
                    reads=[("qkb", s)], writes=[("ps", 0)], inc=(dc == 1))
            P.op("dve", lambda e, s2=s2: e.tensor_tensor(out=SM[s2][:], in0=psum[:, 0, 0:128], in1=MT[:], op=ALU.mult),
                 reads=[("ps", 0), "MT"], writes=[("SM", s2)])
            P.op("pool", lambda e, s=s, s2=s2: e.tensor_tensor(
                out=qx[s2][:].rearrange("p (a b) -> p a b", a=2),
                in0=qkb[s][:, 0:256].rearrange("p (a b) -> p a b", a=2),
                in1=xi[:].unsqueeze(1).broadcast_to([128, 2, 128]), op=ALU.mult),
                reads=[("qkb", s), "xi"], writes=[("qx", s2)])
            pb = 1 + (c % 2)
            sprev = (c - 1) % 2
            P.op("pe", lambda e, s=s, s2=s2, pb=pb: e.matmul(
                psum[:, pb, :], lhsT=SM[s2][:], rhs=kvb[s][:, 256:768], start=True, stop=False),
                reads=[("SM", s2), ("kvb", s)], writes=[("ps", pb)], inc=False)
            for dc in range(2):
                P.op("pe", lambda e, s2=s2, pb=pb, dc=dc, sprev=sprev: e.matmul(
                    psum[:, pb, :], lhsT=qx[s2][:, dc * 128:(dc + 1) * 128], rhs=Sbf[sprev][:, dc, :],
                    start=False, stop=(dc == 1)),
                    reads=[("qx", s2), ("Sbf", sprev)], writes=[("ps", pb)], inc=(dc == 1))
            P.op("act", lambda e, s2=s2, pb=pb: e.copy(out=osb[s2][:], in_=psum[:, pb, :]),
                 reads=[("ps", pb)], writes=[("osb", s2)])
            P.op("sp", lambda e, c=c, s2=s2: e.dma_start(out=o_o[c - 2], in_=osb[s2][:]),
                 reads=[("osb", s2)], sem=f"sto{s2}")
        s2 = c % 2
        P.op("act", lambda e, s=s, s2=s2: e.activation(
            out=kz[s2][:], in_=kvb[s][:, 0:256], func=AF.Copy, scale=zg[:, 0:1]),
            reads=[("kvb", s), "zg"], writes=[("kz", s2)])
        for dc in range(2):
            pb = 3 + dc
            P.op("pe", lambda e, s=s, s2=s2, dc=dc, pb=pb: e.matmul(
                psum[:, pb, :], lhsT=kz[s2][:, dc * 128:(dc + 1) * 128], rhs=kvb[s][:, 256:768],
                start=True, stop=True),
                reads=[("kz", s2), ("kvb", s)], writes=[("ps", pb)], inc=True)
            P.op("dve", lambda e, dc=dc, pb=pb: e.scalar_tensor_tensor(
                out=S[:, dc, :], in0=S[:, dc, :], scalar=zg[:, 1:2], in1=psum[:, pb, :],
                op0=ALU.mult, op1=ALU.add),
                reads=[("ps", pb), "zg", "S"], writes=["S"])
        P.op("act", lambda e, s2=s2: e.copy(out=Sbf[s2][:], in_=S[:]), reads=["S"], writes=[("Sbf", s2)])
        if c < 64:
            fft_stage1(c)
        elif c < 96:
            fft_stage3(c - 64)

    P.emit()
    return nc

def build_l3():
    kb = KB("l3")
    nc, P, sb, psum = kb.nc, kb.P, kb.sb, kb.psum
    x1_d = kb.din("x1T", [128, KC, NT])
    of_d = kb.din("ofT", [128, 16, NT])
    ob_d = kb.din("obT", [128, 16, NT])
    g_d = kb.din("gT", [128, 16, NT], BF16)
    ga_d = kb.din("gaT", [128, KC, NT], BF16)
    gf_d = kb.din("gfT", [128, KC, NT], BF16)
    fc_d = kb.din("FcT", [128, KC, NT], BF16)
    fs_d = kb.din("FsT", [128, KC, NT], BF16)
    mod_d = kb.din("modv", [128, 72, 2])
    nrm_d = kb.din("norms", [128, 4, KC])
    gnw_d = kb.din("gnw", [128, 16])
    cdsd_d = kb.din("cdsd", [128, 2, 2, 256], BF16)
    pret_d = kb.din("p_ret", [KC, 128, 16, 128])
    pfour_d = kb.din("p_four", [KC, 128, KC, 128])
    wout_d = kb.din("w_out", [KC, 128, KC, 128])
    w13_d = kb.din("w13_2", [NJ, 128, KC, 256])
    w2_d = kb.din("w2_2", [KC, 128, NJ, 128])
    out_d = kb.dout("out", [NT, D])

    kb.common()
    xres = sb("xres", [128, KC, NT], F32)
    hT = sb("hT", [128, KC, NT // 2], BF16)
    gbuf = sb("gbuf", [128, NJ, NT // 2], BF16)
    w13b = [sb(f"w13b{i}", [128, KC, 256], BF16) for i in range(2)]
    w2b = [sb(f"w2b{i}", [128, NJ, 128], BF16) for i in range(2)]
    mbuf = sb("mbuf", [128, KC, TG], F32)
    osum = sb("osum", [128, 4, TG], F32)
    ldf = [sb(f"ldf{i}", [128, TG], F32) for i in range(4)]
    ldb = [sb(f"ldb{i}", [128, TG], BF16) for i in range(4)]
    b16 = [sb(f"b16{i}", [128, TG], BF16) for i in range(4)]
    mean = sb("mean", [128, TG], F32)
    rs2 = sb("rs2", [128, TG], F32)
    mod = sb("mod", [128, 72, 2], F32)
    nrm = sb("nrm", [128, 4, KC], F32)
    gnw = sb("gnw", [128, 16], F32)
    eff = sb("eff", [128, 3, KC, 2], F32)
    gate = sb("gate", [128, 3, KC, 2], F32)
    cdsd = sb("cdsd", [128, 2, 2, 256], BF16)
    tmpf = kb.tmpf

    ogT = gbuf[:, 0:8, :].rearrange("p a (b t) -> p (a b) t", t=TG)
    mixT = gbuf[:, 8:12, :].rearrange("p a (b t) -> p (a b) t", t=TG)
    mT = gbuf[:, 12:16, :].rearrange("p a (b t) -> p (a b) t", t=TG)
    fcT = hT[:, 0:4, :].rearrange("p a (b t) -> p (a b) t", t=TG)
    fsT = hT[:, 4:8, :].rearrange("p a (b t) -> p (a b) t", t=TG)
    xn = mbuf
    otile = [osum[:, 0:2, :].rearrange("p a t -> p (a t)"), osum[:, 2:4, :].rearrange("p a t -> p (a t)")]

    for (t, dsrc, key) in ((mod, mod_d, "modv"), (nrm, nrm_d, "nrm"), (gnw, gnw_d, "gnw"), (cdsd, cdsd_d, "cdsd")):
        P.op("sp", lambda e, t=t, dsrc=dsrc: e.dma_start(out=t[:], in_=dsrc), writes=[key], sem="ld0")
    for n in range(3):
        P.op("dve", lambda e, n=n: e.scalar_tensor_tensor(
            out=eff[:, n], in0=mod[:, (3 * n + 1) * 8:(3 * n + 2) * 8, :], scalar=1.0,
            in1=nrm[:, n, :].unsqueeze(2).broadcast_to([128, KC, 2]), op0=ALU.add, op1=ALU.mult),
            reads=["modv", "nrm"], writes=["modv"])
        P.op("dve", lambda e, n=n: e.tensor_scalar(
            out=gate[:, n], in0=mod[:, (3 * n + 2) * 8:(3 * n + 3) * 8, :],
            scalar1=(1.0 if n == 1 else 0.5), scalar2=None, op0=ALU.mult),
            reads=["modv"], writes=["modv"])
    for tg in range(NTG):
        P.op("sp", lambda e, tg=tg: e.dma_start(out=xres[:, :, tg * TG:(tg + 1) * TG],
                                                in_=x1_d[:, :, tg * TG:(tg + 1) * TG]),
             writes=[("xres", tg)], sem="ld_x")

    def wstream(src_blk, nk, s):
        dst = w13b[s][:].rearrange("p a b -> p (a b)")[:, 0:nk * 128].rearrange("p (a b) -> p a b", b=128)
        P.op("pool", lambda e: e.dma_start(out=dst, in_=src_blk), writes=[("w13b", s)], sem=f"w13b{s}")
        return dst

    def mixer(tg):
        c0, c1 = tg * TG, (tg + 1) * TG
        for h in range(4):
            for vc in range(4):
                f = h * 4 + vc
                a, b = kb.rot("ldf", 4), kb.rot("ldf", 4)
                P.op("sp", lambda e, f=f, a=a: e.dma_start(out=ldf[a][:], in_=of_d[:, f, c0:c1]),
                     writes=[("ldf", a)], sem=f"ldf{a}")
                P.op("sp", lambda e, f=f, b=b: e.dma_start(out=ldf[b][:], in_=ob_d[:, f, c0:c1]),
                     writes=[("ldf", b)], sem=f"ldf{b}")
                P.op("dve", lambda e, vc=vc, a=a, b=b: e.tensor_tensor(
                    out=osum[:, vc, :], in0=ldf[a][:], in1=ldf[b][:], op=ALU.add),
                    reads=[("ldf", a), ("ldf", b)], writes=[("osum", vc)])
                u1, u2 = kb.rot("b16", 4), kb.rot("b16", 4)
                P.op("act", lambda e, vc=vc, u1=u1: e.copy(out=b16[u1][:], in_=osum[:, vc, :]),
                     reads=[("osum", vc)], writes=[("b16", u1)])
                P.op("act", lambda e, vc=vc, u2=u2: e.activation(out=b16[u2][:], in_=osum[:, vc, :], func=AF.Square),
                     reads=[("osum", vc)], writes=[("b16", u2)])
                P.op("pe", lambda e, vc=vc, u1=u1: e.matmul(
                    psum[:, 6, :], lhsT=kb.ones_b[:], rhs=b16[u1][:], start=(vc == 0), stop=(vc == 3)),
                    reads=[("b16", u1), "ones_b"], writes=[("ps", 6)], inc=True)
                P.op("pe", lambda e, vc=vc, u2=u2: e.matmul(
                    psum[:, 7, :], lhsT=kb.ones_b[:], rhs=b16[u2][:], start=(vc == 0), stop=(vc == 3)),
                    reads=[("b16", u2), "ones_b"], writes=[("ps", 7)], inc=True)
            P.op("act", lambda e: e.activation(out=mean[:], in_=psum[:, 6, :], func=AF.Copy, scale=1.0 / DV),
                 reads=[("ps", 6)], writes=["mean"])
            t = kb.rot("tmpf", 4)
            P.op("dve", lambda e, t=t: e.tensor_tensor(out=tmpf[t][:], in0=mean[:], in1=mean[:], op=ALU.mult),
                 reads=["mean"], writes=[("tmpf", t)])
            P.op("dve", lambda e, t=t: e.scalar_tensor_tensor(
                out=rs2[:], in0=psum[:, 7, :], scalar=1.0 / DV, in1=tmpf[t][:], op0=ALU.mult, op1=ALU.subtract),
                reads=[("ps", 7), ("tmpf", t)], writes=["rs2"])
            P.op("act", lambda e: e.activation(out=rs2[:], in_=rs2[:], func=AF.Sqrt, bias=kb.epsc[:, 0:1]),
                 reads=["rs2", "epsc"], writes=["rs2"])
            P.op("dve", lambda e: e.reciprocal(out=rs2[:], in_=rs2[:]), reads=["rs2"], writes=["rs2"])
            for vc in range(4):
                f = h * 4 + vc
                gb = kb.rot("ldb", 4)
                P.op("sp", lambda e, f=f, gb=gb: e.dma_start(out=ldb[gb][:], in_=g_d[:, f, c0:c1]),
                     writes=[("ldb", gb)], sem=f"ldb{gb}")
                t1, t2 = kb.rot("tmpf", 4), kb.rot("tmpf", 4)
                P.op("act", lambda e, gb=gb, t1=t1: e.activation(out=tmpf[t1][:], in_=ldb[gb][:], func=AF.Silu),
                     reads=[("ldb", gb)], writes=[("tmpf", t1)])
                P.op("dve", lambda e, vc=vc, t2=t2: e.tensor_tensor(
                    out=tmpf[t2][:], in0=osum[:, vc, :], in1=mean[:], op=ALU.subtract),
                    reads=[("osum", vc), "mean"], writes=[("tmpf", t2)])
                P.op("dve", lambda e, t2=t2: e.tensor_tensor(
                    out=tmpf[t2][:], in0=tmpf[t2][:], in1=rs2[:], op=ALU.mult),
                    reads=[("tmpf", t2), "rs2"], writes=[("tmpf", t2)])
                P.op("dve", lambda e, f=f, t1=t1, t2=t2: e.scalar_tensor_tensor(
                    out=ogT[:, f, :], in0=tmpf[t2][:], scalar=gnw[:, f:f + 1], in1=tmpf[t1][:],
                    op0=ALU.mult, op1=ALU.mult),
                    reads=[("tmpf", t1), ("tmpf", t2), "gnw"], writes=["ogT"])
        P.op("sp", lambda e: e.dma_start(out=fcT, in_=fc_d[:, :, c0:c1]), writes=["fcT"], sem="ld_fc")
        P.op("sp", lambda e: e.dma_start(out=fsT, in_=fs_d[:, :, c0:c1]), writes=["fcT"], sem="ld_fc")
        for dc in range(KC):
            s = kb.rot("w13b", 2)
            wv = wstream(pret_d[dc], 16, s)
            pb = kb.next_bank()
            for kc in range(16):
                P.op("pe", lambda e, wv=wv, kc=kc, pb=pb: e.matmul(
                    psum[:, pb, :], lhsT=wv[:, kc, :], rhs=ogT[:, kc, :], start=(kc == 0), stop=(kc == 15)),
                    reads=[("w13b", s), "ogT"], writes=[("ps", pb)], inc=(kc == 15))
            gb = kb.rot("ldb", 4)
            P.op("sp", lambda e, dc=dc, gb=gb: e.dma_start(out=ldb[gb][:], in_=ga_d[:, dc, c0:c1]),
                 writes=[("ldb", gb)], sem=f"ldb{gb}")
            t1 = kb.rot("tmpf", 4)
            P.op("act", lambda e, gb=gb, t1=t1: e.activation(out=tmpf[t1][:], in_=ldb[gb][:], func=AF.Sigmoid),
                 reads=[("ldb", gb)], writes=[("tmpf", t1)])
            P.op("dve", lambda e, dc=dc, pb=pb, t1=t1: e.tensor_tensor(
                out=mbuf[:, dc, :], in0=psum[:, pb, :], in1=tmpf[t1][:], op=ALU.mult),
                reads=[("ps", pb), ("tmpf", t1)], writes=[("mbuf", dc)])
        for co in range(KC):
            gI, cc = co // 2, co % 2
            pb = kb.next_bank()
            k = 0
            for kc2 in range(2):
                for comp, src in ((0, fcT), (1, fsT)):
                    P.op("pe", lambda e, kc2=kc2, comp=comp, src=src, pb=pb, k=k, gI=gI, cc=cc: e.matmul(
                        psum[:, pb, :], lhsT=cdsd[:, kc2, comp, cc * 128:(cc + 1) * 128],
                        rhs=src[:, gI * 2 + kc2, :], start=(k == 0), stop=(k == 3)),
                        reads=["cdsd", "fcT"], writes=[("ps", pb)], inc=(k == 3))
                    k += 1
            P.op("act", lambda e, co=co, pb=pb: e.copy(out=mixT[:, co, :], in_=psum[:, pb, :]),
                 reads=[("ps", pb)], writes=["mixT"])
        for dc in range(KC):
            s = kb.rot("w13b", 2)
            wv = wstream(pfour_d[dc], KC, s)
            pb = kb.next_bank()
            for kc in range(KC):
                P.op("pe", lambda e, wv=wv, kc=kc, pb=pb: e.matmul(
                    psum[:, pb, :], lhsT=wv[:, kc, :], rhs=mixT[:, kc, :], start=(kc == 0), stop=(kc == KC - 1)),
                    reads=[("w13b", s), "mixT"], writes=[("ps", pb)], inc=(kc == KC - 1))
            gb = kb.rot("ldb", 4)
            P.op("sp", lambda e, dc=dc, gb=gb: e.dma_start(out=ldb[gb][:], in_=gf_d[:, dc, c0:c1]),
                 writes=[("ldb", gb)], sem=f"ldb{gb}")
            t1, t2 = kb.rot("tmpf", 4), kb.rot("tmpf", 4)
            P.op("act", lambda e, gb=gb, t1=t1: e.activation(out=tmpf[t1][:], in_=ldb[gb][:], func=AF.Sigmoid),
                 reads=[("ldb", gb)], writes=[("tmpf", t1)])
            P.op("dve", lambda e, pb=pb, t1=t1, t2=t2: e.tensor_tensor(
                out=tmpf[t2][:], in0=psum[:, pb, :], in1=tmpf[t1][:], op=ALU.mult),
                reads=[("ps", pb), ("tmpf", t1)], writes=[("tmpf", t2)])
            P.op("dve", lambda e, dc=dc, t2=t2: e.tensor_tensor(
                out=mT[:, dc, :], in0=mbuf[:, dc, :], in1=tmpf[t2][:], op=ALU.add),
                reads=[("mbuf", dc), ("tmpf", t2)], writes=["mT"])
        for dc in range(KC):
            s = kb.rot("w13b", 2)
            wv = wstream(wout_d[dc], KC, s)
            pb = kb.next_bank()
            for kc in range(KC):
                P.op("pe", lambda e, wv=wv, kc=kc, pb=pb: e.matmul(
                    psum[:, pb, :], lhsT=wv[:, kc, :], rhs=mT[:, kc, :], start=(kc == 0), stop=(kc == KC - 1)),
                    reads=[("w13b", s), "mT"], writes=[("ps", pb)], inc=(kc == KC - 1))
            P.op("dve", lambda e, dc=dc, pb=pb: e.scalar_tensor_tensor(
                out=xres[:, dc, c0:c1], in0=psum[:, pb, :], scalar=gate[:, 1, dc, 0:1],
                in1=xres[:, dc, c0:c1], op0=ALU.mult, op1=ALU.add),
                reads=[("ps", pb), "modv", ("xres", tg)], writes=[("xres", tg)])

    for tile in range(2):
        for loc in range(2):
            mixer(tile * 2 + loc)
        P.barrier()
        groups = []
        for loc in range(2):
            tg = tile * 2 + loc
            groups.append(dict(
                h=(lambda kc, loc=loc: hT[:, kc, loc * TG:(loc + 1) * TG]), hk=("hT", loc),
                g=(lambda j, loc=loc: gbuf[:, j, loc * TG:(loc + 1) * TG]), gk=("gbuf", loc),
                x=(lambda dc, tg=tg: xres[:, dc, tg * TG:(tg + 1) * TG]), xk=("xres", tg),
                n=TG, tg=tg))
        for G in groups:
            kb.norm_mod(G["x"], G["xk"], G["h"], G["hk"], TG,
                        lambda kc: eff[:, 2, kc, 0:1], lambda kc: mod[:, 6 * 8 + kc, 0:1])
        kb.ffn(groups, w13_d, w2_d, w13b, w2b, lambda dc: gate[:, 2, dc, 0:1])
        for G in groups:
            tg = G["tg"]
            slot = kb.rot("rstd", 2)
            kb.rms_stats(G["x"], G["xk"], TG, slot)
            for kc in range(KC):
                t = kb.rot("tmpf", 4)
                P.op("dve", lambda e, kc=kc, t=t, G=G, slot=slot: e.tensor_tensor(
                    out=tmpf[t][:], in0=G["x"](kc), in1=kb.rstd[slot][:], op=ALU.mult),
                    reads=[G["xk"], ("rstd", slot)], writes=[("tmpf", t)])
                P.op("act", lambda e, kc=kc, t=t: e.activation(
                    out=xn[:, kc, :], in_=tmpf[t][:], func=AF.Copy, scale=nrm[:, 3, kc:kc + 1]),
                    reads=[("tmpf", t), "nrm"], writes=["xn"])
            for cc in range(4):
                c = tg * 4 + cc
                s = c % 2
                for half in range(2):
                    b = kb.next_bank()
                    for q in range(4):
                        dc = half * 4 + q
                        P.op("pe", lambda e, cc=cc, dc=dc, b=b, q=q: e.transpose(
                            out=psum[:, b, q * 128:(q + 1) * 128], in_=xn[:, dc, cc * 128:(cc + 1) * 128],
                            identity=kb.ident[:]),
                            reads=["xn", "ident"], writes=[("ps", b)], inc=(q == 3))
                    if half == 0:
                        P.op("act", lambda e, s=s, b=b: e.copy(out=otile[s][:, 0:512], in_=psum[:, b, :]),
                             reads=[("ps", b)], writes=[("otile", s)])
                    else:
                        P.op("dve", lambda e, s=s, b=b: e.tensor_copy(out=otile[s][:, 512:1024], in_=psum[:, b, :]),
                             reads=[("ps", b)], writes=[("otile", s)])
                P.op("sp", lambda e, c=c, s=s: e.dma_start(out=out_d[c * 128:(c + 1) * 128, :], in_=otile[s][:]),
                     reads=[("otile", s)], sem=f"st{s}")
        P.barrier()

    P.emit()
    return nc

def carve(t, off, shape, dt):
    n = int(np.prod(shape[1:])) * (2 if dt == F32 else 1)
    ap = t[:, off:off + n]
    if dt == F32:
        ap = ap.bitcast(F32)
    if len(shape) == 3:
        ap = ap.rearrange("p (a b) -> p a b", a=shape[1])
    elif len(shape) == 4:
        ap = ap.rearrange("p (a b c) -> p a b c", a=shape[1], b=shape[2])
    return ap


NB = NT // 128
RG = [list(range(NCORES))]


def build_fused():
    kb = KB("fused")
    nc, P, sb, psum = kb.nc, kb.P, kb.sb, kb.psum
    x_d = kb.din("x", [NT, D])
    ctx_d = kb.din("ctx", [CTX, D])
    c_d = kb.din("cvec", [128, KC, 2])
    wada_d = kb.din("w_ada", [72, 128, KC, 128])
    bada_d = kb.din("b_ada", [128, 72])
    nrm_d = kb.din("norms", [128, 4, KC])
    w13a_d = kb.din("w13_1", [NJ, 128, KC, 256])
    w2a_d = kb.din("w2_1", [KC, 128, NJ, 128])
    w13c_d = kb.din("w13_2", [NJ, 128, KC, 256])
    w2c_d = kb.din("w2_2", [KC, 128, NJ, 128])
    winf_d = kb.din("w_inF", [32, 128, KC, 128])
    wint_d = kb.din("w_inT", [12, 128, KC, 512])
    rope_d = kb.din("rope", [2, 128, 4, NT // 2])
    dec_d = kb.din("dec", [128, 8])
    tabs_d = kb.din("tabs", [128, 4, 128])
    xiex_d = kb.din("xiex", [128, 2, 128])
    pex_d = kb.din("pex", [128, 40])
    coefx_d = kb.din("coefx", [128, 2, 9])
    coefm_d = kb.din("coefm", [128, 2, 9])
    gnw_d = kb.din("gnw", [128, 16])
    cdsd_d = kb.din("cdsd", [128, 2, 2, 256], BF16)
    dft_d = kb.din("dft", [128, 3, 128], BF16)
    cs_d = kb.din("cs", [128, 256], BF16)
    tw_d = kb.din("tw", [128, 2, 128])
    idb_d = kb.din("identb", [128, 128], BF16)
    pret_d = kb.din("p_ret", [KC, 128, 16, 128])
    pfour_d = kb.din("p_four", [KC, 128, KC, 128])
    wout_d = kb.din("w_out", [KC, 128, KC, 128])
    out_d = kb.dout("out", [NT, D])

    def dscr(name, shape, dt=BF16, shared=False):
        if shared:
            return nc.dram_tensor(name, list(shape), dt, addr_space="Shared").ap()
        if os.environ.get("MK_DBG") and name not in ("fsend", "totsend", "f2send"):
            return nc.dram_tensor(name, list(shape), dt, kind="ExternalOutput").ap()
        return nc.dram_tensor(name, list(shape), dt).ap()

    Uq = dscr("Uq", [16, 128, NT])
    Ug = dscr("Ug", [16, 128, NT])
    Ktok = dscr("Ktok", [NT, 1024])
    Vtok = dscr("Vtok", [NT, 2048])
    Gtok = dscr("Gtok", [NT, 2048])
    fsend = dscr("fsend", [8 * NT, 128])
    fgath = dscr("fgath", [8 * 8 * NT, 128], shared=True)
    totsend = dscr("totsend", [8 * 256, 512], F32)
    totg = dscr("totg", [8 * 8 * 256, 512], F32, shared=True)
    sctx = dscr("sctx", [8 * 256, 512], F32)
    sstart = dscr("sstart", [8 * 256, 512], F32)
    snaps = dscr("snaps", [4 * NB * 128, 1024])
    f2send = dscr("f2send", [2 * 8 * 128, NT])
    f2g = dscr("f2g", [8 * 2 * 8 * 128, NT], shared=True)
    floc = dscr("floc", [8 * 16, 128 * 128])
    f2loc = dscr("f2loc", [8 * 2 * 128, NT])

    kb.common()
    xres = sb("xres", [128, KC, NT], F32)
    gb_raw = sb("gbuf", [128, NJ * 1024], BF16)
    A2 = sb("A2", [128, 19456], BF16)
    gbuf = carve(gb_raw, 0, [128, NJ, 1024], BF16)
    wbuf = [sb(f"wbuf{i}", [128, NJ * 128], BF16) for i in range(2)]
    w13b = [carve(w, 0, [128, KC, 256], BF16) for w in wbuf]
    w2b = [carve(w, 0, [128, NJ, 128], BF16) for w in wbuf]
    identb = sb("identb", [128, 128], BF16)
    cv = sb("cv", [128, KC, 2], F32)
    cvs = sb("cvs", [128, KC, 2], BF16)
    bada = sb("bada", [128, 72], F32)
    nrm = sb("nrm", [128, 4, KC], F32)
    mod = sb("mod", [128, 72, 2], F32)
    eff = sb("eff", [128, 3, KC, 2], F32)
    gate = sb("gate", [128, 3, KC, 2], F32)
    ubuf = [sb(f"ubuf{i}", [128, TG], BF16) for i in range(4)]
    dec = sb("dec", [128, 8], F32)
    lg = sb("lg", [128, 8], F32)
    one1 = sb("one1", [128, 1], F32)
    tabs = sb("tabs", [128, 4, 128], F32)
    xiex = sb("xiex", [128, 2, 128], F32)
    pex = sb("pex", [128, 40], F32)
    coefx = sb("coefx", [128, 2, 9], F32)
    coefm = sb("coefm", [128, 2, 9], F32)
    coef = sb("coef", [128, 8, 9], F32)
    zt = sb("zt", [128, 8, 16], F32)
    zc = sb("zc", [128, 8, 2], F32)
    zeta = sb("zeta", [128, 8], F32)
    gch = sb("gch", [128, 8], F32)
    mt = sb("mt", [128, 4, 128], F32)
    xi = sb("xi", [128, 8, 128], F32)
    tmpe = sb("tmpe", [128, 128], F32)
    gnw = sb("gnw", [128, 16], F32)
    cdsd = sb("cdsd", [128, 2, 2, 256], BF16)
    dft = sb("dft", [128, 3, 128], BF16)
    cs = sb("cs", [128, 256], BF16)
    tw = sb("tw", [128, 2, 128], F32)
    small = sb("small", [128, 32], F32)
    tmpf = kb.tmpf

    for (t, dsrc, key) in ((cv, c_d, "cv"), (bada, bada_d, "bada"), (nrm, nrm_d, "nrm"), (dec, dec_d, "dec"),
                           (tabs, tabs_d, "tabs"), (xiex, xiex_d, "xiex"), (pex, pex_d, "pex"),
                           (coefx, coefx_d, "coefx"), (coefm, coefm_d, "coefm"), (gnw, gnw_d, "gnw"),
                           (cdsd, cdsd_d, "cdsd"), (dft, dft_d, "dft"), (cs, cs_d, "cs"), (tw, tw_d, "tw"),
                           (identb, idb_d, "identb")):
        P.op("sp", lambda e, t=t, dsrc=dsrc: e.dma_start(out=t[:], in_=dsrc), writes=[key], sem=f"ld_{key}")
    P.op("dve", lambda e: e.memset(one1[:], 1.0), writes=["one1"])
    P.op("act", lambda e: e.activation(out=cvs[:], in_=cv[:], func=AF.Silu), reads=["cv"], writes=["cvs"])
    P.op("act", lambda e: e.activation(out=lg[:], in_=dec[:], func=AF.Exp, scale=-1.0), reads=["dec"], writes=["lg"])
    P.op("act", lambda e: e.activation(out=lg[:], in_=lg[:], func=AF.Ln, bias=one1[:, 0:1]),
         reads=["lg", "one1"], writes=["lg"])
    P.op("dve", lambda e: e.tensor_scalar(out=lg[:], in0=lg[:], scalar1=-1.0, scalar2=None, op0=ALU.mult),
         reads=["lg"], writes=["lg"])
    for dh in range(8):
        d, h = dh // 4, dh % 4
        sc = lg[:, dh:dh + 1]
        ecol = 2 + 16 * d
        P.op("act", lambda e, dh=dh, sc=sc, ecol=ecol: e.activation(
            out=zt[:, dh, :], in_=pex[:, ecol:ecol + 16], func=AF.Exp, scale=sc), reads=["lg", "pex"], writes=["dtab"])
        P.op("act", lambda e, dh=dh, sc=sc, d=d: e.activation(
            out=zc[:, dh, :], in_=pex[:, 34 + 2 * d:36 + 2 * d], func=AF.Exp, scale=sc), reads=["lg", "pex"], writes=["dtab"])
        P.op("act", lambda e, dh=dh, sc=sc, d=d: e.activation(
            out=zeta[:, dh:dh + 1], in_=pex[:, d:d + 1], func=AF.Exp, scale=sc), reads=["lg", "pex"], writes=["dtab"])
        P.op("act", lambda e, dh=dh, sc=sc: e.activation(
            out=gch[:, dh:dh + 1], in_=pex[:, 38:39], func=AF.Exp, scale=sc), reads=["lg", "pex"], writes=["dtab"])
        P.op("act", lambda e, dh=dh, sc=sc, d=d: e.activation(
            out=xi[:, dh, :], in_=xiex[:, d, :], func=AF.Exp, scale=sc), reads=["lg", "xiex"], writes=["dtab"])
        P.op("act", lambda e, dh=dh, sc=sc, d=d: e.activation(
            out=coef[:, dh, :], in_=coefx[:, d, :], func=AF.Exp, scale=sc), reads=["lg", "coefx"], writes=["coef"])
        P.op("dve", lambda e, dh=dh, d=d: e.tensor_tensor(
            out=coef[:, dh, :], in0=coef[:, dh, :], in1=coefm[:, d, :], op=ALU.mult),
            reads=["coef", "coefm"], writes=["coef"])
    for h in range(4):
        P.op("act", lambda e, h=h: e.activation(out=tmpe[:], in_=tabs[:, 0, :], func=AF.Exp, scale=lg[:, h:h + 1]),
             reads=["lg", "tabs"], writes=["tmpe"])
        P.op("dve", lambda e, h=h: e.tensor_tensor(out=mt[:, h, :], in0=tmpe[:], in1=tabs[:, 1, :], op=ALU.mult),
             reads=["tmpe", "tabs"], writes=["dtab"])
        P.op("act", lambda e, h=h: e.activation(out=tmpe[:], in_=tabs[:, 2, :], func=AF.Exp,
                                                scale=lg[:, 4 + h:5 + h]),
             reads=["lg", "tabs", "dtab"], writes=["tmpe"])
        P.op("dve", lambda e, h=h: e.tensor_tensor(out=tmpe[:], in0=tmpe[:], in1=tabs[:, 3, :], op=ALU.mult),
             reads=["tmpe", "tabs"], writes=["tmpe"])
        P.op("dve", lambda e, h=h: e.tensor_tensor(out=mt[:, h, :], in0=mt[:, h, :], in1=tmpe[:], op=ALU.add),
             reads=["tmpe", "dtab"], writes=["dtab"])

    for nb in range(72):
        s = kb.rot("wbuf", 2)
        P.op("pool", lambda e, nb=nb, s=s: e.dma_start(out=w13b[s][:, :, 0:128], in_=wada_d[nb]),
             writes=[("wbuf", s)], sem=f"wbuf{s}")
        for kc in range(KC):
            P.op("pe", lambda e, nb=nb, s=s, kc=kc: e.matmul(
                psum[:, 7, nb * 2:nb * 2 + 2], lhsT=w13b[s][:, kc, 0:128], rhs=cvs[:, kc, :],
                start=(kc == 0), stop=(kc == KC - 1)),
                reads=[("wbuf", s), "cvs"], writes=[("ps", 7)], inc=(kc == KC - 1))
    P.op("dve", lambda e: e.tensor_tensor(
        out=mod[:], in0=psum[:, 7, 0:144].rearrange("p (n s) -> p n s", s=2),
        in1=bada[:].unsqueeze(2).broadcast_to([128, 72, 2]), op=ALU.add),
        reads=[("ps", 7), "bada"], writes=["modv"])
    for n in range(3):
        P.op("dve", lambda e, n=n: e.scalar_tensor_tensor(
            out=eff[:, n], in0=mod[:, (3 * n + 1) * 8:(3 * n + 2) * 8, :], scalar=1.0,
            in1=nrm[:, n, :].unsqueeze(2).broadcast_to([128, KC, 2]), op0=ALU.add, op1=ALU.mult),
            reads=["modv", "nrm"], writes=["modv"])
        P.op("dve", lambda e, n=n: e.tensor_scalar(
            out=gate[:, n], in0=mod[:, (3 * n + 2) * 8:(3 * n + 3) * 8, :],
            scalar1=(1.0 if n == 1 else 0.5), scalar2=None, op0=ALU.mult),
            reads=["modv"], writes=["modv"])

    def sc_ap(n, s):
        return lambda kc: eff[:, n, kc, s:s + 1]

    def sh_ap(n, s):
        return lambda kc: mod[:, (3 * n) * 8 + kc, s:s + 1]

    def gt_ap(n, s):
        return lambda dc: gate[:, n, dc, s:s + 1]

    hT = carve(A2, 0, [128, KC, 1024], BF16)
    ropet = carve(A2, 8192, [128, 4, 1024], F32)
    xt = [carve(A2, 8192 + i * 2048, [128, D], F32) for i in range(2)]

    kb.load_tokmajor_to_featmajor(
        x_d, NT // 128, lambda c, half: xres[:, half * 4:half * 4 + 4, c * 128:(c + 1) * 128],
        lambda c: ("xres", c // 4), xt)
    xcs = carve(A2, 12288, [128, KC, CTX], F32)
    P.barrier()

    ktokbuf = carve(gb_raw, 0, [128, 8, 1024], BF16)
    wtok = [carve(gb_raw, 8192 + i * 4096, [128, KC, 512], BF16) for i in range(2)]
    stg = [carve(gb_raw, 16384 + i * 512, [128, 512], BF16) for i in range(4)]
    kctok = sb("kctok", [128, 2, 1024], BF16)
    vctok = sb("vctok", [128, 2, 2048], BF16)

    def proj_feat(groups, chunks, rope, out_fn, tile):
        pend = {}
        for ch in chunks:
            s = kb.rot("wbuf", 2)
            P.op("pool", lambda e, ch=ch, s=s: e.dma_start(out=w13b[s][:, :, 0:128], in_=winf_d[ch]),
                 writes=[("wbuf", s)], sem=f"wbuf{s}")
            for gi, G in enumerate(groups):
                n = G["n"]
                pb = kb.next_bank()
                for kc in range(KC):
                    P.op("pe", lambda e, s=s, kc=kc, G=G, pb=pb, n=n: e.matmul(
                        psum[:, pb, :n], lhsT=w13b[s][:, kc, 0:128], rhs=G["h"](kc),
                        start=(kc == 0), stop=(kc == KC - 1)),
                        reads=[("wbuf", s), G["hk"]], writes=[("ps", pb)], inc=(kc == KC - 1))
                if rope and ch < 16:
                    if ch % 2 == 0:
                        pend[gi] = pb
                        continue
                    pa = pend[gi]
                    c0 = G["col0"]
                    ti = 0 if ch < 8 else 2
                    csn = ropet[:, ti, c0:c0 + n]
                    sn = ropet[:, ti + 1, c0:c0 + n]
                    t1, t2, t3, t4 = [kb.rot("tmpf", 4) for _ in range(4)]
                    tm = tmpf
                    for (tt, pp, tb) in ((t1, pa, csn), (t2, pb, sn), (t3, pa, sn), (t4, pb, csn)):
                        P.op("dve", lambda e, tt=tt, pp=pp, tb=tb, n=n: e.tensor_tensor(
                            out=tm[tt][:, :n], in0=psum[:, pp, :n], in1=tb, op=ALU.mult),
                            reads=[("ps", pp), "ropet"], writes=[("tmpf", tt)])
                    u1, u2 = kb.rot("ubuf", 4), kb.rot("ubuf", 4)
                    P.op("dve", lambda e, t1=t1, t2=t2, u1=u1, n=n: e.tensor_tensor(
                        out=ubuf[u1][:, :n], in0=tm[t1][:, :n], in1=tm[t2][:, :n], op=ALU.subtract),
                        reads=[("tmpf", t1), ("tmpf", t2)], writes=[("ubuf", u1)])
                    P.op("dve", lambda e, t3=t3, t4=t4, u2=u2, n=n: e.tensor_tensor(
                        out=ubuf[u2][:, :n], in0=tm[t3][:, :n], in1=tm[t4][:, :n], op=ALU.add),
                        reads=[("tmpf", t3), ("tmpf", t4)], writes=[("ubuf", u2)])
                    for (uu, chh) in ((u1, ch - 1), (u2, ch)):
                        P.op("sp", lambda e, uu=uu, chh=chh, G=G, n=n: e.dma_start(
                            out=out_fn(chh, G), in_=ubuf[uu][:, :n]),
                            reads=[("ubuf", uu)], writes=[P.uniq("Uq")], sem=f"stu{uu}")
                        if chh >= 8:
                            pt = kb.next_bank()
                            pv = psum[:, pt, 0:256].bitcast(BF16)
                            for q in range(4):
                                P.op("pe", lambda e, uu=uu, q=q, pv=pv: e.transpose(
                                    out=pv[:, q * 128:(q + 1) * 128], in_=ubuf[uu][:, q * 128:(q + 1) * 128],
                                    identity=identb[:]),
                                    reads=[("ubuf", uu), "identb"], writes=[("ps", pt)], inc=(q == 3))
                            b0 = (G["col0"] // 128)
                            P.op("act", lambda e, pv=pv, b0=b0, chh=chh: e.copy(
                                out=ktokbuf[:, b0:b0 + 4, (chh - 8) * 128:(chh - 7) * 128],
                                in_=pv.rearrange("p (q t) -> p q t", q=4)),
                                reads=[("ps", pt)], writes=["ktokbuf"])
                else:
                    u = kb.rot("ubuf", 4)
                    P.op("act", lambda e, pb=pb, u=u, n=n: e.copy(out=ubuf[u][:, :n], in_=psum[:, pb, :n]),
                         reads=[("ps", pb)], writes=[("ubuf", u)])
                    P.op("sp", lambda e, u=u, ch=ch, G=G, n=n: e.dma_start(
                        out=out_fn(ch, G), in_=ubuf[u][:, :n]),
                        reads=[("ubuf", u)], writes=[P.uniq("Uq")], sem=f"stu{u}")

    def proj_tok(hview, hk, nblk, cbs, sink):
        for cb in cbs:
            s = kb.rot("wtok", 2)
            for half in range(4):
                P.op("pool", lambda e, cb=cb, s=s, half=half: e.dma_start(
                    out=wtok[s][:, half * 2:half * 2 + 2, :], in_=wint_d[cb, :, half * 2:half * 2 + 2, :]),
                    writes=[("wtok", s)], sem=f"wtok{s}")
            for blk in range(nblk):
                pb = kb.next_bank()
                for kc in range(KC):
                    P.op("pe", lambda e, s=s, kc=kc, blk=blk, pb=pb: e.matmul(
                        psum[:, pb, :], lhsT=hview(kc, blk), rhs=wtok[s][:, kc, :],
                        start=(kc == 0), stop=(kc == KC - 1)),
                        reads=[("wtok", s), hk], writes=[("ps", pb)], inc=(kc == KC - 1))
                sink(cb, blk, pb)

    for tile in range(2):
        P.op("sp", lambda e, tile=tile: e.dma_start(out=ropet, in_=rope_d[tile]), writes=["ropet"], sem="ld_rope")
        groups = []
        for loc in range(2):
            tg = tile * 2 + loc
            groups.append(dict(
                h=(lambda kc, loc=loc: hT[:, kc, loc * TG:(loc + 1) * TG]), hk=("hT", loc),
                g=(lambda j, loc=loc: gbuf[:, j, loc * TG:(loc + 1) * TG]), gk=("gbuf", loc),
                x=(lambda dc, tg=tg: xres[:, dc, tg * TG:(tg + 1) * TG]), xk=("xres", tg),
                n=TG, col0=loc * TG, tg=tg))
        for G in groups:
            kb.norm_mod(G["x"], G["xk"], G["h"], G["hk"], TG, sc_ap(0, 0), sh_ap(0, 0))
        kb.ffn(groups, w13a_d, w2a_d, w13b, w2b, gt_ap(0, 0), wk13="wbuf", wk2="wbuf")
        for G in groups:
            kb.norm_mod(G["x"], G["xk"], G["h"], G["hk"], TG, sc_ap(1, 0), sh_ap(1, 0))
        P.barrier()

        def outf(ch, G):
            if ch < 16:
                return Uq[ch, :, G["tg"] * TG:(G["tg"] + 1) * TG]
            return Ug[ch - 16, :, G["tg"] * TG:(G["tg"] + 1) * TG]
        proj_feat(groups, list(range(32)), True, outf, tile)
        for blk in range(8):
            r0 = tile * 1024 + blk * 128
            P.op("sp", lambda e, blk=blk, r0=r0: e.dma_start(out=Ktok[r0:r0 + 128, :], in_=ktokbuf[:, blk, :]),
                 reads=["ktokbuf"], writes=[P.uniq("Ktok")], sem="st_kt")

        def sink(cb, blk, pb, tile=tile):
            u = kb.rot("stg", 4)
            P.op("act", lambda e, pb=pb, u=u: e.copy(out=stg[u], in_=psum[:, pb, :]),
                 reads=[("ps", pb)], writes=[("stg", u)])
            r0 = tile * 1024 + blk * 128
            if cb < 4:
                P.op("sp", lambda e, u=u: e.dma_start(out=Vtok[r0:r0 + 128, cb * 512:(cb + 1) * 512], in_=stg[u]),
                     reads=[("stg", u)], writes=[P.uniq("Vtok")], sem=f"ststg{u}")
            elif cb < 8:
                P.op("sp", lambda e, u=u: e.dma_start(
                    out=Gtok[r0:r0 + 128, (cb - 4) * 512:(cb - 3) * 512], in_=stg[u]),
                    reads=[("stg", u)], writes=[P.uniq("Gtok")], sem=f"ststg{u}")
            else:
                c4 = (cb - 8) * 4
                dst = fsend.rearrange("(c t) ch -> t c ch", c=8)[r0:r0 + 128, c4:c4 + 4, :]
                P.op("sp", lambda e, u=u, dst=dst: e.dma_start(
                    out=dst, in_=stg[u].rearrange("p (c ch) -> p c ch", c=4)),
                    reads=[("stg", u)], writes=[P.uniq("fsend")], sem=f"ststg{u}")
        proj_tok(lambda kc, blk: hT[:, kc, blk * 128:(blk + 1) * 128], ("hT", 0), 8, list(range(10)), sink)
        P.barrier()

    P.op("pool", lambda e: e.collective_compute("AllGather", ALU.bypass, replica_groups=RG, ins=[fsend], outs=[fgath]),
         reads=["fsend"], writes=["fgath"], sem="cc_f", amt=1)

    kb.load_tokmajor_to_featmajor(
        ctx_d, CTX // 128, lambda c, half: xcs[:, half * 4:half * 4 + 4, c * 128:(c + 1) * 128],
        lambda c: "xc", xt)
    Gc = dict(h=(lambda kc: hT[:, kc, 0:CTX]), hk=("hT", 0),
              g=(lambda j: gbuf[:, j, 0:CTX]), gk=("gbuf", 0),
              x=(lambda dc: xcs[:, dc, :]), xk="xc", n=CTX, col0=0)
    kb.norm_mod(Gc["x"], "xc", Gc["h"], Gc["hk"], CTX, sc_ap(0, 1), sh_ap(0, 1))
    kb.ffn([Gc], w13a_d, w2a_d, w13b, w2b, gt_ap(0, 1), wk13="wbuf", wk2="wbuf")
    kb.norm_mod(Gc["x"], "xc", Gc["h"], Gc["hk"], CTX, sc_ap(1, 1), sh_ap(1, 1))
    P.barrier()

    def sinkc(cb, blk, pb):
        if cb >= 10:
            P.op("act", lambda e: e.copy(out=kctok[:, blk, (cb - 10) * 512:(cb - 9) * 512], in_=psum[:, pb, :]),
                 reads=[("ps", pb)], writes=["kvctok"])
        else:
            P.op("act", lambda e: e.copy(out=vctok[:, blk, cb * 512:(cb + 1) * 512], in_=psum[:, pb, :]),
                 reads=[("ps", pb)], writes=["kvctok"])
    proj_tok(lambda kc, blk: hT[:, kc, blk * 128:(blk + 1) * 128], ("hT", 0), 2, [0, 1, 2, 3, 10, 11], sinkc)
    P.barrier()

    kzb = [carve(A2, i * 256, [128, 256], BF16) for i in range(2)]
    sst = [carve(A2, 512 + i * 2048, [128, 2, 512], F32) for i in range(2)]
    kall = carve(A2, 4608, [128, NB, 256], BF16)
    vall = carve(A2, 4608 + 4096, [128, NB, 512], BF16)

    def accumulate_state(kview, vview, nblk, zcol, dst_rows, dh, inkey="kv_in"):
        for blk in range(nblk):
            s = kb.rot("kzb", 2)
            P.op("act", lambda e, blk=blk, s=s: e.activation(
                out=kzb[s], in_=kview(blk), func=AF.Copy, scale=zcol(blk)),
                reads=[inkey, "dtab"], writes=[("kzb", s)])
            for dc in range(2):
                P.op("pe", lambda e, blk=blk, s=s, dc=dc: e.matmul(
                    psum[:, 3 + dc, :], lhsT=kzb[s][:, dc * 128:(dc + 1) * 128], rhs=vview(blk),
                    start=(blk == 0), stop=(blk == nblk - 1)),
                    reads=[("kzb", s), inkey], writes=[("ps", 3 + dc)], inc=True)
        s = kb.rot("sst", 2)
        for dc in range(2):
            P.op("act", lambda e, s=s, dc=dc: e.copy(out=sst[s][:, dc, :], in_=psum[:, 3 + dc, :]),
                 reads=[("ps", 3 + dc)], writes=[("sst", s)])
        P.op("sp", lambda e, s=s: e.dma_start(
            out=dst_rows.rearrange("(dc p) v -> p dc v", dc=2), in_=sst[s]),
            reads=[("sst", s)], writes=[P.uniq("states")], sem=f"stst{s}")

    for dh in range(8):
        h = dh % 4
        accumulate_state(lambda blk, h=h: kctok[:, blk, h * 256:(h + 1) * 256],
                         lambda blk, h=h: vctok[:, blk, h * 512:(h + 1) * 512], 2,
                         lambda blk, dh=dh: zc[:, dh, blk:blk + 1], sctx[dh * 256:(dh + 1) * 256, :], dh,
                         inkey="kvctok")
    for h in range(4):
        P.op("sp", lambda e, h=h: e.dma_start(
            out=kall, in_=Ktok[:, h * 256:(h + 1) * 256].rearrange("(b p) d -> p b d", p=128)),
            reads=["Ktok"], writes=["kv_in"], sem="ld_kv")
        P.op("sp", lambda e, h=h: e.dma_start(
            out=vall, in_=Vtok[:, h * 512:(h + 1) * 512].rearrange("(b p) d -> p b d", p=128)),
            reads=["Vtok"], writes=["kv_in"], sem="ld_kv")
        for d in range(2):
            dh = d * 4 + h
            accumulate_state(lambda blk: kall[:, blk, :], lambda blk: vall[:, blk, :], NB,
                             lambda blk, dh=dh: zt[:, dh, blk:blk + 1], totsend[dh * 256:(dh + 1) * 256, :], dh)
    P.barrier()
    if os.environ.get("MK_DBG"):
        totdbg = dscr("totdbg", [8 * 256, 512], F32)
        P.op("sp", lambda e: e.dma_start(out=totdbg, in_=totsend), reads=["states"], sem="dbg")
    P.op("pool", lambda e: e.collective_compute("AllGather", ALU.bypass, replica_groups=RG,
                                                ins=[totsend], outs=[totg]),
         reads=["states"], writes=["totg"], sem="cc_t", amt=1)

    acc = kctok[:].rearrange("p a b -> p (a b)").bitcast(F32).rearrange("p (a b) -> p a b", a=2)
    vflat = vctok[:].rearrange("p a b -> p (a b)")
    term = [vflat[:, i * 2048:(i + 1) * 2048].bitcast(F32).rearrange("p (a b) -> p a b", a=2) for i in range(2)]

    def phase6_piece(dh):
        for t9 in range(9):
            s = kb.rot("term", 2)
            if t9 == 0:
                src = sctx[dh * 256:(dh + 1) * 256, :]
            else:
                r0 = ((t9 - 1) * 8 + dh) * 256
                src = totg[r0:r0 + 256, :]
            P.op("sp", lambda e, s=s, src=src: e.dma_start(
                out=term[s], in_=src.rearrange("(dc p) v -> p dc v", dc=2)),
                reads=["states", "totg"], writes=[("term", s)], sem=f"ldterm{s}")
            if t9 == 0:
                P.op("dve", lambda e, s=s, dh=dh: e.tensor_scalar(
                    out=acc, in0=term[s], scalar1=coef[:, dh, 0:1], scalar2=None, op0=ALU.mult),
                    reads=[("term", s), "coef"], writes=["acc"])
            else:
                P.op("dve", lambda e, s=s, dh=dh, t9=t9: e.scalar_tensor_tensor(
                    out=acc, in0=term[s], scalar=coef[:, dh, t9:t9 + 1], in1=acc, op0=ALU.mult, op1=ALU.add),
                    reads=[("term", s), "coef", "acc"], writes=["acc"])
        P.op("sp", lambda e, dh=dh: e.dma_start(
            out=sstart[dh * 256:(dh + 1) * 256, :].rearrange("(dc p) v -> p dc v", dc=2), in_=acc),
            reads=["acc"], writes=[P.uniq("sstart")], sem="st_ss")

    farr = carve(gb_raw, 0, [128, 128, 128], BF16)
    Zlo = carve(A2, 0, [128, 2, 64, 128], BF16)
    NTF, NFO = 6, 6
    tf = [carve(gb_raw, 16384 + i * 512, [128, 2, 128], F32) for i in range(NTF)]
    fo = [carve(gb_raw, 16384 + NTF * 512 + i * 512, [128, 512], BF16) for i in range(NFO)]
    fg5 = fgath.rearrange("(r c tb l2) ch -> r c tb (l2 ch)", r=8, c=8, tb=16)
    P.op("sp", lambda e: e.dma_start(
        out=floc.rearrange("(r a t) n -> r a t n", r=8, a=1),
        in_=fg5[:, bass.ds(P.cid(e), 1), :, :]),
        reads=["fgath"], writes=["floc"], sem="ld_floc")
    for q4 in range(4):
        P.op("sp", lambda e, q4=q4: e.dma_start(
            out=farr[32 * q4:32 * q4 + 32, :, :].rearrange("p a b -> p (a b)"),
            in_=floc[32 * q4:32 * q4 + 32, :]),
            reads=["floc"], writes=["farr"], sem="ld_farr")
    f2v = f2send.rearrange("(comp r ch) t -> comp r ch t", comp=2, r=8)
    for hf in range(2):
        for pi in range(32):
            if pi % 8 == 4:
                phase6_piece(hf * 4 + pi // 8)
            b = 6 + (pi % 2)
            for u in range(2):
                ch = hf * 64 + pi * 2 + u
                P.op("pe", lambda e, ch=ch, u=u, b=b: e.matmul(
                    psum[:, b, u * 256:(u + 1) * 256], lhsT=farr[:, :, ch], rhs=cs[:], start=True, stop=True),
                    reads=["farr", "cs"], writes=[("ps", b)], inc=(u == 1))
            Y = psum[:, b, :].rearrange("p (u c l) -> p u c l", u=2, c=2)
            Yc, Ys = Y[:, :, 0, :], Y[:, :, 1, :]
            Tc = tw[:, 0, :].unsqueeze(1).broadcast_to([128, 2, 128])
            Ts = tw[:, 1, :].unsqueeze(1).broadcast_to([128, 2, 128])
            t = [kb.rot("tf", NTF) for _ in range(4)]
            for (ti, a, bb) in ((t[0], Yc, Tc), (t[1], Ys, Ts), (t[2], Yc, Ts), (t[3], Ys, Tc)):
                P.op("dve", lambda e, ti=ti, a=a, bb=bb: e.tensor_tensor(out=tf[ti], in0=a, in1=bb, op=ALU.mult),
                     reads=[("ps", b), "tw"], writes=[("tf", ti)])
            P.op("dve", lambda e, t=t, pi=pi: e.tensor_tensor(
                out=Zlo[:, 0, pi * 2:pi * 2 + 2, :], in0=tf[t[0]], in1=tf[t[1]], op=ALU.subtract),
                reads=[("tf", t[0]), ("tf", t[1])], writes=["Z"])
            P.op("dve", lambda e, t=t, pi=pi: e.tensor_tensor(
                out=Zlo[:, 1, pi * 2:pi * 2 + 2, :], in0=tf[t[2]], in1=tf[t[3]], op=ALU.add),
                reads=[("tf", t[2]), ("tf", t[3])], writes=["Z"])
        for gi in range(16):
            zcv = Zlo[:, 0, gi * 4:(gi + 1) * 4, :].rearrange("p a b -> p (a b)")
            zsv = Zlo[:, 1, gi * 4:(gi + 1) * 4, :].rearrange("p a b -> p (a b)")
            for comp in range(2):
                b = 6 + comp
                pairs = ((dft[:, 0, :], zcv), (dft[:, 2, :], zsv)) if comp == 0 else \
                        ((dft[:, 1, :], zcv), (dft[:, 0, :], zsv))
                for ii, (w, z) in enumerate(pairs):
                    P.op("pe", lambda e, w=w, z=z, b=b, ii=ii: e.matmul(
                        psum[:, b, :], lhsT=w, rhs=z, start=(ii == 0), stop=(ii == 1)),
                        reads=["Z", "dft"], writes=[("ps", b)], inc=(ii == 1))
                s = kb.rot("fo", NFO)
                P.op("act", lambda e, s=s, b=b: e.copy(out=fo[s], in_=psum[:, b, :]),
                     reads=[("ps", b)], writes=[("fo", s)])
                ch0 = hf * 64 + gi * 4
                for r in range(8):
                    dst = f2v[comp, r, ch0:ch0 + 4, :].rearrange("ch (tb l) -> tb ch l", tb=16)
                    P.op("act" if r % 2 else "sp", lambda e, s=s, r=r, dst=dst: e.dma_start(
                        out=dst, in_=fo[s][16 * r:16 * r + 16, :].rearrange("p (ch l) -> p ch l", ch=4)),
                        reads=[("fo", s)], writes=[P.uniq("f2send")], sem=f"stF{s}")
    P.barrier()
    P.op("pool", lambda e: e.collective_compute("AllGather", ALU.bypass, replica_groups=RG,
                                                ins=[f2send], outs=[f2g]),
         reads=["f2send"], writes=["f2g"], sem="cc_F", amt=1)

    if os.environ.get("MK_DBG"):
        ssdbg = dscr("ssdbg", [8 * 256, 512], F32)
        P.op("sp", lambda e: e.dma_start(out=ssdbg, in_=sstart), reads=["sstart"], sem="dbg")
        P.barrier(exclude=("cc_F",))
    Sst = carve(A2, 0, [128, 2, 512], F32)
    snapb = [carve(A2, 2048 + i * 1024, [128, 2, 512], BF16) for i in range(2)]
    kzb2 = [carve(A2, 4096 + i * 256, [128, 256], BF16) for i in range(2)]
    snv = snaps.rearrange("(h b p) n -> h b p n", h=4, b=NB)
    for h in range(4):
        dh = 4 + h
        P.op("sp", lambda e, h=h: e.dma_start(
            out=kall, in_=Ktok[:, h * 256:(h + 1) * 256].rearrange("(b p) d -> p b d", p=128)),
            reads=["Ktok"], writes=["kv_in"], sem="ld_kv")
        P.op("sp", lambda e, h=h: e.dma_start(
            out=vall, in_=Vtok[:, h * 512:(h + 1) * 512].rearrange("(b p) d -> p b d", p=128)),
            reads=["Vtok"], writes=["kv_in"], sem="ld_kv")
        P.op("sp", lambda e, dh=dh: e.dma_start(
            out=Sst, in_=sstart[dh * 256:(dh + 1) * 256, :].rearrange("(dc p) v -> p dc v", dc=2)),
            reads=["sstart"], writes=["Sst"], sem="ld_ss")
        for blk in range(NB - 1, -1, -1):
            s = kb.rot("snapb", 2)
            P.op("act", lambda e, s=s: e.copy(out=snapb[s], in_=Sst), reads=["Sst"], writes=[("snapb", s)])
            P.op("sp", lambda e, s=s, h=h, blk=blk: e.dma_start(
                out=snv[h, blk], in_=snapb[s].rearrange("p a b -> p (a b)")),
                reads=[("snapb", s)], writes=[P.uniq("snaps")], sem=f"stsn{s}")
            if blk == 0:
                break
            s2 = kb.rot("kzb2", 2)
            P.op("act", lambda e, s2=s2, blk=blk, dh=dh: e.activation(
                out=kzb2[s2], in_=kall[:, blk, :], func=AF.Copy, scale=zeta[:, dh:dh + 1]),
                reads=["kv_in", "dtab"], writes=[("kzb2", s2)])
            for dc in range(2):
                P.op("pe", lambda e, s2=s2, blk=blk, dc=dc: e.matmul(
                    psum[:, 3 + dc, :], lhsT=kzb2[s2][:, dc * 128:(dc + 1) * 128], rhs=vall[:, blk, :],
                    start=True, stop=True),
                    reads=[("kzb2", s2), "kv_in"], writes=[("ps", 3 + dc)], inc=True)
                P.op("dve", lambda e, dc=dc, dh=dh: e.scalar_tensor_tensor(
                    out=Sst[:, dc, :], in0=Sst[:, dc, :], scalar=gch[:, dh:dh + 1], in1=psum[:, 3 + dc, :],
                    op0=ALU.mult, op1=ALU.add),
                    reads=[("ps", 3 + dc), "Sst", "dtab"], writes=["Sst"])
    P.barrier(exclude=("cc_F",))

    ogT = carve(gb_raw, 0, [128, 16, 1024], BF16)
    qT = carve(A2, 0, [128, 2, 1024], BF16)
    kT = carve(A2, 2048, [128, 2, 1024], BF16)
    ktk = carve(A2, 4096, [128, 8, 256], BF16)
    vtk = carve(A2, 6144, [128, 8, 512], BF16)
    gtk = [carve(A2, 10240 + i * 512, [128, 512], BF16) for i in range(2)]
    snp = [carve(A2, 11264 + i * 1024, [128, 2, 512], BF16) for i in range(2)]
    Sf = carve(A2, 13312, [128, 2, 512], F32)
    Sfb = carve(A2, 15360, [128, 2, 512], BF16)
    SMb = [carve(A2, 16384 + i * 128, [128, 128], BF16) for i in range(2)]
    qfb = [carve(A2, 16640 + i * 512, [128, 2, 2, 128], BF16) for i in range(2)]
    kzf = [carve(A2, 17664 + i * 256, [128, 256], BF16) for i in range(2)]
    ogh = [carve(A2, 18176 + i * 512, [128, 512], BF16) for i in range(2)]
    mixT = carve(A2, 0, [128, 8, 512], BF16)
    mT = carve(A2, 4096, [128, 8, 512], BF16)
    mbuf = carve(A2, 8192, [128, 8, 512], F32)
    ldb = [carve(A2, 16384 + i * 512, [128, 512], BF16) for i in range(4)]
    xn = mbuf
    otile = [carve(A2, 0 + i * 2048, [128, 1024], F32) for i in range(2)]

    def wstream(src_blk, nk, s):
        dst = wbuf[s][:, 0:nk * 128].rearrange("p (a b) -> p a b", b=128)
        P.op("pool", lambda e: e.dma_start(out=dst, in_=src_blk), writes=[("wbuf", s)], sem=f"wbuf{s}")
        return dst

    f2gv = f2g.rearrange("(k comp r ch) t -> k comp r ch t", k=8, comp=2, r=8)

    def retention_tile(tile):
        t0 = tile * 1024
        for h in range(4):
            P.op("sp", lambda e, h=h: e.dma_start(
                out=qT, in_=Uq[2 * h:2 * h + 2, :, t0:t0 + 1024].rearrange("c p t -> p c t")),
                reads=["Uq"], writes=["qT"], sem="ld_q")
            P.op("sp", lambda e, h=h: e.dma_start(
                out=kT, in_=Uq[8 + 2 * h:10 + 2 * h, :, t0:t0 + 1024].rearrange("c p t -> p c t")),
                reads=["Uq"], writes=["qT"], sem="ld_q")
            P.op("sp", lambda e, h=h: e.dma_start(
                out=ktk, in_=Ktok[t0:t0 + 1024, h * 256:(h + 1) * 256].rearrange("(b p) d -> p b d", p=128)),
                reads=["Ktok"], writes=["ktk"], sem="ld_kt")
            P.op("sp", lambda e, h=h: e.dma_start(
                out=vtk, in_=Vtok[t0:t0 + 1024, h * 512:(h + 1) * 512].rearrange("(b p) d -> p b d", p=128)),
                reads=["Vtok"], writes=["ktk"], sem="ld_kt")
            P.op("sp", lambda e, h=h: e.dma_start(
                out=Sf, in_=sstart[h * 256:(h + 1) * 256, :].rearrange("(dc p) v -> p dc v", dc=2)),
                reads=["sstart"], writes=["Sf"], sem="ld_sf")
            P.op("act", lambda e: e.copy(out=Sfb, in_=Sf), reads=["Sf"], writes=["Sfb"])
            st = {}

            def stageA(b8):
                blk = tile * 8 + b8
                c0 = b8 * 128
                sg = kb.rot("gtk", 2)
                P.op("sp", lambda e, sg=sg, blk=blk, h=h: e.dma_start(
                    out=gtk[sg], in_=Gtok[blk * 128:(blk + 1) * 128, h * 512:(h + 1) * 512]),
                    reads=["Gtok"], writes=[("gtk", sg)], sem=f"ld_g{sg}")
                sn = kb.rot("snp", 2)
                P.op("sp", lambda e, sn=sn, blk=blk, h=h: e.dma_start(
                    out=snp[sn].rearrange("p a b -> p (a b)"), in_=snv[h, blk]),
                    reads=["snaps"], writes=[("snp", sn)], sem=f"ld_sn{sn}")
                for dc in range(2):
                    P.op("pe", lambda e, dc=dc, c0=c0: e.matmul(
                        psum[:, 0, 0:128], lhsT=kT[:, dc, c0:c0 + 128], rhs=qT[:, dc, c0:c0 + 128],
                        start=(dc == 0), stop=(dc == 1)),
                        reads=["qT"], writes=[("ps", 0)], inc=(dc == 1))
                sm = kb.rot("SMb", 2)
                P.op("dve", lambda e, sm=sm, h=h: e.tensor_tensor(
                    out=SMb[sm], in0=psum[:, 0, 0:128], in1=mt[:, h, :], op=ALU.mult),
                    reads=[("ps", 0), "dtab"], writes=[("SMb", sm)])
                sq_ = kb.rot("qfb", 2)
                for d in range(2):
                    P.op("pool", lambda e, sq_=sq_, d=d, c0=c0, h=h: e.tensor_tensor(
                        out=qfb[sq_][:, d, :, :], in0=qT[:, :, c0:c0 + 128],
                        in1=xi[:, d * 4 + h, :].unsqueeze(1).broadcast_to([128, 2, 128]), op=ALU.mult),
                        reads=["qT", "dtab"], writes=[("qfb", sq_)])
                sk = kb.rot("kzf", 2)
                P.op("act", lambda e, sk=sk, b8=b8, h=h: e.activation(
                    out=kzf[sk], in_=ktk[:, b8, :], func=AF.Copy, scale=zeta[:, h:h + 1]),
                    reads=["ktk", "dtab"], writes=[("kzf", sk)])
                st[b8] = dict(sg=sg, sn=sn, sm=sm, sq=sq_, sk=sk)

            def stageB(b8):
                d_ = st[b8]
                sm, sq_, sn, sk = d_["sm"], d_["sq"], d_["sn"], d_["sk"]
                pb = 1 + (b8 % 2)
                P.op("pe", lambda e, sm=sm, b8=b8, pb=pb: e.matmul(
                    psum[:, pb, :], lhsT=SMb[sm], rhs=vtk[:, b8, :], start=True, stop=False),
                    reads=[("SMb", sm), "ktk"], writes=[("ps", pb)], inc=False)
                for dc in range(2):
                    P.op("pe", lambda e, sq_=sq_, dc=dc, pb=pb, sn=sn: e.matmul(
                        psum[:, pb, :], lhsT=qfb[sq_][:, 1, dc, :], rhs=snp[sn][:, dc, :],
                        start=False, stop=False),
                        reads=[("qfb", sq_), ("snp", sn)], writes=[("ps", pb)], inc=False)
                for dc in range(2):
                    P.op("pe", lambda e, sq_=sq_, dc=dc, pb=pb: e.matmul(
                        psum[:, pb, :], lhsT=qfb[sq_][:, 0, dc, :], rhs=Sfb[:, dc, :], start=False, stop=(dc == 1)),
                        reads=[("qfb", sq_), "Sfb"], writes=[("ps", pb)], inc=(dc == 1))
                for dc in range(2):
                    P.op("pe", lambda e, sk=sk, b8=b8, dc=dc: e.matmul(
                        psum[:, 3 + dc, :], lhsT=kzf[sk][:, dc * 128:(dc + 1) * 128], rhs=vtk[:, b8, :],
                        start=True, stop=True),
                        reads=[("kzf", sk), "ktk"], writes=[("ps", 3 + dc)], inc=True)
                    P.op("dve", lambda e, dc=dc, h=h: e.scalar_tensor_tensor(
                        out=Sf[:, dc, :], in0=Sf[:, dc, :], scalar=gch[:, h:h + 1], in1=psum[:, 3 + dc, :],
                        op0=ALU.mult, op1=ALU.add),
                        reads=[("ps", 3 + dc), "Sf", "dtab"], writes=["Sf"])
                P.op("act", lambda e: e.copy(out=Sfb, in_=Sf), reads=["Sf"], writes=["Sfb"])

            def stageC(b8):
                d_ = st[b8]
                sg = d_["sg"]
                c0 = b8 * 128
                pb = 1 + (b8 % 2)
                sl = kb.rot("small", 2) * 16
                sm_ = small[:, sl:sl + 16]
                key = ("small", sl)
                P.op("dve", lambda e, pb=pb: e.bn_stats(out=sm_[:, 0:6], in_=psum[:, pb, :]),
                     reads=[("ps", pb)], writes=[key])
                P.op("dve", lambda e: e.bn_aggr(out=sm_[:, 6:8], in_=sm_[:, 0:6]), reads=[key], writes=[key])
                P.op("act", lambda e: e.activation(out=sm_[:, 8:9], in_=sm_[:, 7:8], func=AF.Sqrt,
                                                   bias=kb.epsc[:, 0:1]),
                     reads=[key, "epsc"], writes=[key])
                P.op("dve", lambda e: e.reciprocal(out=sm_[:, 8:9], in_=sm_[:, 8:9]), reads=[key], writes=[key])
                P.op("dve", lambda e: e.scalar_tensor_tensor(
                    out=sm_[:, 9:10], in0=sm_[:, 6:7], scalar=-1.0, in1=sm_[:, 8:9],
                    op0=ALU.mult, op1=ALU.mult), reads=[key], writes=[key])
                t1, t2 = kb.rot("tmpf", 4), kb.rot("tmpf", 4)
                P.op("act", lambda e, pb=pb, t1=t1: e.activation(
                    out=tmpf[t1][:], in_=psum[:, pb, :], func=AF.Identity, scale=sm_[:, 8:9], bias=sm_[:, 9:10]),
                    reads=[("ps", pb), key], writes=[("tmpf", t1)])
                P.op("act", lambda e, sg=sg, t2=t2: e.activation(out=tmpf[t2][:], in_=gtk[sg], func=AF.Silu),
                     reads=[("gtk", sg)], writes=[("tmpf", t2)])
                so = kb.rot("ogh", 2)
                P.op("dve", lambda e, t1=t1, t2=t2, so=so: e.tensor_tensor(
                    out=ogh[so], in0=tmpf[t1][:], in1=tmpf[t2][:], op=ALU.mult),
                    reads=[("tmpf", t1), ("tmpf", t2)], writes=[("ogh", so)])
                pv = psum[:, 5, 0:256].bitcast(BF16)
                for q in range(4):
                    P.op("pe", lambda e, so=so, q=q, pv=pv: e.transpose(
                        out=pv[:, q * 128:(q + 1) * 128], in_=ogh[so][:, q * 128:(q + 1) * 128], identity=identb[:]),
                        reads=[("ogh", so), "identb"], writes=[("ps", 5)], inc=(q == 3))
                for q in range(4):
                    f_ = h * 4 + q
                    P.op("act", lambda e, q=q, f_=f_, c0=c0, pv=pv: e.activation(
                        out=ogT[:, f_, c0:c0 + 128], in_=pv[:, q * 128:(q + 1) * 128], func=AF.Copy,
                        scale=gnw[:, f_:f_ + 1]),
                        reads=[("ps", 5), "gnw"], writes=["ogT"])

            for b8 in range(8):
                stageA(b8)
                stageB(b8)
                if b8 > 0:
                    stageC(b8 - 1)
            stageC(7)
            P.op("sp", lambda e, h=h: e.dma_start(
                out=sstart[h * 256:(h + 1) * 256, :].rearrange("(dc p) v -> p dc v", dc=2), in_=Sf),
                reads=["Sf"], writes=["sstart"], sem="st_sf")

    def mixer(tile, loc):
        tg = tile * 2 + loc
        c0, c1 = tg * TG, (tg + 1) * TG
        l0, l1 = loc * TG, (loc + 1) * TG
        for dc in range(KC):
            s = kb.rot("wbuf", 2)
            wv = wstream(pret_d[dc], 16, s)
            pb = kb.next_bank()
            for kc in range(16):
                P.op("pe", lambda e, wv=wv, kc=kc, pb=pb: e.matmul(
                    psum[:, pb, :], lhsT=wv[:, kc, :], rhs=ogT[:, kc, l0:l1], start=(kc == 0), stop=(kc == 15)),
                    reads=[("wbuf", s), "ogT"], writes=[("ps", pb)], inc=(kc == 15))
            gb = kb.rot("ldb", 4)
            P.op("sp", lambda e, dc=dc, gb=gb: e.dma_start(out=ldb[gb], in_=Ug[dc, :, c0:c1]),
                 reads=["Uq"], writes=[("ldb", gb)], sem=f"ldb{gb}")
            t1 = kb.rot("tmpf", 4)
            P.op("act", lambda e, gb=gb, t1=t1: e.activation(out=tmpf[t1][:], in_=ldb[gb], func=AF.Sigmoid),
                 reads=[("ldb", gb)], writes=[("tmpf", t1)])
            P.op("dve", lambda e, dc=dc, pb=pb, t1=t1: e.tensor_tensor(
                out=mbuf[:, dc, :], in0=psum[:, pb, :], in1=tmpf[t1][:], op=ALU.mult),
                reads=[("ps", pb), ("tmpf", t1)], writes=[("mbuf", dc)])
        for co in range(KC):
            gI, cc = co // 2, co % 2
            pb = kb.next_bank()
            k = 0
            for kc2 in range(2):
                for comp in range(2):
                    gb = kb.rot("ldb", 4)
                    kk = gI * 2 + kc2
                    P.op("sp", lambda e, gb=gb, kk=kk, comp=comp: e.dma_start(
                        out=ldb[gb],
                        in_=f2loc[(kk * 2 + comp) * 128:(kk * 2 + comp + 1) * 128, c0:c1]),
                        reads=["f2loc"], writes=[("ldb", gb)], sem=f"ldb{gb}")
                    P.op("pe", lambda e, kc2=kc2, comp=comp, gb=gb, pb=pb, k=k, cc=cc: e.matmul(
                        psum[:, pb, :], lhsT=cdsd[:, kc2, comp, cc * 128:(cc + 1) * 128],
                        rhs=ldb[gb], start=(k == 0), stop=(k == 3)),
                        reads=["cdsd", ("ldb", gb)], writes=[("ps", pb)], inc=(k == 3))
                    k += 1
            P.op("act", lambda e, co=co, pb=pb: e.copy(out=mixT[:, co, :], in_=psum[:, pb, :]),
                 reads=[("ps", pb)], writes=["mixT"])
        for dc in range(KC):
            s = kb.rot("wbuf", 2)
            wv = wstream(pfour_d[dc], KC, s)
            pb = kb.next_bank()
            for kc in range(KC):
                P.op("pe", lambda e, wv=wv, kc=kc, pb=pb: e.matmul(
                    psum[:, pb, :], lhsT=wv[:, kc, :], rhs=mixT[:, kc, :], start=(kc == 0), stop=(kc == KC - 1)),
                    reads=[("wbuf", s), "mixT"], writes=[("ps", pb)], inc=(kc == KC - 1))
            gb = kb.rot("ldb", 4)
            P.op("sp", lambda e, dc=dc, gb=gb: e.dma_start(out=ldb[gb], in_=Ug[8 + dc, :, c0:c1]),
                 reads=["Uq"], writes=[("ldb", gb)], sem=f"ldb{gb}")
            t1, t2 = kb.rot("tmpf", 4), kb.rot("tmpf", 4)
            P.op("act", lambda e, gb=gb, t1=t1: e.activation(out=tmpf[t1][:], in_=ldb[gb], func=AF.Sigmoid),
                 reads=[("ldb", gb)], writes=[("tmpf", t1)])
            P.op("dve", lambda e, pb=pb, t1=t1, t2=t2: e.tensor_tensor(
                out=tmpf[t2][:], in0=psum[:, pb, :], in1=tmpf[t1][:], op=ALU.mult),
                reads=[("ps", pb), ("tmpf", t1)], writes=[("tmpf", t2)])
            P.op("dve", lambda e, dc=dc, t2=t2: e.tensor_tensor(
                out=mT[:, dc, :], in0=mbuf[:, dc, :], in1=tmpf[t2][:], op=ALU.add),
                reads=[("mbuf", dc), ("tmpf", t2)], writes=["mT"])
        for dc in range(KC):
            s = kb.rot("wbuf", 2)
            wv = wstream(wout_d[dc], KC, s)
            pb = kb.next_bank()
            for kc in range(KC):
                P.op("pe", lambda e, wv=wv, kc=kc, pb=pb: e.matmul(
                    psum[:, pb, :], lhsT=wv[:, kc, :], rhs=mT[:, kc, :], start=(kc == 0), stop=(kc == KC - 1)),
                    reads=[("wbuf", s), "mT"], writes=[("ps", pb)], inc=(kc == KC - 1))
            P.op("dve", lambda e, dc=dc, pb=pb: e.scalar_tensor_tensor(
                out=xres[:, dc, c0:c1], in0=psum[:, pb, :], scalar=gate[:, 1, dc, 0:1],
                in1=xres[:, dc, c0:c1], op0=ALU.mult, op1=ALU.add),
                reads=[("ps", pb), "modv", ("xres", tg)], writes=[("xres", tg)])

    for tile in range(2):
        retention_tile(tile)
        if tile == 0:
            P.op("sp", lambda e: e.dma_start(
                out=f2loc.rearrange("(k comp a ch) t -> k comp a ch t", k=8, comp=2, a=1),
                in_=f2gv[:, :, bass.ds(P.cid(e), 1), :, :]),
                reads=["f2g"], writes=["f2loc"], sem="ld_f2loc")
        P.barrier()
        for loc in range(2):
            mixer(tile, loc)
        P.barrier()
        groups = []
        for loc in range(2):
            tg = tile * 2 + loc
            groups.append(dict(
                h=(lambda kc, loc=loc: hT[:, kc, loc * TG:(loc + 1) * TG]), hk=("hT", loc),
                g=(lambda j, loc=loc: gbuf[:, j, loc * TG:(loc + 1) * TG]), gk=("gbuf", loc),
                x=(lambda dc, tg=tg: xres[:, dc, tg * TG:(tg + 1) * TG]), xk=("xres", tg),
                n=TG, tg=tg))
        for G in groups:
            kb.norm_mod(G["x"], G["xk"], G["h"], G["hk"], TG,
                        lambda kc: eff[:, 2, kc, 0:1], lambda kc: mod[:, 6 * 8 + kc, 0:1])
        kb.ffn(groups, w13c_d, w2c_d, w13b, w2b, lambda dc: gate[:, 2, dc, 0:1], wk13="wbuf", wk2="wbuf")
        P.barrier()
        for G in groups:
            tg = G["tg"]
            slot = kb.rot("rstd", 2)
            kb.rms_stats(G["x"], G["xk"], TG, slot)
            for kc in range(KC):
                t = kb.rot("tmpf", 4)
                P.op("dve", lambda e, kc=kc, t=t, G=G, slot=slot: e.tensor_tensor(
                    out=tmpf[t][:], in0=G["x"](kc), in1=kb.rstd[slot][:], op=ALU.mult),
                    reads=[G["xk"], ("rstd", slot)], writes=[("tmpf", t)])
                P.op("act", lambda e, kc=kc, t=t: e.activation(
                    out=xn[:, kc, :], in_=tmpf[t][:], func=AF.Copy, scale=nrm[:, 3, kc:kc + 1]),
                    reads=[("tmpf", t), "nrm"], writes=["xn"])
            for cc in range(4):
                c = tg * 4 + cc
                s = c % 2
                for half in range(2):
                    b = kb.next_bank()
                    for q in range(4):
                        dc = half * 4 + q
                        P.op("pe", lambda e, cc=cc, dc=dc, b=b, q=q: e.transpose(
                            out=psum[:, b, q * 128:(q + 1) * 128], in_=xn[:, dc, cc * 128:(cc + 1) * 128],
                            identity=kb.ident[:]),
                            reads=["xn", "ident"], writes=[("ps", b)], inc=(q == 3))
                    if half == 0:
                        P.op("act", lambda e, s=s, b=b: e.copy(out=otile[s][:, 0:512], in_=psum[:, b, :]),
                             reads=[("ps", b)], writes=[("otile", s)])
                    else:
                        P.op("dve", lambda e, s=s, b=b: e.tensor_copy(out=otile[s][:, 512:1024], in_=psum[:, b, :]),
                             reads=[("ps", b)], writes=[("otile", s)])
                P.op("sp", lambda e, c=c, s=s: e.dma_start(out=out_d[c * 128:(c + 1) * 128, :], in_=otile[s]),
                     reads=[("otile", s)], sem=f"st{s}")
        P.barrier()

    print("sbuf bytes remaining", nc.sbuf_bytes_remaining, "ops", P.n_ops)
    P.emit()
    return nc

RET_H = 4
DK = 256
DV = 512
_NC = {}


def _get(name, fn):
    if name not in _NC:
        _NC[name] = fn()
    return _NC[name]


def _run(nc, in_maps):
    if os.environ.get("MK_TRACE"):
        res = run_bass_kernel_spmd(nc, in_maps, core_ids=list(range(NCORES)), trace=True)
        print("EXEC_TIME_NS", res.exec_time_ns)
    else:
        res = run_bass_kernel_spmd(nc, in_maps, core_ids=list(range(NCORES)))
    return res.results


def rope_tables():
    n_freq = DK // 4
    inv = (10000.0 ** (-np.arange(n_freq, dtype=np.float32) / n_freq)).astype(np.float32)
    tok = np.arange(L)
    row = (tok // 64).astype(np.float32)
    col = (tok % 64).astype(np.float32)
    ang = np.concatenate([row[:, None] * inv, col[:, None] * inv], axis=-1).astype(np.float32)
    cos = np.cos(ang).astype(np.float32).T
    sin = np.sin(ang).astype(np.float32).T
    t = np.stack([cos / 16.0, sin / 16.0, cos, sin], axis=1)
    t = t.reshape(128, 4, NCORES, 2, NT // 2).transpose(2, 3, 0, 1, 4)
    return np.ascontiguousarray(t.astype(np.float32))


def perm_win(w_in):
    idx = np.arange(9216)
    for base in (0, 1024):
        for h in range(RET_H):
            o = base + h * DK
            idx[o:o + DK] = np.concatenate([np.arange(o, o + DK, 2), np.arange(o + 1, o + DK, 2)])
    return w_in[:, idx]


def launch1(x, c, ctx, c_ctx, w_ada, b_ada, norm_ffn1, w13_ffn1, w2_ffn1, norm_mix, w_in,
            norm_ffn2, norm_final):
    nc = _get("l1", build_l1)
    cvec = np.ascontiguousarray(np.stack([lay_vec(c[0]), lay_vec(c_ctx)], axis=2))
    rt = rope_tables()
    shared = {
        "cvec": cvec,
        "w_ada": lay_kmajor(w_ada[0], 128),
        "b_ada": lay_vec(b_ada[0]),
        "norms": np.ascontiguousarray(np.stack(
            [lay_vec(norm_ffn1[0]), lay_vec(norm_mix[0]), lay_vec(norm_ffn2[0]), lay_vec(norm_final)], axis=1)),
        "w13_1": lay_w13(w13_ffn1[0]),
        "w2_1": lay_kmajor(w2_ffn1[0], 128),
        "w_in": lay_kmajor(perm_win(w_in[0]), 128),
        "ident": np.eye(128, dtype=np.float32),
        "ctx": np.ascontiguousarray(ctx[0]),
    }
    in_maps = []
    for k in range(NCORES):
        m = dict(shared)
        m["x"] = np.ascontiguousarray(x[0, k * NT:(k + 1) * NT])
        m["rope"] = rt[k]
        in_maps.append(m)
    return _run(nc, in_maps)


_DBG = {}


def dft_tables():
    n = np.arange(128, dtype=np.float64)
    ang = 2.0 * np.pi * np.outer(n, n) / 128.0
    C, S = np.cos(ang), np.sin(ang)
    dft = np.stack([C, S, -S], axis=1)
    cs = np.concatenate([C, S], axis=1)
    tau = 2.0 * np.pi * np.outer(n, n) / float(L)
    tw = np.stack([np.cos(tau), np.sin(tau)], axis=1)
    m = np.arange(256, dtype=np.float64)
    phi = 2.0 * np.pi * np.outer(m, m) / 256.0
    nrmz = 1.0 / np.sqrt(float(L) * 256.0)
    CD, SD = np.cos(phi) * nrmz, -np.sin(phi) * nrmz
    cdsd = np.stack([CD, SD], axis=1).reshape(2, 128, 2, 256).transpose(1, 0, 2, 3)
    import ml_dtypes
    bf = ml_dtypes.bfloat16
    return (dft.astype(np.float32).astype(bf), cs.astype(np.float32).astype(bf), tw.astype(np.float32),
            np.ascontiguousarray(cdsd).astype(np.float32).astype(bf))


def ret_tables():
    j = np.arange(128)[:, None]
    i = np.arange(128)[None, :]
    d1 = np.maximum(i - j, 0).astype(np.float32)
    u1 = (i >= j).astype(np.float32)
    pos = np.broadcast_to((np.arange(128) + 1).astype(np.float32)[None, :], (128, 128))
    tabs = np.ascontiguousarray(np.stack([d1, u1, pos], axis=1))
    pcol = np.stack([127.0 - np.arange(128), np.full(128, 128.0)], axis=1).astype(np.float32)
    return tabs, pcol


def launch2(uT, kvc, decay_fwd, decay_bwd):
    nc = _get("l2", build_l2)
    dft, cs, tw, _ = dft_tables()
    tabs, pcol = ret_tables()
    bf = uT.dtype
    in_maps = []
    for core in range(NCORES):
        d, h = core // 4, core % 4
        kT = np.concatenate([kvc[2 * h:2 * h + 2], uT[8 + 2 * h:8 + 2 * h + 2]], axis=2)
        qT = np.concatenate([np.zeros((2, 128, CTX), dtype=bf), uT[2 * h:2 * h + 2]], axis=2)
        vT = np.concatenate([kvc[8 + 4 * h:8 + 4 * h + 4], uT[16 + 4 * h:16 + 4 * h + 4]], axis=2)
        if d == 1:
            kT = np.concatenate([kT[:, :, :CTX][:, :, ::-1], kT[:, :, CTX:][:, :, ::-1]], axis=2)
            qT = np.concatenate([qT[:, :, :CTX][:, :, ::-1], qT[:, :, CTX:][:, :, ::-1]], axis=2)
            vT = np.concatenate([vT[:, :, :CTX][:, :, ::-1], vT[:, :, CTX:][:, :, ::-1]], axis=2)
        S_ = NCHK * 128
        q4 = qT.reshape(2, 128, NCHK, 128).transpose(2, 1, 0, 3)
        k4 = kT.reshape(2, 128, NCHK, 128).transpose(2, 1, 0, 3)
        qk = np.ascontiguousarray(np.concatenate([q4, k4], axis=2).reshape(NCHK, 128, 512))
        ktok = kT.transpose(2, 0, 1).reshape(S_, 256)
        vtok = vT.transpose(2, 0, 1).reshape(S_, 512)
        kv = np.ascontiguousarray(np.concatenate([ktok, vtok], axis=1).reshape(NCHK, 128, 768))
        dec = (decay_fwd if d == 0 else decay_bwd)[0, h]
        farr = np.ascontiguousarray(uT[48 + core].T.reshape(128, 128, 128))
        in_maps.append({
            "qk": qk, "kv": kv, "dec": np.full((128, 1), dec, dtype=np.float32),
            "tabs": tabs, "pcol": pcol, "farr": farr, "dft": dft, "cs": cs, "tw": tw,
        })
    return _run(nc, in_maps)


def launch3(x1T, uT, res2, modv, norms, ret_gn_w, p_ret, p_four, w_out, w13_ffn2, w2_ffn2):
    nc = _get("l3", build_l3)
    _, _, _, cdsd = dft_tables()
    of = np.concatenate([res2[h]["o"].reshape(L, DV) for h in range(4)], axis=1)
    ob = np.concatenate([res2[4 + h]["o"].reshape(L, DV)[::-1] for h in range(4)], axis=1)
    Fc = np.stack([np.asarray(res2[k]["F"])[0].reshape(128, 128, 128).transpose(1, 0, 2).reshape(128, L)
                   for k in range(NCORES)], axis=0)
    Fs = np.stack([np.asarray(res2[k]["F"])[1].reshape(128, 128, 128).transpose(1, 0, 2).reshape(128, L)
                   for k in range(NCORES)], axis=0)
    shared = {
        "modv": modv, "norms": norms, "gnw": lay_vec(ret_gn_w[0]), "cdsd": cdsd,
        "p_ret": lay_kmajor(p_ret[0], 128), "p_four": lay_kmajor(p_four[0], 128),
        "w_out": lay_kmajor(w_out[0], 128), "w13_2": lay_w13(w13_ffn2[0]), "w2_2": lay_kmajor(w2_ffn2[0], 128),
        "ident": np.eye(128, dtype=np.float32),
    }
    in_maps = []
    for k in range(NCORES):
        sl = slice(k * NT, (k + 1) * NT)
        m = dict(shared)
        m["x1T"] = np.ascontiguousarray(x1T[k])
        m["ofT"] = np.ascontiguousarray(of[sl].T.reshape(16, 128, NT).transpose(1, 0, 2))
        m["obT"] = np.ascontiguousarray(ob[sl].T.reshape(16, 128, NT).transpose(1, 0, 2))
        m["gT"] = np.ascontiguousarray(uT[32:48, :, sl].transpose(1, 0, 2))
        m["gaT"] = np.ascontiguousarray(uT[56:64, :, sl].transpose(1, 0, 2))
        m["gfT"] = np.ascontiguousarray(uT[64:72, :, sl].transpose(1, 0, 2))
        m["FcT"] = np.ascontiguousarray(Fc[:, :, sl].transpose(1, 0, 2))
        m["FsT"] = np.ascontiguousarray(Fs[:, :, sl].transpose(1, 0, 2))
        in_maps.append(m)
    return _run(nc, in_maps)


def kernel_unfused(x, c, ctx, c_ctx, w_ada, b_ada, norm_ffn1, w13_ffn1, w2_ffn1, norm_mix, w_in,
           decay_fwd, decay_bwd, ret_gn_w, p_ret, p_four, w_out, norm_ffn2, w13_ffn2, w2_ffn2,
           norm_final):
    f = lambda a: np.asarray(a, dtype=np.float32)
    (x, c, ctx, c_ctx, w_ada, b_ada, norm_ffn1, w13_ffn1, w2_ffn1, norm_mix, w_in, decay_fwd, decay_bwd,
     ret_gn_w, p_ret, p_four, w_out, norm_ffn2, w13_ffn2, w2_ffn2, norm_final) = [f(a) for a in (
        x, c, ctx, c_ctx, w_ada, b_ada, norm_ffn1, w13_ffn1, w2_ffn1, norm_mix, w_in, decay_fwd, decay_bwd,
        ret_gn_w, p_ret, p_four, w_out, norm_ffn2, w13_ffn2, w2_ffn2, norm_final)]
    r1 = launch1(x, c, ctx, c_ctx, w_ada, b_ada, norm_ffn1, w13_ffn1, w2_ffn1, norm_mix, w_in,
                 norm_ffn2, norm_final)
    uT = np.concatenate([np.asarray(r["uT"]) for r in r1], axis=2)
    kvc = np.asarray(r1[0]["kvc"])
    x1T = [np.asarray(r["x1T"]) for r in r1]
    modv = np.asarray(r1[0]["modv"])
    norms = np.ascontiguousarray(np.stack(
        [lay_vec(norm_ffn1[0]), lay_vec(norm_mix[0]), lay_vec(norm_ffn2[0]), lay_vec(norm_final)], axis=1))
    r2 = launch2(uT, kvc, decay_fwd, decay_bwd)
    _DBG["r2"] = r2
    r3 = launch3(x1T, uT, r2, modv, norms, ret_gn_w, p_ret, p_four, w_out, w13_ffn2, w2_ffn2)
    out = np.concatenate([r["out"] for r in r3], axis=0)
    return out[None].astype(np.float32)


def fused_tables(core):
    j = np.arange(128)[:, None]
    i = np.arange(128)[None, :]
    tabs = np.stack([np.maximum(i - j, 0), (i >= j), np.maximum(j - i, 0), (j >= i)], axis=1).astype(np.float32)
    row = np.arange(128, dtype=np.float32)
    xiex = np.stack([np.broadcast_to(row + 1.0, (128, 128)), np.broadcast_to(128.0 - row, (128, 128))], axis=1)
    p = np.arange(128, dtype=np.float32)
    pex = np.zeros((128, 40), dtype=np.float32)
    pex[:, 0] = 127.0 - p
    pex[:, 1] = p
    for blk in range(16):
        pex[:, 2 + blk] = 2047.0 - 128.0 * blk - p
        pex[:, 18 + blk] = 128.0 * blk + p
    for blk in range(2):
        pex[:, 34 + blk] = 255.0 - 128.0 * blk - p
        pex[:, 36 + blk] = 128.0 * blk + p
    pex[:, 38] = 128.0
    cx = np.zeros((2, 9), dtype=np.float32)
    cm = np.zeros((2, 9), dtype=np.float32)
    cx[0, 0], cm[0, 0] = 2048.0 * core, 1.0
    cx[1, 0], cm[1, 0] = 2048.0 * (7 - core), 1.0
    for k2 in range(8):
        if k2 < core:
            cx[0, 1 + k2], cm[0, 1 + k2] = 2048.0 * (core - 1 - k2), 1.0
        if k2 > core:
            cx[1, 1 + k2], cm[1, 1 + k2] = 2048.0 * (k2 - core - 1), 1.0
    coefx = np.ascontiguousarray(np.broadcast_to(cx[None], (128, 2, 9))).astype(np.float32)
    coefm = np.ascontiguousarray(np.broadcast_to(cm[None], (128, 2, 9))).astype(np.float32)
    return (np.ascontiguousarray(tabs), np.ascontiguousarray(xiex.astype(np.float32)), pex, coefx, coefm)


def kernel_fused(x, c, ctx, c_ctx, w_ada, b_ada, norm_ffn1, w13_ffn1, w2_ffn1, norm_mix, w_in,
                 decay_fwd, decay_bwd, ret_gn_w, p_ret, p_four, w_out, norm_ffn2, w13_ffn2, w2_ffn2,
                 norm_final):
    import ml_dtypes
    nc = _get("fused", build_fused)
    dft, cs, tw, cdsd = dft_tables()
    rt = rope_tables()
    wp = perm_win(w_in[0])
    w_inF = np.concatenate([lay_kmajor(wp[:, 0:1024], 128), lay_kmajor(wp[:, 1024:2048], 128),
                            lay_kmajor(wp[:, 7168:8192], 128), lay_kmajor(wp[:, 8192:9216], 128)], axis=0)
    w_inT = np.concatenate([lay_kmajor(wp[:, 2048:4096], 512), lay_kmajor(wp[:, 4096:6144], 512),
                            lay_kmajor(wp[:, 6144:7168], 512), lay_kmajor(wp[:, 1024:2048], 512)], axis=0)
    dec = np.ascontiguousarray(np.broadcast_to(
        np.concatenate([decay_fwd[0], decay_bwd[0]])[None, :], (128, 8))).astype(np.float32)
    shared = {
        "ctx": np.ascontiguousarray(ctx[0]),
        "cvec": np.ascontiguousarray(np.stack([lay_vec(c[0]), lay_vec(c_ctx)], axis=2)),
        "w_ada": lay_kmajor(w_ada[0], 128), "b_ada": lay_vec(b_ada[0]),
        "norms": np.ascontiguousarray(np.stack(
            [lay_vec(norm_ffn1[0]), lay_vec(norm_mix[0]), lay_vec(norm_ffn2[0]), lay_vec(norm_final)], axis=1)),
        "w13_1": lay_w13(w13_ffn1[0]), "w2_1": lay_kmajor(w2_ffn1[0], 128),
        "w13_2": lay_w13(w13_ffn2[0]), "w2_2": lay_kmajor(w2_ffn2[0], 128),
        "w_inF": np.ascontiguousarray(w_inF), "w_inT": np.ascontiguousarray(w_inT),
        "dec": dec, "gnw": lay_vec(ret_gn_w[0]), "cdsd": cdsd, "dft": dft, "cs": cs, "tw": tw,
        "identb": np.eye(128, dtype=np.float32).astype(ml_dtypes.bfloat16),
        "ident": np.eye(128, dtype=np.float32),
        "p_ret": lay_kmajor(p_ret[0], 128), "p_four": lay_kmajor(p_four[0], 128), "w_out": lay_kmajor(w_out[0], 128),
    }
    in_maps = []
    for k in range(NCORES):
        m = dict(shared)
        m["x"] = np.ascontiguousarray(x[0, k * NT:(k + 1) * NT])
        m["rope"] = rt[k]
        m["tabs"], m["xiex"], m["pex"], m["coefx"], m["coefm"] = fused_tables(k)
        in_maps.append(m)
    r = _run(nc, in_maps)
    _DBG["fused"] = r
    out = np.concatenate([rr["out"] for rr in r], axis=0)
    return out[None].astype(np.float32)


def kernel(x, c, ctx, c_ctx, w_ada, b_ada, norm_ffn1, w13_ffn1, w2_ffn1, norm_mix, w_in,
           decay_fwd, decay_bwd, ret_gn_w, p_ret, p_four, w_out, norm_ffn2, w13_ffn2, w2_ffn2,
           norm_final):
    f = lambda a: np.asarray(a, dtype=np.float32)
    args = [f(a) for a in (x, c, ctx, c_ctx, w_ada, b_ada, norm_ffn1, w13_ffn1, w2_ffn1, norm_mix, w_in,
                           decay_fwd, decay_bwd, ret_gn_w, p_ret, p_four, w_out, norm_ffn2, w13_ffn2, w2_ffn2,
                           norm_final)]
    return kernel_fused(*args)
```
